# Optimizing a Trainium2 kernel written in Bass

```python
import math
import jax, jax.numpy as jnp
from jax import lax
import numpy as np

D_MODEL = 1024
BATCH = 4
SEQ = 4096
DEPTH = 4

DA_HEADS = 4
DA_QK_DIM = 64
DA_V_DIM = 2 * DA_QK_DIM
DA_QK_WIDTH = DA_HEADS * 2 * DA_QK_DIM
DA_WIDTH = DA_HEADS * DA_V_DIM
ROPE_THETA = 500000.0
ROT_DIM = DA_QK_DIM // 4
Q_BLOCK = 128
HG_HEADS = 4
HG_K_DIM = 128
HG_V_DIM = 128
HG_K_WIDTH = HG_HEADS * HG_K_DIM
HG_V_WIDTH = HG_HEADS * HG_V_DIM
HG_CHUNK = 64
FFN_HIDDEN = -(-8 * D_MODEL // (3 * 256)) * 256
NORM_EPS = 1e-6
IN_SIZES = (DA_QK_WIDTH, DA_QK_WIDTH, DA_WIDTH,
            HG_K_WIDTH, HG_K_WIDTH, HG_K_WIDTH, HG_V_WIDTH, HG_V_WIDTH,
            D_MODEL, D_MODEL)
IN_WIDTH = sum(IN_SIZES)

kernel_name = "hybrid_diffattn_hgrn2_gated_encoder"


def rmsnorm(x, gain):
    xf = x.astype(jnp.float32)
    y = xf * lax.rsqrt(jnp.mean(xf * xf, axis=-1, keepdims=True) + NORM_EPS)
    return (y * gain.astype(jnp.float32)).astype(x.dtype)


def split_columns(p):
    outs, start = [], 0
    for size in IN_SIZES:
        outs.append(p[..., start:start + size])
        start += size
    return outs


def rope_tables(positions):
    inv_freq = ROPE_THETA ** (-(jnp.arange(0, ROT_DIM, 2, dtype=jnp.float32) / ROT_DIM))
    ang = positions.astype(jnp.float32)[..., None] * inv_freq
    return jnp.cos(ang), jnp.sin(ang)


def apply_partial_rope(t, cos, sin):
    c = cos[:, :, None, None, :].astype(t.dtype)
    s = sin[:, :, None, None, :].astype(t.dtype)
    half = ROT_DIM // 2
    t1 = t[..., :half]
    t2 = t[..., half:ROT_DIM]
    rot = jnp.concatenate([t1 * c - t2 * s, t2 * c + t1 * s], axis=-1)
    return jnp.concatenate([rot, t[..., ROT_DIM:]], axis=-1)


def diff_attention(h_q, h_k, h_v, cos, sin, lam, norm_gain, layer):
    B, S, _ = h_q.shape
    q = h_q.reshape(B, S, DA_HEADS, 2, DA_QK_DIM)
    k = h_k.reshape(B, S, DA_HEADS, 2, DA_QK_DIM)
    v = h_v.reshape(B, S, DA_HEADS, DA_V_DIM)
    q = apply_partial_rope(q, cos, sin) * (DA_QK_DIM ** -0.5)
    k = apply_partial_rope(k, cos, sin)
    lam_init = 0.8 - 0.6 * math.exp(-0.3 * layer)
    l32 = lam.astype(jnp.float32)
    lam_full = (jnp.exp(jnp.sum(l32[0] * l32[1])) - jnp.exp(jnp.sum(l32[2] * l32[3]))
                + lam_init)
    nq = S // Q_BLOCK
    qb = q.reshape(B, nq, Q_BLOCK, DA_HEADS, 2, DA_QK_DIM).transpose(1, 0, 2, 3, 4, 5)

    def block(qi):
        s = jnp.einsum('bqhcd,bkhcd->bhcqk', qi, k).astype(jnp.float32)
        p = jax.nn.softmax(s, axis=-1)
        w = p[:, :, 0] - lam_full * p[:, :, 1]
        return jnp.einsum('bhqk,bkhv->bqhv', w.astype(v.dtype), v)

    o = lax.map(block, qb)
    o = o.transpose(1, 0, 2, 3, 4).reshape(B, S, DA_HEADS, DA_V_DIM)
    o = rmsnorm(o, norm_gain.reshape(DA_HEADS, DA_V_DIM)) * (1.0 - lam_init)
    return o.reshape(B, S, DA_WIDTH)


def hgrn_lower_bounds(lb_logits):
    p = jax.nn.softmax(lb_logits.astype(jnp.float32), axis=1)
    c = jnp.cumsum(p, axis=1)
    return c - c[:, :1]


def log_forget(z, lb):
    return jnp.logaddexp(jnp.log(lb), jnp.log1p(-lb) + jax.nn.log_sigmoid(z))


def chunk_scan(q, k, g, v):
    B, S, H, dk = q.shape
    dv = v.shape[-1]
    n = S // HG_CHUNK

    def to_chunks(t):
        return t.reshape(B, n, HG_CHUNK, H, t.shape[-1]).transpose(1, 0, 3, 2, 4)

    tril = jnp.tril(jnp.ones((HG_CHUNK, HG_CHUNK), dtype=bool))

    def step(state, inp):
        qc, kc, gc, vc = inp
        b = jnp.cumsum(gc, axis=2)
        inter = jnp.einsum('bhck,bhkv->bhcv', qc * jnp.exp(b), state)
        diff = b[:, :, :, None, :] - b[:, :, None, :, :]
        decay = jnp.exp(jnp.where(tril[:, :, None], diff, -jnp.inf))
        scores = jnp.einsum('bhtk,bhsk,bhtsk->bhts', qc, kc, decay)
        intra = jnp.einsum('bhts,bhsv->bhtv', scores, vc)
        b_last = b[:, :, -1, :]
        state = (jnp.exp(b_last)[..., None] * state
                 + jnp.einsum('bhck,bhcv->bhkv', kc * jnp.exp(b_last[:, :, None, :] - b), vc))
        return state, inter + intra

    s0 = jnp.zeros((B, H, dk, dv), jnp.float32)
    _, out = lax.scan(step, s0, (to_chunks(q), to_chunks(k), to_chunks(g), to_chunks(v)))
    return out.transpose(1, 0, 3, 2, 4).reshape(B, S, H, dv)


def hgrn2_bidirectional(h_q, h_ff, h_fb, h_i, h_g, lb_f, lb_b, norm_gain):
    B, S, _ = h_q.shape
    f32 = jnp.float32
    q = h_q.astype(f32).reshape(B, S, HG_HEADS, HG_K_DIM)
    v = h_i.astype(f32).reshape(B, S, HG_HEADS, HG_V_DIM)
    g_f = log_forget(h_ff.astype(f32).reshape(B, S, HG_HEADS, HG_K_DIM),
                     lb_f.reshape(HG_HEADS, HG_K_DIM))
    g_b = log_forget(h_fb.astype(f32).reshape(B, S, HG_HEADS, HG_K_DIM),
                     lb_b.reshape(HG_HEADS, HG_K_DIM))
    k_f = -jnp.expm1(g_f)
    k_b = -jnp.expm1(g_b)
    out_f = chunk_scan(q, k_f, g_f, v)
    out_b = chunk_scan(q[:, ::-1], k_b[:, ::-1], g_b[:, ::-1], v[:, ::-1])[:, ::-1]
    o = rmsnorm(out_f + out_b, norm_gain.reshape(HG_HEADS, HG_V_DIM))
    o = o * jax.nn.sigmoid(h_g.astype(f32).reshape(B, S, HG_HEADS, HG_V_DIM))
    return o.reshape(B, S, HG_V_WIDTH).astype(h_q.dtype)


def setup_inputs(seed: int = 0) -> dict:
    key = jax.random.key(seed)
    ks = jax.random.split(key, 18)
    nrm = jax.random.normal
    f32 = jnp.float32
    x = nrm(ks[0], (BATCH, SEQ, D_MODEL), f32)
    offset = jax.random.randint(ks[1], (BATCH, 1), 0, 1024, dtype=jnp.int32)
    positions = offset + jnp.arange(SEQ, dtype=jnp.int32)[None, :]
    w_in = nrm(ks[2], (DEPTH, D_MODEL, IN_WIDTH), f32) * D_MODEL ** -0.5
    da_lambda = nrm(ks[3], (DEPTH, 4, DA_QK_DIM), f32) * 0.1
    da_norm = 1.0 + 0.02 * nrm(ks[4], (DEPTH, DA_WIDTH), f32)
    hg_lb_logits = 0.5 * nrm(ks[5], (2, DEPTH, HG_K_WIDTH), f32)
    hg_norm = 1.0 + 0.02 * nrm(ks[6], (DEPTH, HG_V_WIDTH), f32)
    w_a = nrm(ks[7], (DEPTH, DA_WIDTH, D_MODEL), f32) * DA_WIDTH ** -0.5
    w_b = nrm(ks[8], (DEPTH, HG_V_WIDTH, D_MODEL), f32) * HG_V_WIDTH ** -0.5
    w_o = nrm(ks[9], (DEPTH, D_MODEL, D_MODEL), f32) * D_MODEL ** -0.5
    attn_norm = 1.0 + 0.02 * nrm(ks[10], (DEPTH, D_MODEL), f32)
    ffn_norm = 1.0 + 0.02 * nrm(ks[11], (DEPTH, D_MODEL), f32)
    w_gate = nrm(ks[12], (DEPTH, D_MODEL, FFN_HIDDEN), f32) * D_MODEL ** -0.5
    w_up = nrm(ks[13], (DEPTH, D_MODEL, FFN_HIDDEN), f32) * D_MODEL ** -0.5
    w_down = nrm(ks[14], (DEPTH, FFN_HIDDEN, D_MODEL), f32) * FFN_HIDDEN ** -0.5
    final_norm = 1.0 + 0.02 * nrm(ks[15], (D_MODEL,), f32)
    return {"x": x, "positions": positions, "w_in": w_in, "da_lambda": da_lambda,
            "da_norm": da_norm, "hg_lb_logits": hg_lb_logits, "hg_norm": hg_norm,
            "w_a": w_a, "w_b": w_b, "w_o": w_o, "attn_norm": attn_norm,
            "ffn_norm": ffn_norm, "w_gate": w_gate, "w_up": w_up, "w_down": w_down,
            "final_norm": final_norm}


def reference(x, positions, w_in, da_lambda, da_norm, hg_lb_logits, hg_norm, w_a, w_b,
              w_o, attn_norm, ffn_norm, w_gate, w_up, w_down, final_norm):
    cos, sin = rope_tables(positions)
    lbs = hgrn_lower_bounds(hg_lb_logits)
    for layer in range(DEPTH):
        h = rmsnorm(x, attn_norm[layer])
        proj = h @ w_in[layer]
        (a_q, a_k, a_v, b_q, b_ff, b_fb, b_i, b_g, gate_a, gate_b) = split_columns(proj)
        y_a = diff_attention(a_q, a_k, a_v, cos, sin, da_lambda[layer], da_norm[layer],
                             layer) @ w_a[layer]
        y_b = hgrn2_bidirectional(b_q, b_ff, b_fb, b_i, b_g,
                                  lbs[0, layer].astype(x.dtype).astype(jnp.float32),
                                  lbs[1, layer].astype(x.dtype).astype(jnp.float32),
                                  hg_norm[layer]) @ w_b[layer]
        merged = jax.nn.sigmoid(gate_a) * y_a + jax.nn.sigmoid(gate_b) * y_b
        x = x + merged @ w_o[layer]
        h = rmsnorm(x, ffn_norm[layer])
        x = x + (jax.nn.silu(h @ w_gate[layer]) * (h @ w_up[layer])) @ w_down[layer]
    return rmsnorm(x, final_norm)
```

```python
import math
from contextlib import ExitStack
import numpy as np
import concourse.bass as bass
import concourse.mybir as mybir
from concourse.bass_utils import run_bass_kernel_spmd

F32 = mybir.dt.float32
BF16 = mybir.dt.bfloat16
I32 = mybir.dt.int32
AF = mybir.ActivationFunctionType
ALU = mybir.AluOpType

NCORES = 8
T = 2048
NT = T // 128
NB = T // 512
D = 1024
KC = 8
L = 4
FF = 2816
INW = 6144
EPS = 1e-6
C_AQ, C_AK, C_AV, C_BQ, C_G1, C_G2, C_BI, C_BG, C_GA, C_GB = 0, 512, 1024, 1536, 2048, 2560, 3072, 3584, 4096, 5120
KVW = 4 * 2048 + 4 * 16 * 130

CI_ID = 0
CI_MF = 128
CI_MB = 192
CI_FREQ = 256
CI_SGN = 257
CI_EPS = 258
CI_HPI = 259
CI_ONE = 260
CI_SEL = 261
NCST = 264


class Res:
    __slots__ = ("name", "w", "r", "excl")

    def __init__(self, name, excl=False):
        self.name, self.w, self.r, self.excl = name, None, {}, excl


class Prog:
    ENGS = ("pe", "act", "dve", "pool", "sp")

    def __init__(self):
        self.ops = {e: [] for e in self.ENGS}
        self.cnt = {e: 0 for e in self.ENGS}
        self.seen = {e: {} for e in self.ENGS}
        self.fence_deps = {}
        self.res = {}
        self.epoch = {e: 0 for e in self.ENGS}
        self.ekey = {e: e for e in self.ENGS}
    EPOCH_MAX = 12000
    STRICT_SAME_ENGINE = True

    def _own(self, k, eng):
        return k == eng or (isinstance(k, str) and k.startswith(eng + "@"))

    def R(self, *key, excl=False):
        r = self.res.get(key)
        if r is None:
            r = self.res[key] = Res(key, excl)
        return r

    def alias(self, dst, src):
        for k, v in src.r.items():
            if dst.r.get(k, 0) < v:
                dst.r[k] = v
        if src.w is not None and dst.r.get(src.w[0], 0) < src.w[1]:
            dst.r[src.w[0]] = src.w[1]

    def fence(self):
        self.fence_deps = {k: v for k, v in self.cnt.items() if not str(k).startswith("cc_kv")}

    def emit(self, eng, fn, reads=(), writes=(), dma=None, inc=None, nofence=False):
        deps = {} if nofence else dict(self.fence_deps)
        raw_own = {}

        def add(d):
            if d is not None and deps.get(d[0], 0) < d[1]:
                deps[d[0]] = d[1]

        writes = list(writes) + [r for r in reads if r.excl]
        for r in reads:
            add(r.w)
            if r.w is not None and self._own(r.w[0], eng):
                raw_own[r.w[0]] = max(raw_own.get(r.w[0], 0), r.w[1])
        for w in writes:
            if not (dma is not None and w.w is not None and w.w[0] == dma):
                add(w.w)
            for k, v in w.r.items():
                add((k, v))
        if dma is None and (eng == "pe" or not self.STRICT_SAME_ENGINE):
            for k in [k for k in deps if self._own(k, eng)]:
                deps.pop(k)
            if eng != "pe":
                deps.update(raw_own)
        waits = []
        seen = self.seen[eng]
        for k, v in deps.items():
            if seen.get(k, 0) < v:
                seen[k] = v
                waits.append((k, v))
        if dma is None:
            if self.cnt[self.ekey[eng]] >= self.EPOCH_MAX:
                self.epoch[eng] += 1
                self.ekey[eng] = "%s@%d" % (eng, self.epoch[eng])
                self.cnt[self.ekey[eng]] = 0
            key, step = self.ekey[eng], 1
        else:
            key, step = dma, (16 if inc is None else inc)
            if key not in self.cnt:
                self.cnt[key] = 0
        self.cnt[key] += step
        val = self.cnt[key]
        self.ops[eng].append((waits, fn, key, step))
        for r in reads:
            if r.r.get(key, 0) < val:
                r.r[key] = val
        for w in writes:
            w.w = (key, val)
            w.r = {}
        return (key, val)

    def wait_all(self, eng):
        waits = []
        for k, v in self.cnt.items():
            if v > 0 and self.seen[eng].get(k, 0) < v and not self._own(k, eng):
                self.seen[eng][k] = v
                waits.append((k, v))
        self.ops[eng].append((waits, None, None, 0))

    def build(self, nc, st):
        sems = {k: st.enter_context(nc.semaphore("s_" + str(k))) for k in self.cnt}
        engmap = {"pe": "tensor", "act": "scalar", "dve": "vector", "pool": "gpsimd", "sp": "sync"}
        block = st.enter_context(nc.Block())
        for e in self.ENGS:
            ops = self.ops[e]

            def body(eng, ops=ops):
                for waits, fn, key, step in ops:
                    for k, v in waits:
                        eng.wait_ge(sems[k], v)
                    if fn is not None:
                        fn(eng).then_inc(sems[key], step)

            getattr(block, engmap[e])(body)


def build_program(n_layers=L, dbg=None, stop=None):
    nc = bass.Bass("TRN2", target_bir_lowering=False)
    P = Prog()
    st = ExitStack()

    def din(name, shape, dt=F32):
        return nc.dram_tensor(name, list(shape), dt, kind="ExternalInput").ap()

    x_d = din("x", [T, D])
    pos_d = din("pos", [1, T], I32)
    cst_d = din("cst", [128, NCST])
    gA_d = din("gA", [128, L * 8])
    gF_d = din("gF", [128, L * 8])
    gN_d = din("gN", [128, 8])
    lbl_d = din("lbl", [128, 32])
    dan_d = din("da_norm", [L, 512])
    hgn_d = din("hg_norm", [L, 512])
    lam_d = din("da_lambda", [1, L * 256])
    w_in_d = din("w_in", [L, D, INW])
    w_a_d = din("w_a", [L, 512, D])
    w_b_d = din("w_b", [L, 512, D])
    w_o_d = din("w_o", [L, D, D])
    w_g_d = din("w_gate", [L, D, FF])
    w_u_d = din("w_up", [L, D, FF])
    w_d_d = din("w_down", [L, FF, D])
    out_d = nc.dram_tensor("out", [T, D], F32, kind="ExternalOutput").ap()
    xd = nc.dram_tensor("xd", [T, D], F32, kind="Internal").ap()
    KVH = 2048 + 2080
    kv_srcs = [nc.dram_tensor("kv_src%d" % h, [128, KVH], BF16, kind="Internal").ap() for h in range(4)]
    kv_all2 = [[nc.dram_tensor("kv_all%d_%d" % (i, h), [256, KVH], BF16, kind="Internal").ap() for h in range(4)]
               for i in range(2)]
    st_src = nc.dram_tensor("st_src", [128, 512], F32, kind="Internal").ap()
    st_all2 = [nc.dram_tensor("st_all%d" % i, [256, 512], F32, kind="Internal").ap() for i in range(2)]
    dbg_out = {}
    if dbg:
        for name, shape in dbg.items():
            dbg_out[name] = nc.dram_tensor("dbg_" + name, list(shape), F32, kind="ExternalOutput").ap()
    groups = [[0, 1], [2, 3], [4, 5], [6, 7]]

    def sb(name, shape, dt):
        return st.enter_context(nc.sbuf_tensor(name, list(shape), dt))

    hT = sb("hT", [128, KC, T], BF16)
    ctab = sb("ctab", [128, T], BF16)
    stab = sb("stab", [128, T], BF16)
    cst = sb("cst_sb", [128, NCST], F32)
    identb = sb("identb", [128, 128], BF16)
    gA = sb("gA_sb", [128, L * 8], F32)
    gF = sb("gF_sb", [128, L * 8], F32)
    gN = sb("gN_sb", [128, 8], F32)
    lb = sb("lb_sb", [128, 32], F32)
    oml = sb("oml_sb", [128, 32], F32)
    nlam = sb("nlam_sb", [128, 8], F32)
    dan = sb("dan_sb", [128, 512], F32)
    hgn = sb("hgn_sb", [128, 512], F32)
    small = sb("small_sb", [128, 192], F32)
    wA = [sb("wA%d" % i, [128, 4096], BF16) for i in range(2)]
    wB = [sb("wB%d" % i, [128, 4096], BF16) for i in range(2)]
    ARENA = 124 * 1024
    arena = sb("arena", [128, ARENA // 2], BF16)
    ps_all = st.enter_context(nc.psum_tensor("ps_all", [128, 4096], F32))

    def av(off, shape, dt):
        n = int(np.prod(shape[1:]))
        if dt == F32:
            a = arena[:, off // 2: off // 2 + 2 * n].bitcast(F32)
        elif dt == I32:
            a = arena[:, off // 2: off // 2 + 2 * n].bitcast(I32)
        else:
            a = arena[:, off // 2: off // 2 + n]
        if len(shape) == 3:
            a = a.rearrange("p (a b) -> p a b", a=shape[1])
        elif len(shape) == 4:
            a = a.rearrange("p (a b c) -> p a b c", a=shape[1], b=shape[2])
        return a

    K = 1024
    xbuf = av(0, [128, NT, D], F32)

    def bank(b):
        return ps_all[:, b * 512:(b + 1) * 512]

    def RB(b):
        return P.R("bank", b, excl=True)

    psbf = ps_all[:, 7 * 512: 8 * 512].bitcast(BF16)

    def bcast_free(ap2, n_outer, n_inner):
        e = ap2.ap
        return bass.AP(ap2.tensor, ap2.offset, [list(e[0]), list(e[1]), [0, n_inner]])

    def bcast_mid(ap2, n_mid):
        e = ap2.ap
        return bass.AP(ap2.tensor, ap2.offset, [list(e[0]), [0, n_mid], list(e[1])])

    def mm(out, lhsT, rhs, start, stop, reads, writes, skip=False):
        P.emit("pe", lambda e: e.matmul(out, lhsT, rhs, start=start, stop=stop, skip_group_check=skip),
               reads, writes)

    def tr(out, in_, reads, writes):
        P.emit("pe", lambda e: e.transpose(out, in_, identb[:]), reads, writes)

    def act(out, in_, func, reads, writes, scale=None, bias=None, accum=None):
        kw = {}
        if scale is not None:
            kw["scale"] = scale
        if bias is not None:
            kw["bias"] = bias
        if accum is not None:
            kw["accum_out"] = accum
        P.emit("act", lambda e: e.activation(out, in_, func, **kw), reads, writes)

    def tt(eng, out, in0, in1, op, reads, writes):
        P.emit(eng, lambda e: e.tensor_tensor(out, in0, in1, op), reads, writes)

    def ts(eng, out, in0, s1, s2, op0, op1, reads, writes):
        if op1 is None:
            P.emit(eng, lambda e: e.tensor_scalar(out, in0, s1, None, op0), reads, writes)
        else:
            P.emit(eng, lambda e: e.tensor_scalar(out, in0, s1, s2, op0, op1), reads, writes)

    def stt(out, in0, scalar, in1, op0, op1, reads, writes):
        P.emit("dve", lambda e: e.scalar_tensor_tensor(out, in0, scalar, in1, op0, op1), reads, writes)

    def cp(eng, out, in_, reads, writes):
        if eng == "act":
            P.emit("act", lambda e: e.copy(out, in_), reads, writes)
        else:
            P.emit(eng, lambda e: e.tensor_copy(out, in_), reads, writes)

    def recip(out, in_, reads, writes):
        P.emit("dve", lambda e: e.reciprocal(out, in_), reads, writes)

    def memset(eng, ap, c, writes):
        P.emit(eng, lambda e: e.memset(ap, c), (), writes)

    def dma(q, out, in_, reads, writes, sem, nofence=False, **kw):
        P.emit(q, lambda e: e.dma_start(out=out, in_=in_, **kw), reads, writes, dma=sem, nofence=nofence)

    def dump(name, src_ap, reads):
        if name in dbg_out:
            dma("pool", dbg_out[name], src_ap, reads, [P.R("dbg", name)], "dbg_" + name)

    def wload(slot_t, ncols_total, dram_view, col_off, ncols, kch, res, sem):
        dst = slot_t[:, 0:kch * ncols_total].rearrange("p (k c) -> p k c", k=kch)[:, :, col_off:col_off + ncols]
        dma("pool", dst, dram_view.rearrange("(k p) c -> p k c", p=128), [], [res], sem,
            nofence=False)

    r_cst = P.R("cst")
    r_hT = [P.R("hT", i) for i in range(NT)]
    r_x = [P.R("x", i) for i in range(NT)]
    cc = lambda i: cst[:, i:i + 1]

    dma("sp", cst[:], cst_d, [], [r_cst], "ld_cst")
    r_par = P.R("params")
    for dst_t, src in ((gA, gA_d), (gF, gF_d), (gN, gN_d), (lb, lbl_d)):
        dma("sp", dst_t[:], src, [], [r_par], "ld_par")
    cp("dve", identb[:], cst[:, CI_ID:CI_ID + 128], [r_cst], [P.R("identb")])
    r_id = P.R("identb")
    for i in range(NT):
        dma("sp", xbuf[:, i, :], x_d[i * 128:(i + 1) * 128, :], [], [r_x[i]], "ld_x%d" % i)

    r_small = P.R("small")
    lbv = lb[:].rearrange("p (r l h) -> p r l h", r=2, l=4)
    act(lb[:], lb[:], AF.Exp, [r_par], [r_par])
    ssum = small[:, 0:8].rearrange("p (r h) -> p r h", r=2)
    tt("dve", ssum, lbv[:, :, 0, :], lbv[:, :, 1, :], ALU.add, [r_par], [r_small])
    tt("dve", ssum, ssum, lbv[:, :, 2, :], ALU.add, [r_par, r_small], [r_small])
    tt("dve", ssum, ssum, lbv[:, :, 3, :], ALU.add, [r_par, r_small], [r_small])
    recip(ssum, ssum, [r_small], [r_small])
    for l in range(4):
        tt("dve", lbv[:, :, l, :], lbv[:, :, l, :], ssum, ALU.mult, [r_par, r_small], [r_par])
    tt("dve", lbv[:, :, 2, :], lbv[:, :, 2, :], lbv[:, :, 1, :], ALU.add, [r_par], [r_par])
    tt("dve", lbv[:, :, 3, :], lbv[:, :, 3, :], lbv[:, :, 2, :], ALU.add, [r_par], [r_par])
    memset("dve", lbv[:, :, 0, :], 0.0, [r_par])
    r_oml = P.R("oml")
    ts("dve", oml[:], lb[:], -1.0, 1.0, ALU.mult, ALU.add, [r_par], [r_oml])

    lamt = av(64 * K, [128, L * 256], F32)
    r_lam = P.R("lamt")
    dma("sp", lamt, bass.AP(lam_d.tensor, 0, [[0, 128], [1, L * 256]]), [], [r_lam], "ld_lam")
    lam4 = lamt.rearrange("p (l f d) -> p l f d", l=L, f=4)
    lprod = av(72 * K, [128, L, 2, 64], F32)
    r_lp = P.R("lprod")
    for l in range(L):
        tt("dve", lprod[:, l, 0, :], lam4[:, l, 0, :], lam4[:, l, 1, :], ALU.mult, [r_lam], [r_lp])
        tt("dve", lprod[:, l, 1, :], lam4[:, l, 2, :], lam4[:, l, 3, :], ALU.mult, [r_lam], [r_lp])
    r_nlam = P.R("nlam")
    lsum = small[:, 8:16]
    for l in range(L):
        for j in range(2):
            act(lprod[:, l, j, :], lprod[:, l, j, :], AF.Identity, [r_lp], [r_lp, r_small],
                accum=small[:, 8 + l * 2 + j: 9 + l * 2 + j])
    act(lsum, lsum, AF.Exp, [r_small], [r_small])
    for l in range(L):
        lam_init = 0.8 - 0.6 * math.exp(-0.3 * l)
        stt(nlam[:, l:l + 1], small[:, 9 + 2 * l:10 + 2 * l], -lam_init, small[:, 8 + 2 * l:9 + 2 * l],
            ALU.add, ALU.subtract, [r_small], [r_nlam])

    posi = av(64 * K, [128, T], I32)
    ang = av(72 * K, [128, T], F32)
    t1 = av(80 * K, [128, T], F32)
    t2 = av(88 * K, [128, T], F32)
    t3 = av(96 * K, [128, T], F32)
    r_pi, r_ang, r_t1, r_t2, r_t3 = P.R("posi"), P.R("ang"), P.R("t1"), P.R("t2"), P.R("t3")
    P.fence()
    dma("sp", posi, bass.AP(pos_d.tensor, 0, [[0, 128], [1, T]]), [], [r_pi], "ld_pos")
    cp("dve", ang, posi, [r_pi], [r_ang])
    ts("dve", ang, ang, cc(CI_FREQ), None, ALU.mult, None, [r_ang, r_cst], [r_ang])
    ts("dve", t1, ang, 1.0 / (2 * math.pi), None, ALU.mult, None, [r_ang], [r_t1])
    ki = av(64 * K, [128, T], I32)
    cp("dve", ki, t1, [r_t1], [r_pi])
    cp("dve", t1, ki, [r_pi], [r_t1])
    stt(ang, t1, -2 * math.pi, ang, ALU.mult, ALU.add, [r_t1, r_ang], [r_ang])
    act(t1, ang, AF.Sin, [r_ang], [r_t1], scale=0.25)
    act(t2, ang, AF.Sin, [r_ang, r_cst], [r_t2], scale=0.25, bias=cc(CI_HPI))
    tt("dve", t3, t1, t2, ALU.mult, [r_t1, r_t2], [r_t3])
    ts("dve", t3, t3, 2.0, None, ALU.mult, None, [r_t3], [r_t3])
    tt("dve", t2, t1, t1, ALU.mult, [r_t1], [r_t2])
    ts("dve", t2, t2, -2.0, 1.0, ALU.mult, ALU.add, [r_t2], [r_t2])
    tt("dve", t1, t3, t2, ALU.mult, [r_t3, r_t2], [r_t1])
    r_tab = P.R("tabs")
    ts("dve", stab[:], t1, 2.0, cc(CI_SGN), ALU.mult, ALU.mult, [r_t1, r_cst], [r_tab])
    tt("dve", t2, t3, t3, ALU.mult, [r_t3], [r_t2])
    ts("dve", ctab[:], t2, -2.0, 1.0, ALU.mult, ALU.add, [r_t2], [r_tab])
    dump("ctab", ctab[:], [r_tab])
    dump("stab", stab[:], [r_tab])
    P.fence()

    if stop == 'setup':
        n_layers = 0
    def rms_to_hT(gain_tile, l):
        xs = [av(64 * K + j * 2 * K, [128, D], BF16) for j in range(2)]
        r_xs = [P.R("xs", j) for j in range(2)]
        sqs = [av(68 * K + j * 4 * K, [128, D], F32) for j in range(2)]
        r_sqs = [P.R("sq", j) for j in range(2)]
        pb2 = [ps_all[:, (6 + j) * 512:(7 + j) * 512].bitcast(BF16) for j in range(2)]
        r_ssl = [P.R("ss", i) for i in range(NT)]
        r_rstd = P.R("rstd16")
        rstd16 = small[:, 64:80]
        for i in range(NT):
            act(sqs[i % 2], xbuf[:, i, :], AF.Square, [r_x[i]], [r_sqs[i % 2], r_ssl[i]], accum=small[:, 16 + i:17 + i])
        act(rstd16, small[:, 16:32], AF.Ln, r_ssl + [r_cst], [r_rstd], scale=1.0 / D, bias=cc(CI_EPS))
        act(rstd16, rstd16, AF.Exp, [r_rstd], [r_rstd], scale=-0.5)
        for i in range(NT):
            j = i % 2
            act(xs[j], xbuf[:, i, :], AF.Copy, [r_x[i], r_rstd], [r_xs[j]], scale=rstd16[:, i:i + 1])
            for k in range(KC):
                tr(pb2[j][:, k * 128:(k + 1) * 128], xs[j][:, k * 128:(k + 1) * 128], [r_xs[j], r_id], [RB(6 + j)])
            tt("dve", hT[:, :, i * 128:(i + 1) * 128], pb2[j].rearrange("p (k t) -> p k t", k=KC),
               bcast_free(gain_tile[:, l * 8:(l + 1) * 8], KC, 128), ALU.mult, [RB(6 + j), r_par], [r_hT[i]])

    def proj_fm(ps_out, w_slot_view, col0, tb, reads, bankres):
        for k in range(KC):
            mm(ps_out, w_slot_view[:, k, col0:col0 + 128], hT[:, k, tb * 512:(tb + 1) * 512],
               k == 0, k == KC - 1, reads + r_hT[tb * 4:(tb + 1) * 4], [bankres])

    for l in range(n_layers):
        kv_all, st_all = kv_all2[l % 2], st_all2[l % 2]
        r_xd = [P.R("xd", i) for i in range(NT)]
        for i in range(NT):
            dma("sp", xd[i * 128:(i + 1) * 128, :], xbuf[:, i, :], [r_x[i]], [r_xd[i]], "st_xd%d" % i)
        r_dan, r_hgn = P.R("dan"), P.R("hgn")
        dma("sp", dan[:], bass.AP(dan_d.tensor, l * 512, [[0, 128], [1, 512]]), [], [r_dan], "ld_dan")
        dma("sp", hgn[:], bass.AP(hgn_d.tensor, l * 512, [[0, 128], [1, 512]]), [], [r_hgn], "ld_hgn")
        lam_init = 0.8 - 0.6 * math.exp(-0.3 * l)
        ts("dve", dan[:], dan[:], 1.0 - lam_init, None, ALU.mult, None, [r_dan], [r_dan])
        rms_to_hT(gA, l)
        if l == 0:
            dump("hT", hT[:, 0, :], r_hT)
        P.fence()
        if stop == 'S0':
            P.fence()
            break

        kT_loc = av(0, [128, 4, T], BF16)
        V_loc = av(16 * K, [128, 4, NT, 130], BF16)
        Wp = av(34 * K, [128, KC, 512], BF16)
        vtok = av(108 * K, [128, NT, 512], BF16)
        rt1 = [av(42 * K + j * 2 * K, [128, 512], F32) for j in range(2)]
        rt2 = [av(46 * K + j * 2 * K, [128, 512], F32) for j in range(2)]
        r_kT, r_V, r_Wp, r_vtok = P.R("kT_loc"), P.R("V_loc"), P.R("Wp"), P.R("vtok")
        r_wA = [P.R("wA", j) for j in range(2)]
        r_wB = [P.R("wB", j) for j in range(2)]
        wAv = [wA[j][:].rearrange("p (k c) -> p k c", k=KC) for j in range(2)]
        wload(wA[0], 512, w_in_d[l, :, C_AK:C_AK + 512], 0, 512, KC, r_wA[0], "ld_wA0")
        wload(wA[1], 512, w_in_d[l, :, C_AV:C_AV + 512], 0, 512, KC, r_wA[1], "ld_wA1")
        memset("pool", V_loc[:, :, :, 128:129], 1.0, [r_V])
        memset("pool", V_loc[:, :, :, 129:130], 0.0, [r_V])
        memset("pool", Wp, 0.0, [r_Wp])

        def build_partner(wsrc_view, r_src):
            s4 = wsrc_view.rearrange("p k (g d) -> p k g d", d=64)
            d4 = Wp.rearrange("p k (g d) -> p k g d", d=64)
            for k in range(KC):
                cp("pool", d4[:, k, :, 0:8], s4[:, k, :, 8:16], [r_src], [r_Wp])
                cp("pool", d4[:, k, :, 8:16], s4[:, k, :, 0:8], [r_src], [r_Wp])

        def rope_proj(wv, r_w, dstT, r_dst, j0):
            n = j0
            for h in range(4):
                for tb in range(NB):
                    b0, b1 = (n % 2) * 2, (n % 2) * 2 + 1
                    proj_fm(bank(b0), wv, h * 128, tb, [r_w], RB(b0))
                    proj_fm(bank(b1), Wp, h * 128, tb, [r_Wp], RB(b1))
                    j = n % 2
                    r1, r2 = P.R("rt1", j), P.R("rt2", j)
                    tt("dve", rt1[j], bank(b0), ctab[:, tb * 512:(tb + 1) * 512], ALU.mult, [RB(b0), r_tab], [r1])
                    tt("dve", rt2[j], bank(b1), stab[:, tb * 512:(tb + 1) * 512], ALU.mult, [RB(b1), r_tab], [r2])
                    tt("pool", dstT[:, h, tb * 512:(tb + 1) * 512], rt1[j], rt2[j], ALU.add, [r1, r2], [r_dst])
                    n += 1

        build_partner(wAv[0], r_wA[0])
        rope_proj(wAv[0], r_wA[0], kT_loc, r_kT, 0)
        for i in range(NT):
            b = 4 + (i % 2)
            for k in range(KC):
                mm(bank(b), hT[:, k, i * 128:(i + 1) * 128], wAv[1][:, k, :], k == 0, k == KC - 1,
                   [r_hT[i], r_wA[1]], [RB(b)])
            cp("act", V_loc[:, :, i, 0:128], bank(b).rearrange("p (h d) -> p h d", h=4), [RB(b)], [r_V])
        r_kvs = [P.R("kv_src", h) for h in range(4)]
        r_kva = [P.R("kv_all", l % 2, h) for h in range(4)]
        for h in range(4):
            dma("sp", kv_srcs[h][:, 0:2048], kT_loc[:, h, :], [r_kT], [r_kvs[h]], "st_kv%d" % h)
            dma("sp", kv_srcs[h][:, 2048:KVH], V_loc[:, h, :, :].rearrange("p i d -> p (i d)"), [r_V], [r_kvs[h]],
                "st_kv%d" % h)

        def kv_exchange():
            for h in range(4):
                P.emit("pool", lambda e, src=kv_srcs[h], dst=kv_all[h]: e.collective_compute(
                    "AllGather", ALU.bypass, replica_groups=groups, ins=[src], outs=[dst]),
                    [r_kvs[h]], [r_kva[h]], dma="cc_kv%d" % h, inc=1)
        wload(wA[1], 512, w_in_d[l, :, C_BI:C_BI + 512], 0, 512, KC, r_wA[1], "ld_wA1")
        for i in range(NT):
            b = 4 + (i % 2)
            for k in range(KC):
                mm(bank(b), hT[:, k, i * 128:(i + 1) * 128], wAv[1][:, k, :], k == 0, k == KC - 1,
                   [r_hT[i], r_wA[1]], [RB(b)])
            cp("act", vtok[:, i, :], bank(b), [RB(b)], [r_vtok])
        if l == 0:
            dump("kT", kT_loc[:, 0, :], [r_kT])
            dump("vtok", vtok[:, 0, :], [r_vtok])
        P.fence()
        if stop == 'S1':
            P.fence()
            break

        o1 = av(92 * K, [128, NT, 512], BF16)
        o_bT = av(76 * K, [128, 4, T], BF16)
        r_o1 = [P.R("o1", i) for i in range(NT)]
        r_obT = P.R("o_bT")
        S32 = av(41 * K, [128, 4, 128], F32)
        Sb = av(43 * K, [128, 4, 128], BF16)
        r_S32 = [P.R("S32", h) for h in range(4)]
        r_Sb = [P.R("Sb", h) for h in range(4)]
        hg_scb = [av(h * 10496, [128, NT, 64], BF16) for h in range(4)]
        hg_ktok = [av(h * 10496 + 2048, [128, NT, 128], BF16) for h in range(4)]
        hg_qh = [av(h * 10496 + 6144, [128, T], BF16) for h in range(4)]
        hg_dec = [av(h * 10496 + 10240, [128, 32], F32) for h in range(4)]
        g_t = av(44 * K, [128, T], F32)
        a_t = av(52 * K, [128, T], F32)
        kk_t = av(60 * K, [128, T], BF16)
        T0 = 64 * K
        tmpA = wB[1][:, 0:1024].bitcast(F32)
        tmpEq = wB[1][:, 1024:2048].bitcast(F32)
        tmpEk = wB[1][:, 2048:3072].bitcast(F32)
        qtb = wB[1][:, 3072:3584]
        ktb = wB[1][:, 3584:4096]
        kcb = av(T0, [128, 512], BF16)
        chs = av(T0 + 1 * K, [128, 8, 32], F32)
        osum = av(T0 + 2 * K, [128, 4, 128], F32)
        ybf = av(T0 + 4 * K, [128, 512], BF16)
        sgt = av(T0 + 5 * K, [128, 512], F32)
        sqj = av(T0 + 7 * K, [128, 128], F32)
        stin = av(T0 + 8 * K, [128, 2, 512], F32)
        r_g, r_a, r_kk, r_tA, r_Eq, r_Ek = P.R("g_t"), P.R("a_t"), P.R("kk_t"), P.R("tmpA"), P.R("tmpEq"), P.R("tmpEk")
        r_qtb, r_ktb, r_kcb, r_chs = P.R("qtb"), P.R("ktb"), P.R("kcb"), P.R("chs")
        r_osum, r_ybf, r_sgt, r_sqj, r_stin = P.R("osum"), P.R("ybf"), P.R("sgt"), P.R("sqj"), P.R("stin")
        r_hgp = [P.R("hgp", h) for h in range(4)]
        EkF = wB[1][:, 0:4096].bitcast(F32)
        kcf = av(T0 + 8 * K, [128, T], BF16)
        r_EkF, r_kcf = P.R("EkF"), P.R("kcf")
        wBg = wB[0][:].rearrange("p (k c) -> p k c", k=KC)

        for ph in (1, 2):
            gcol = C_G1 if ph == 1 else C_G2
            ldir = ph - 1
            if ph == 1:
                for h in range(4):
                    memset("pool", S32[:, h, :], 0.0, [r_S32[h]])
                    memset("pool", Sb[:, h, :], 0.0, [r_Sb[h]])
            else:
                pass

            def state_exchange():
                r_sts, r_sta = P.R("st_src"), P.R("st_all", l % 2)
                dma("sp", st_src, S32.rearrange("p h d -> p (h d)"), r_S32, [r_sts], "st_st")
                P.emit("pool", lambda e, st_all=st_all: e.collective_compute("AllGather", ALU.bypass, replica_groups=groups,
                                                                             ins=[st_src], outs=[st_all]),
                       [r_sts], [r_sta], dma="cc_st", inc=1)
                dma("sp", stin, st_all.rearrange("(r p) c -> p r c", p=128), [r_sta], [r_stin], "ld_st")
            def hg_load(h):
                j = h % 2
                wload(wA[j], 256, w_in_d[l, :, C_BQ + h * 128:C_BQ + (h + 1) * 128], 0, 128, KC, r_wA[j], "ld_wA%d" % j)
                wload(wA[j], 256, w_in_d[l, :, gcol + h * 128:gcol + (h + 1) * 128], 128, 128, KC, r_wA[j], "ld_wA%d" % j)

            g2 = [g_t, av(76 * K, [128, T], F32)]
            a2 = [a_t, av(84 * K, [128, T], F32)]
            kk2 = [kk_t, av(T0 + 2 * K, [128, T], BF16)]
            chs2 = [chs, av(T0 + 6 * K, [128, 8, 32], F32)]
            Ek2 = [wB[1][:, 0:4096].bitcast(F32), wB[0][:, 0:4096].bitcast(F32)]
            goff = [44 * K, 76 * K]
            aoff = [52 * K, 84 * K]
            r_g2 = [r_g, P.R("g_B")]
            r_a2 = [r_a, P.R("a_B")]
            r_kk2 = [r_kk, P.R("kk_B")]
            r_chs2 = [r_chs, P.R("chs_B")]
            r_Ek2 = [r_EkF, P.R("EkF_B")]
            r_qtf2 = [P.R("qtf", q) for q in range(2)]
            r_ktf2 = [P.R("ktf", q) for q in range(2)]
            r_kcf2 = [P.R("kcf", q) for q in range(2)]
            P.alias(r_Ek2[1], r_wB[0])
            v3 = lambda ap: ap.rearrange("p (c t) -> p c t", t=64)
            mcol = CI_MF if ph == 1 else CI_MB

            def stage1(h):
                j = h % 2
                wv = wA[j][:, 0:KC * 256].rearrange("p (k c) -> p k c", k=KC)
                g_, a_, kk_, chs_ = g2[j], a2[j], kk2[j], chs2[j]
                rg, ra, rkk, rch = r_g2[j], r_a2[j], r_kk2[j], r_chs2[j]
                lbc = lb[:, ldir * 16 + l * 4 + h: ldir * 16 + l * 4 + h + 1]
                omc = oml[:, ldir * 16 + l * 4 + h: ldir * 16 + l * 4 + h + 1]
                rb03 = [RB(0), RB(1), RB(2), RB(3)]
                for tb in range(NB):
                    proj_fm(bank(tb), wv, 128, tb, [r_wA[j]], RB(tb))
                    yield
                act(g_, ps_all[:, 0:2048], AF.Exp, rb03, [rg], scale=-1.0)
                yield
                ts("dve", g_, g_, 1.0, None, ALU.add, None, [rg], [rg])
                yield
                recip(g_, g_, [rg], [rg])
                yield
                ts("dve", g_, g_, omc, lbc, ALU.mult, ALU.add, [rg, r_par, r_oml], [rg])
                yield
                ts("dve", kk_, g_, -1.0, 1.0, ALU.mult, ALU.add, [rg], [rkk])
                yield
                act(g_, g_, AF.Ln, [rg], [rg])
                yield
                P.emit("dve", lambda e, a_=a_, g_=g_: e.tensor_tensor_scan(
                    a_, cst[:, CI_ONE:CI_ONE + 1].broadcast_to([128, T]), g_, 0.0, ALU.mult, ALU.add),
                       [rg, r_cst], [ra])
                yield
                a3, g3 = v3(a_), v3(g_)
                am, u0, p1, D1, D2, tq = (chs_[:, n, :] for n in range(6))
                if ph == 1:
                    cp("dve", am, a3[:, :, 32], [ra], [rch]); yield
                    tt("dve", u0, a3[:, :, 0], g3[:, :, 0], ALU.subtract, [ra, rg], [rch]); yield
                    cp("dve", p1, a3[:, :, 63], [ra], [rch]); yield
                    tt("dve", D1, am, u0, ALU.subtract, [rch], [rch]); yield
                    tt("dve", D2, p1, am, ALU.subtract, [rch], [rch]); yield
                    tt("dve", tq, p1, u0, ALU.subtract, [rch], [rch]); yield
                else:
                    cp("dve", p1, a3[:, :, 63], [ra], [rch]); yield
                    tt("dve", g_, g_, a_, ALU.subtract, [rg, ra], [rg]); yield
                    cp("dve", am, g3[:, :, 32], [rg], [rch]); yield
                    cp("dve", u0, g3[:, :, 0], [rg], [rch]); yield
                    tt("dve", D1, p1, am, ALU.add, [rch], [rch]); yield
                    tt("dve", D2, u0, am, ALU.subtract, [rch], [rch]); yield
                    tt("dve", tq, p1, u0, ALU.add, [rch], [rch]); yield
                act(D1, D1, AF.Exp, [rch], [rch]); yield
                act(D2, D2, AF.Exp, [rch], [rch]); yield
                act(hg_dec[h], tq, AF.Exp, [rch], [r_hgp[h]]); yield

            def stage2(h):
                j = h % 2
                wv = wA[j][:, 0:KC * 256].rearrange("p (k c) -> p k c", k=KC)
                chs_ = chs2[j]
                rkk, rch, rEk = r_kk2[j], r_chs2[j], r_Ek2[j]
                kk_, EkF_ = kk2[j], Ek2[j]
                am, u0, p1, D1, D2, tq = (chs_[:, n, :] for n in range(6))
                if ph == 1:
                    asrc, r_asrc, EqB, r_EqB, off_src, off_eq = a2[j], r_a2[j], g2[j], r_g2[j], aoff[j], goff[j]
                else:
                    asrc, r_asrc, EqB, r_EqB, off_src, off_eq = g2[j], r_g2[j], a2[j], r_a2[j], goff[j], aoff[j]
                rb47 = [RB(4), RB(5), RB(6), RB(7)]
                tt("dve", v3(asrc), v3(asrc), bcast_free(am, 32, 64), ALU.subtract, [r_asrc, rch], [r_asrc]); yield
                act(EqB, asrc, AF.Exp, [r_asrc], [r_EqB]); yield
                act(EkF_, asrc, AF.Exp, [r_asrc], [rEk], scale=-1.0); yield
                for tb in range(NB):
                    proj_fm(bank(4 + tb), wv, 0, tb, [r_wA[j]], RB(4 + tb))
                    yield
                qtf = av(off_src, [128, T], BF16)
                ktf = av(off_src + 4 * K, [128, T], BF16)
                kcf_ = av(off_eq, [128, T], BF16)
                r_qtf, r_ktf, r_kcf_ = r_qtf2[j], r_ktf2[j], r_kcf2[j]
                P.alias(r_qtf, r_asrc)
                P.alias(r_ktf, r_asrc)
                tt("dve", qtf, ps_all[:, 2048:4096], EqB, ALU.mult, rb47 + [r_EqB], [r_qtf]); yield
                tt("pool", ktf, kk_, EkF_, ALU.mult, [rkk, rEk], [r_ktf]); yield
                tt("dve", v3(EqB), v3(EqB), bcast_free(D1, 32, 64), ALU.mult, [r_EqB, rch], [r_EqB]); yield
                tt("pool", v3(EkF_), v3(EkF_), bcast_free(D2, 32, 64), ALU.mult, [rEk, rch], [rEk]); yield
                tt("dve", hg_qh[h], ps_all[:, 2048:4096], EqB, ALU.mult, rb47 + [r_EqB], [r_hgp[h]]); yield
                P.alias(r_kcf_, r_EqB)
                tt("pool", kcf_, kk_, EkF_, ALU.mult, [rkk, rEk], [r_kcf_]); yield
                for c in range(32):
                    i, pb = c // 2, (c % 2) * 64
                    mm(ps_all[pb:pb + 64, 2048 + i * 64: 2048 + (i + 1) * 64], ktf[:, c * 64:(c + 1) * 64],
                       qtf[:, c * 64:(c + 1) * 64], True, True, [r_ktf, r_qtf], [RB(4 + i // 8)])
                    if c % 8 == 7:
                        yield
                tt("dve", hg_scb[h], ps_all[:, 2048:3072].rearrange("p (i t) -> p i t", t=64),
                   bcast_mid(cst[:, mcol:mcol + 64], 16), ALU.mult, [RB(4), RB(5), r_cst], [r_hgp[h]]); yield
                pbT = ps_all[:, 3072:4096].bitcast(BF16)
                for c in range(32):
                    i, pb = c // 2, (c % 2) * 64
                    tr(pbT[pb:pb + 64, i * 128:(i + 1) * 128], kcf_[:, c * 64:(c + 1) * 64], [r_kcf_, r_id], [RB(6 + i // 8)])
                    if c % 8 == 7:
                        yield
                cp("act", hg_ktok[h], pbT.rearrange("p (i d) -> p i d", d=128), [RB(6), RB(7)], [r_hgp[h]]); yield
                P.alias(r_asrc, r_qtf)
                P.alias(r_asrc, r_ktf)
                P.alias(r_EqB, r_kcf_)

            def run_gens(gens):
                gens = list(gens)
                while gens:
                    for g in list(gens):
                        try:
                            next(g)
                        except StopIteration:
                            gens.remove(g)

            hg_load(0)
            hg_load(1)
            if ph == 1:
                kv_exchange()
            if ph == 2:
                state_exchange()
            run_gens([stage1(0)])
            for h in range(4):
                run_gens([stage2(h)] + ([stage1(h + 1)] if h + 1 < 4 else []))
                if h + 2 < 4:
                    hg_load(h + 2)
            P.alias(r_wB[0], r_Ek2[1])
            if ph == 2:
                S32f = S32.rearrange("p h d -> p (h d)")
                ts("dve", S32f, stin[:, 0, :], cc(CI_SEL), None, ALU.mult, None, [r_stin, r_cst], r_S32)
                stt(S32f, stin[:, 1, :], cc(CI_SEL + 1), S32f, ALU.mult, ALU.add, [r_stin, r_cst] + r_S32, r_S32)
                for h in range(4):
                    cp("act", Sb[:, h, :], S32[:, h, :], [r_S32[h]], [r_Sb[h]])
            corder = range(32) if ph == 1 else range(31, -1, -1)
            for c in corder:
                i, pb = c // 2, (c % 2) * 64
                for h in range(4):
                    hs = slice(h * 128, (h + 1) * 128)
                    bo, bs = bank(h), bank(4 + h)
                    mm(bo[pb:pb + 64, 0:128], hg_scb[h][pb:pb + 64, i, :], vtok[pb:pb + 64, i, hs], True, False,
                       [r_hgp[h], r_vtok], [RB(h)])
                    mm(bo[pb:pb + 64, 0:128], hg_qh[h][:, c * 64:(c + 1) * 64], Sb[:, h, :], False, True,
                       [r_hgp[h], r_Sb[h]], [RB(h)])
                    mm(bs[:, 0:128], hg_ktok[h][pb:pb + 64, i, :], vtok[pb:pb + 64, i, hs], True, True,
                       [r_hgp[h], r_vtok], [RB(4 + h)])
                    if ph == 1:
                        cp("act", o1[pb:pb + 64, i, hs], bo[pb:pb + 64, 0:128], [RB(h)], [r_o1[i]])
                    else:
                        tt("dve", o1[pb:pb + 64, i, hs], bo[pb:pb + 64, 0:128], o1[pb:pb + 64, i, hs], ALU.add,
                           [RB(h), r_o1[i]], [r_o1[i]])
                    stt(S32[:, h, :], S32[:, h, :], hg_dec[h][:, c:c + 1], bs[:, 0:128], ALU.mult, ALU.add,
                        [r_S32[h], r_hgp[h], RB(4 + h)], [r_S32[h]])
                    cp("act", Sb[:, h, :], S32[:, h, :], [r_S32[h]], [r_Sb[h]])
            if ph == 2:
                wload(wB[0], 512, w_in_d[l, :, C_BG:C_BG + 512], 0, 512, KC, r_wB[0], "ld_wB0")
                P.fence()
                ss64 = small[:, 96:160]
                r_ss64 = P.R("ss64")
                sqj2 = av(T0, [128, 128], F32)
                r_sqj2 = P.R("sqj2")
                ybf4 = av(T0 + 2 * K, [128, 4, 512], BF16)
                tmpf = av(T0 + 6 * K, [128, 512], F32)
                sg2 = [av(T0 + 8 * K + q * 2 * K, [128, 512], F32) for q in range(2)]
                r_ybf4, r_tmpf = P.R("ybf4"), P.R("tmpf")
                r_sg2 = [P.R("sg2", q) for q in range(2)]
                pb2 = [ps_all[:, (6 + q) * 512:(7 + q) * 512].bitcast(BF16) for q in range(2)]
                for i in range(NT):
                    for h in range(4):
                        act(sqj2, o1[:, i, h * 128:(h + 1) * 128], AF.Square, [r_o1[i]], [P.R("ss64e", i * 4 + h)],
                            accum=small[:, 96 + i * 4 + h: 97 + i * 4 + h])
                act(ss64, ss64, AF.Ln, [P.R("ss64e", q) for q in range(64)] + [r_cst], [r_ss64], scale=1.0 / 128,
                    bias=cc(CI_EPS))
                act(ss64, ss64, AF.Exp, [r_ss64], [r_ss64], scale=-0.5)
                nq = 0
                for tb in range(NB):
                    for t4 in range(4):
                        i = tb * 4 + t4
                        tt("dve", tmpf.rearrange("p (h d) -> p h d", h=4), o1[:, i, :].rearrange("p (h d) -> p h d", h=4),
                           bcast_free(ss64[:, i * 4:(i + 1) * 4], 4, 128), ALU.mult, [r_o1[i], r_ss64], [r_tmpf])
                        tt("dve", ybf4[:, t4, :], tmpf, hgn[:], ALU.mult, [r_tmpf, r_hgn], [r_ybf4])
                    for h in range(4):
                        q = nq % 2
                        nq += 1
                        proj_fm(bank(q), wBg, h * 128, tb, [r_wB[0]], RB(q))
                        act(sg2[q], bank(q), AF.Sigmoid, [RB(q)], [r_sg2[q]])
                        for t4 in range(4):
                            tr(pb2[q][:, t4 * 128:(t4 + 1) * 128], ybf4[:, t4, h * 128:(h + 1) * 128], [r_ybf4, r_id], [RB(6 + q)])
                        tt("dve", o_bT[:, h, tb * 512:(tb + 1) * 512], pb2[q][:, 0:512], sg2[q], ALU.mult,
                           [RB(6 + q), r_sg2[q]], [r_obT])
            P.fence()
        if l == 0:
            dump("obT", o_bT[:, 0, :], [r_obT])
        if stop == 'S4':
            P.fence()
            break

        o_aT = av(108 * K, [128, 4, T], BF16)
        r_oaT = P.R("o_aT")
        qT = [av(j * 4 * K, [128, T], BF16) for j in range(4)]
        Kf2 = [av(16 * K + j * 8 * K, [128, 2, T], BF16) for j in range(2)]
        Vf2 = [av(32 * K + j * 9 * K, [128, 2, NT * 130], BF16) for j in range(2)]
        Et = [av(50 * K + j * 2 * K, [128, 1024], BF16) for j in range(3)]
        ep_t0 = av(56 * K, [128, 4, 128], F32)
        ep_o = av(58 * K, [128, 4, 128], F32)
        ep_y = av(60 * K, [128, 4, 128], BF16)
        ep_sq = av(61 * K, [128, 128], F32)
        ep_acc = av(62 * K, [128, 1040], F32)
        Wp = av(50 * K, [128, KC, 512], BF16)
        rt1 = [av(58 * K + j * 2 * K, [128, 512], F32) for j in range(2)]
        rt2 = [av(62 * K + j * 2 * K, [128, 512], F32) for j in range(2)]
        r_epa, r_epo, r_epy = P.R("ep_acc"), P.R("ep_o"), P.R("ep_y")
        r_qT = [P.R("qT", j) for j in range(4)]
        r_Kf2 = [P.R("Kf", j) for j in range(2)]
        r_Vf2 = [P.R("Vf", j) for j in range(2)]
        r_Et = [P.R("Et", j) for j in range(3)]
        r_ep = P.R("ep")
        r_Wp = P.R("Wp")
        wload(wA[0], 512, w_in_d[l, :, C_AQ:C_AQ + 512], 0, 512, KC, r_wA[0], "ld_wA0")
        memset("pool", Wp, 0.0, [r_Wp])
        build_partner(wAv[0], r_wA[0])

        def kv_load(h):
            kv3 = kv_all[h].rearrange("(r p) c -> p r c", p=128)
            dma("sp", Kf2[h % 2], kv3[:, :, 0:2048], [r_kva[h]], [r_Kf2[h % 2]], "ld_Kf%d" % (h % 2))
            dma("sp", Vf2[h % 2], kv3[:, :, 2048:KVH], [r_kva[h]], [r_Vf2[h % 2]], "ld_Vf%d" % (h % 2))

        kv_load(0)
        kv_load(1)
        n_e = 0
        for h in range(4):
            for tb in range(NB):
                b0, b1 = 4 + (tb % 2) * 2, 5 + (tb % 2) * 2
                proj_fm(bank(b0), wAv[0], h * 128, tb, [r_wA[0]], RB(b0))
                proj_fm(bank(b1), Wp, h * 128, tb, [r_Wp], RB(b1))
                j = tb % 2
                r1, r2 = P.R("rt1", j), P.R("rt2", j)
                tt("dve", rt1[j], bank(b0), ctab[:, tb * 512:(tb + 1) * 512], ALU.mult, [RB(b0), r_tab], [r1])
                tt("dve", rt2[j], bank(b1), stab[:, tb * 512:(tb + 1) * 512], ALU.mult, [RB(b1), r_tab], [r2])
                tt("pool", qT[h][:, tb * 512:(tb + 1) * 512], rt1[j], rt2[j], ALU.add, [r1, r2], [r_qT[h]])
            if l == 0 and h == 0:
                dump("qT", qT[0], [r_qT[0]])
        P.fence()
        for h in range(4):
            jq = h
            Kf, Vf, r_Kf, r_Vf = Kf2[h % 2], Vf2[h % 2], r_Kf2[h % 2], r_Vf2[h % 2]
            steps = [(qb, kt) for qb in range(NB) for kt in range(32)]

            def emit_qk(s):
                qb, kt = steps[s]
                sbi = s % 2
                r_, i_ = kt // 16, kt % 16
                for c in range(2):
                    ps_ = slice(c * 64, (c + 1) * 64)
                    bnk = 2 * sbi + c
                    mm(bank(bnk), Kf[ps_, r_, i_ * 128:(i_ + 1) * 128], qT[jq][ps_, qb * 512:(qb + 1) * 512], True, True,
                       [r_Kf, r_qT[jq]], [RB(bnk)])

            def emit_exp_pv(s):
                nonlocal n_e
                qb, kt = steps[s]
                sbi = s % 2
                r_, i_ = kt // 16, kt % 16
                eb = n_e % 3
                n_e += 1
                act(Et[eb], ps_all[:, sbi * 1024:(sbi + 1) * 1024], AF.Exp, [RB(2 * sbi), RB(2 * sbi + 1)],
                    [r_Et[eb]], scale=0.125)
                for c in range(2):
                    for jj in range(4):
                        a_ = jj * 2 + c
                        bnk, col = 4 + a_ // 3, (a_ % 3) * 130
                        first = (kt == 0 and c == 0 and jj in (0, 2, 3))
                        mm(bank(bnk)[:, col:col + 130], Et[eb][:, c * 512 + jj * 128: c * 512 + (jj + 1) * 128],
                           Vf[:, r_, i_ * 130:(i_ + 1) * 130], first, kt == 31,
                           [r_Et[eb], r_Vf], [RB(bnk)], skip=True)

            def emit_evac(qb):
                cp("dve", ep_acc[:, 0:390], bank(4)[:, 0:390], [RB(4)], [r_epa])
                cp("dve", ep_acc[:, 390:780], bank(5)[:, 0:390], [RB(5)], [r_epa])
                cp("dve", ep_acc[:, 780:1040], bank(6)[:, 0:260], [RB(6)], [r_epa])

            def emit_epilogue(qb):
                acc3 = ep_acc.rearrange("p (a d) -> p a d", d=130)
                acc4 = ep_acc.rearrange("p (j c d) -> p j c d", c=2, d=130)
                rr = small[:, 48:56]
                rr2 = rr.rearrange("p (j c) -> p j c", c=2)
                r_rr = P.R("rr")
                recip(rr, acc3[:, :, 128], [r_epa], [r_rr])
                tt("dve", rr2[:, :, 1], rr2[:, :, 1], nlam[:, l:l + 1].broadcast_to([128, 4]), ALU.mult,
                   [r_rr, r_nlam], [r_rr])
                tt("dve", ep_t0, acc4[:, :, 0, 0:128], bcast_free(rr2[:, :, 0], 4, 128), ALU.mult, [r_epa, r_rr], [r_ep])
                tt("dve", ep_o, acc4[:, :, 1, 0:128], bcast_free(rr2[:, :, 1], 4, 128), ALU.mult, [r_epa, r_rr], [r_epo])
                tt("dve", ep_o, ep_o, ep_t0, ALU.add, [r_epo, r_ep], [r_epo])
                s2 = small[:, 56:60]
                r_s2 = P.R("ss2")
                for jj in range(4):
                    act(ep_sq, ep_o[:, jj, :], AF.Square, [r_epo], [P.R("ep_sq"), r_s2], accum=small[:, 56 + jj:57 + jj])
                act(s2, s2, AF.Ln, [r_s2, r_cst], [r_s2], scale=1.0 / 128, bias=cc(CI_EPS))
                act(s2, s2, AF.Exp, [r_s2], [r_s2], scale=-0.5)
                tt("dve", ep_o, ep_o, bcast_free(s2, 4, 128), ALU.mult, [r_epo, r_s2], [r_epo])
                tt("dve", ep_y, ep_o, bcast_mid(dan[:, h * 128:(h + 1) * 128], 4), ALU.mult, [r_epo, r_dan], [r_epy])
                for jj in range(4):
                    tr(psbf[:, jj * 128:(jj + 1) * 128], ep_y[:, jj, :], [r_epy, r_id], [RB(7)])
                cp("dve", o_aT[:, h, qb * 512:(qb + 1) * 512], psbf[:, 0:512], [RB(7)], [r_oaT])

            emit_qk(0)
            pending = None
            for s in range(len(steps)):
                if s + 1 < len(steps):
                    emit_qk(s + 1)
                emit_exp_pv(s)
                qb, kt = steps[s]
                if pending is not None and s == pending[1]:
                    emit_epilogue(pending[0])
                    pending = None
                if kt == 31:
                    emit_evac(qb)
                    if qb == NB - 1:
                        emit_epilogue(qb)
                    else:
                        pending = (qb, s + 6)
            if h + 2 < 4:
                kv_load(h + 2)
        if l == 0:
            dump("oaT", o_aT[:, 0, :], [r_oaT])
        P.fence()
        if stop == 'S3':
            P.fence()
            break

        mg = [av(92 * K + j * 4 * K, [128, T], BF16) for j in range(4)]
        sg5 = [av(64 * K + j * 2 * K, [128, 512], F32) for j in range(4)]
        r_mg = [P.R("mg", j) for j in range(4)]
        r_sg5 = [P.R("sg5", j) for j in range(4)]
        for i in range(NT):
            dma("sp", xbuf[:, i, :], xd[i * 128:(i + 1) * 128, :], [r_xd[i]], [r_x[i]], "ld_x%d" % i)
        n5 = 0
        r_wo = [P.R("wo5", q) for q in range(4)]

        def wo_view(ch):
            return wB[ch % 2][:, 1024 + ((ch // 2) % 2) * 1024: 2048 + ((ch // 2) % 2) * 1024]

        def s5_load(ch):
            j = ch % 2
            wload(wA[j], 256, w_in_d[l, :, C_GA + ch * 128:C_GA + (ch + 1) * 128], 0, 128, KC, r_wA[j], "ld_wA%d" % j)
            wload(wA[j], 256, w_in_d[l, :, C_GB + ch * 128:C_GB + (ch + 1) * 128], 128, 128, KC, r_wA[j], "ld_wA%d" % j)
            wload(wB[j], 256, w_a_d[l, :, ch * 128:(ch + 1) * 128], 0, 128, 4, r_wB[j], "ld_wB%d" % j)
            wload(wB[j], 256, w_b_d[l, :, ch * 128:(ch + 1) * 128], 128, 128, 4, r_wB[j], "ld_wB%d" % j)
            dma("pool", wo_view(ch), w_o_d[l, ch * 128:(ch + 1) * 128, :], [], [r_wo[ch % 4]], "ld_wo%d" % (ch % 4))

        s5_load(0)
        for ch in range(KC):
            j = ch % 2
            jm = ch % 4
            wv = wA[j][:, 0:KC * 256].rearrange("p (k c) -> p k c", k=KC)
            wab = wB[j][:, 0:4 * 256].rearrange("p (k c) -> p k c", k=4)
            if ch + 1 < KC:
                s5_load(ch + 1)
            for tb in range(NB):
                sl = slice(tb * 512, (tb + 1) * 512)
                b4 = (n5 % 2) * 4
                sa, sb_ = (n5 % 2) * 2, (n5 % 2) * 2 + 1
                n5 += 1
                proj_fm(bank(b4), wv, 0, tb, [r_wA[j]], RB(b4))
                proj_fm(bank(b4 + 1), wv, 128, tb, [r_wA[j]], RB(b4 + 1))
                for k in range(4):
                    mm(bank(b4 + 2), wab[:, k, 0:128], o_aT[:, k, sl], k == 0, k == 3, [r_wB[j], r_oaT], [RB(b4 + 2)])
                for k in range(4):
                    mm(bank(b4 + 3), wab[:, k, 128:256], o_bT[:, k, sl], k == 0, k == 3, [r_wB[j], r_obT], [RB(b4 + 3)])
                act(sg5[sa], bank(b4), AF.Sigmoid, [RB(b4)], [r_sg5[sa]])
                act(sg5[sb_], bank(b4 + 1), AF.Sigmoid, [RB(b4 + 1)], [r_sg5[sb_]])
                tt("dve", sg5[sa], bank(b4 + 2), sg5[sa], ALU.mult, [RB(b4 + 2), r_sg5[sa]], [r_sg5[sa]])
                tt("dve", sg5[sb_], bank(b4 + 3), sg5[sb_], ALU.mult, [RB(b4 + 3), r_sg5[sb_]], [r_sg5[sb_]])
                tt("pool", mg[jm][:, sl], sg5[sa], sg5[sb_], ALU.add, [r_sg5[sa], r_sg5[sb_]], [r_mg[jm]])
            if ch % 2 == 1:
                for i in range(NT):
                    for hf in range(2):
                        b = (i * 2 + hf) % 8
                        for q_, (cj, wj) in enumerate((((ch - 1) % 4, (ch - 1) % 2), (ch % 4, ch % 2))):
                            mm(bank(b), mg[cj][:, i * 128:(i + 1) * 128], wo_view(ch - 1 + q_)[:, hf * 512:(hf + 1) * 512],
                               q_ == 0, q_ == 1, [r_mg[cj], r_wo[(ch - 1 + q_) % 4]], [RB(b)])
                        tt("dve", xbuf[:, i, hf * 512:(hf + 1) * 512], bank(b), xbuf[:, i, hf * 512:(hf + 1) * 512], ALU.add,
                           [RB(b), r_x[i]], [r_x[i]])
        if l == 0:
            dump("x1", xbuf[:, 0, :], r_x)
        P.fence()
        if stop == 'S5':
            P.fence()
            break

        rms_to_hT(gF, l)
        P.fence()
        actb = [av(64 * K + j * 16 * K, [128, 4, T], BF16) for j in range(2)]
        r_actb = [P.R("actb", j) for j in range(2)]
        su = [av(96 * K + j * 2 * K, [128, 512], F32) for j in range(2)]
        r_su = [P.R("su", j) for j in range(2)]
        hgroups = [(0, 4), (4, 4), (8, 4), (12, 4), (16, 3), (19, 3)]
        for gi, (c0, ncg) in enumerate(hgroups):
            j = gi % 2
            wgu = wA[j][:].rearrange("p (k c) -> p k c", k=KC)
            ncol = ncg * 128
            wload(wA[j], 512, w_g_d[l, :, c0 * 128:c0 * 128 + ncol], 0, ncol, KC, r_wA[j], "ld_wA%d" % j)
            wgu2 = wB[j][:].rearrange("p (k c) -> p k c", k=KC)
            wload(wB[j], 512, w_u_d[l, :, c0 * 128:c0 * 128 + ncol], 0, ncol, KC, r_wB[j], "ld_wB%d" % j)
            for cg in range(ncg):
                for tb in range(NB):
                    sl = slice(tb * 512, (tb + 1) * 512)
                    n = (cg * NB + tb) % 2
                    proj_fm(bank(2 * n), wgu, cg * 128, tb, [r_wA[j]], RB(2 * n))
                    proj_fm(bank(2 * n + 1), wgu2, cg * 128, tb, [r_wB[j]], RB(2 * n + 1))
                    act(su[n], bank(2 * n), AF.Silu, [RB(2 * n)], [r_su[n]])
                    tt("dve", actb[j][:, cg, sl], bank(2 * n + 1), su[n], ALU.mult, [RB(2 * n + 1), r_su[n]], [r_actb[j]])
            wdn = av(100 * K, [128, 4, D], BF16)
            r_wdn = P.R("wdn")
            dma("pool", wdn[:, 0:ncg, :], w_d_d[l, c0 * 128:(c0 + ncg) * 128, :].rearrange("(k p) c -> p k c", p=128),
                [], [r_wdn], "ld_wdn")
            for i in range(NT):
                for hf in range(2):
                    b = 4 + (i * 2 + hf) % 4
                    for cg in range(ncg):
                        mm(bank(b), actb[j][:, cg, i * 128:(i + 1) * 128], wdn[:, cg, hf * 512:(hf + 1) * 512],
                           cg == 0, cg == ncg - 1, [r_actb[j], r_wdn], [RB(b)])
                    tt("dve", xbuf[:, i, hf * 512:(hf + 1) * 512], bank(b), xbuf[:, i, hf * 512:(hf + 1) * 512], ALU.add,
                       [RB(b), r_x[i]], [r_x[i]])
        if l == 0:
            dump("x2", xbuf[:, 0, :], r_x)
        P.fence()

    gNb = av(64 * K, [128, D], F32)
    r_gNb = P.R("gNb")
    fin = [av(68 * K + j * 4 * K, [128, D], F32) for j in range(2)]
    r_fin = [P.R("fin", j) for j in range(2)]
    sqf = av(76 * K, [128, D], F32)
    gNfull_d = din("gNfull", [1, D])
    dma("sp", gNb, bass.AP(gNfull_d.tensor, 0, [[0, 128], [1, D]]), [], [r_gNb], "ld_gNb")
    r_out = P.R("out")
    for i in range(NT):
        j = i % 2
        ssc = small[:, 16 + i:17 + i]
        r_ss = P.R("ss", i)
        act(sqf, xbuf[:, i, :], AF.Square, [r_x[i]], [P.R("sqf"), r_ss], accum=ssc)
        act(ssc, ssc, AF.Ln, [r_ss, r_cst], [r_ss], scale=1.0 / D, bias=cc(CI_EPS))
        act(ssc, ssc, AF.Exp, [r_ss], [r_ss], scale=-0.5)
        stt(fin[j], xbuf[:, i, :], ssc, gNb, ALU.mult, ALU.mult, [r_x[i], r_ss, r_gNb], [r_fin[j]])
        dma("sp", out_d[i * 128:(i + 1) * 128, :], fin[j], [r_fin[j]], [r_out], "st_out%d" % j)
    P.wait_all("sp")
    P.build(nc, st)
    st.close()
    return nc


_PROG_CACHE = {}


def _consts():
    c = np.zeros((128, NCST), np.float32)
    c[:, CI_ID:CI_ID + 128] = np.eye(128, dtype=np.float32)
    s = np.arange(128) % 64
    t = np.arange(64)
    c[:, CI_MF:CI_MF + 64] = (s[:, None] <= t[None, :]).astype(np.float32)
    c[:, CI_MB:CI_MB + 64] = (s[:, None] >= t[None, :]).astype(np.float32)
    d = np.arange(128) % 64
    inv = 500000.0 ** (-(np.arange(0, 16, 2, dtype=np.float32) / 16.0))
    f = np.zeros(128, np.float32)
    f[d < 16] = inv[d[d < 16] % 8]
    c[:, CI_FREQ] = f
    c[:, CI_SGN] = np.where(d < 8, -1.0, 1.0)
    c[:, CI_EPS] = EPS
    c[:, CI_HPI] = math.pi / 2
    c[:, CI_ONE] = 1.0
    return c


def _prep_inputs(inputs):
    x = np.asarray(inputs["x"], np.float32)
    pos = np.asarray(inputs["positions"], np.int32)
    w_in = np.asarray(inputs["w_in"], np.float32)
    w_in_odd = np.concatenate([w_in[:, :, :C_G1], w_in[:, :, C_G2:C_BI], w_in[:, :, C_G1:C_G2], w_in[:, :, C_BI:]], axis=2)
    w_in_odd = np.ascontiguousarray(w_in_odd)
    lbl = np.asarray(inputs["hg_lb_logits"], np.float32)
    fm = lambda a: np.ascontiguousarray(a.reshape(-1, 128).T)
    lbl_even = fm(lbl.reshape(2, L, 4, 128).reshape(32, 128))
    lbl_odd = fm(lbl[::-1].reshape(32, 128))
    cbase = _consts()
    shared = {
        "gA": fm(np.asarray(inputs["attn_norm"], np.float32).reshape(L * 8, 128)),
        "gF": fm(np.asarray(inputs["ffn_norm"], np.float32).reshape(L * 8, 128)),
        "gN": fm(np.asarray(inputs["final_norm"], np.float32).reshape(8, 128)),
        "gNfull": np.asarray(inputs["final_norm"], np.float32).reshape(1, D),
        "da_norm": np.asarray(inputs["da_norm"], np.float32),
        "hg_norm": np.asarray(inputs["hg_norm"], np.float32),
        "da_lambda": np.asarray(inputs["da_lambda"], np.float32).reshape(1, L * 256),
        "w_a": np.asarray(inputs["w_a"], np.float32),
        "w_b": np.asarray(inputs["w_b"], np.float32),
        "w_o": np.asarray(inputs["w_o"], np.float32),
        "w_gate": np.asarray(inputs["w_gate"], np.float32),
        "w_up": np.asarray(inputs["w_up"], np.float32),
        "w_down": np.asarray(inputs["w_down"], np.float32),
    }
    maps = []
    for c in range(NCORES):
        b, half = c // 2, c % 2
        if half == 0:
            xs = x[b, :T]
            ps = pos[b, :T]
        else:
            xs = x[b][::-1][:T]
            ps = pos[b][::-1][:T]
        cs = cbase.copy()
        cs[:, CI_SEL + (1 - half)] = 1.0
        m = dict(shared)
        m.update({"x": np.ascontiguousarray(xs), "pos": np.ascontiguousarray(ps).reshape(1, T),
                  "cst": cs, "w_in": w_in if half == 0 else w_in_odd,
                  "lbl": lbl_even if half == 0 else lbl_odd})
        maps.append(m)
    return maps


def _assemble(results, key="out"):
    out = np.empty((4, 4096, D), np.float32)
    for c in range(NCORES):
        b, half = c // 2, c % 2
        y = np.asarray(results[c][key], np.float32)
        if half == 0:
            out[b, :T] = y
        else:
            out[b, T:] = y[::-1]
    return out


def kernel(**inputs):
    if "nc" not in _PROG_CACHE:
        _PROG_CACHE["nc"] = build_program(L)
    nc = _PROG_CACHE["nc"]
    maps = _prep_inputs(inputs)
    res = run_bass_kernel_spmd(nc, maps, core_ids=list(range(NCORES)))
    return _assemble(res.results)
```

```python
import math
from contextlib import ExitStack
import numpy as np
import concourse.bass as bass
import concourse.mybir as mybir
from concourse.bass_utils import run_bass_kernel_spmd

F32 = mybir.dt.float32
BF16 = mybir.dt.bfloat16
I32 = mybir.dt.int32
AF = mybir.ActivationFunctionType
ALU = mybir.AluOpType

NCORES = 8
T = 2048
NT = T // 128
NB = T // 512
D = 1024
KC = 8
L = 4
FF = 2816
INW = 6144
EPS = 1e-6
C_AQ, C_AK, C_AV, C_BQ, C_G1, C_G2, C_BI, C_BG, C_GA, C_GB = 0, 512, 1024, 1536, 2048, 2560, 3072, 3584, 4096, 5120
KVW = 4 * 2048 + 4 * 16 * 130

CI_ID = 0
CI_MF = 128
CI_MB = 192
CI_FREQ = 256
CI_SGN = 257
CI_EPS = 258
CI_HPI = 259
CI_ONE = 260
CI_SEL = 261
NCST = 264


class Res:
    __slots__ = ("name", "w", "r", "excl")

    def __init__(self, name, excl=False):
        self.name, self.w, self.r, self.excl = name, None, {}, excl


class Prog:
    ENGS = ("pe", "act", "dve", "pool", "sp")

    def __init__(self):
        self.ops = {e: [] for e in self.ENGS}
        self.cnt = {e: 0 for e in self.ENGS}
        self.seen = {e: {} for e in self.ENGS}
        self.fence_deps = {}
        self.res = {}
        self.epoch = {e: 0 for e in self.ENGS}
        self.ekey = {e: e for e in self.ENGS}
    EPOCH_MAX = 12000
    STRICT_SAME_ENGINE = True

    def _own(self, k, eng):
        return k == eng or (isinstance(k, str) and k.startswith(eng + "@"))

    def R(self, *key, excl=False):
        r = self.res.get(key)
        if r is None:
            r = self.res[key] = Res(key, excl)
        return r

    def alias(self, dst, src):
        for k, v in src.r.items():
            if dst.r.get(k, 0) < v:
                dst.r[k] = v
        if src.w is not None and dst.r.get(src.w[0], 0) < src.w[1]:
            dst.r[src.w[0]] = src.w[1]

    def fence(self):
        self.fence_deps = {k: v for k, v in self.cnt.items() if not str(k).startswith("cc_kv")}

    def emit(self, eng, fn, reads=(), writes=(), dma=None, inc=None, nofence=False):
        deps = {} if nofence else dict(self.fence_deps)
        raw_own = {}

        def add(d):
            if d is not None and deps.get(d[0], 0) < d[1]:
                deps[d[0]] = d[1]

        writes = list(writes) + [r for r in reads if r.excl]
        for r in reads:
            add(r.w)
            if r.w is not None and self._own(r.w[0], eng):
                raw_own[r.w[0]] = max(raw_own.get(r.w[0], 0), r.w[1])
        for w in writes:
            if not (dma is not None and w.w is not None and w.w[0] == dma):
                add(w.w)
            for k, v in w.r.items():
                add((k, v))
        if dma is None and (eng == "pe" or not self.STRICT_SAME_ENGINE):
            for k in [k for k in deps if self._own(k, eng)]:
                deps.pop(k)
            if eng != "pe":
                deps.update(raw_own)
        waits = []
        seen = self.seen[eng]
        for k, v in deps.items():
            if seen.get(k, 0) < v:
                seen[k] = v
                waits.append((k, v))
        if dma is None:
            if self.cnt[self.ekey[eng]] >= self.EPOCH_MAX:
                self.epoch[eng] += 1
                self.ekey[eng] = "%s@%d" % (eng, self.epoch[eng])
                self.cnt[self.ekey[eng]] = 0
            key, step = self.ekey[eng], 1
        else:
            key, step = dma, (16 if inc is None else inc)
            if key not in self.cnt:
                self.cnt[key] = 0
        self.cnt[key] += step
        val = self.cnt[key]
        self.ops[eng].append((waits, fn, key, step))
        for r in reads:
            if r.r.get(key, 0) < val:
                r.r[key] = val
        for w in writes:
            w.w = (key, val)
            w.r = {}
        return (key, val)

    def wait_all(self, eng):
        waits = []
        for k, v in self.cnt.items():
            if v > 0 and self.seen[eng].get(k, 0) < v and not self._own(k, eng):
                self.seen[eng][k] = v
                waits.append((k, v))
        self.ops[eng].append((waits, None, None, 0))

    def build(self, nc, st):
        sems = {k: st.enter_context(nc.semaphore("s_" + str(k))) for k in self.cnt}
        engmap = {"pe": "tensor", "act": "scalar", "dve": "vector", "pool": "gpsimd", "sp": "sync"}
        block = st.enter_context(nc.Block())
        for e in self.ENGS:
            ops = self.ops[e]

            def body(eng, ops=ops):
                for waits, fn, key, step in ops:
                    for k, v in waits:
                        eng.wait_ge(sems[k], v)
                    if fn is not None:
                        fn(eng).then_inc(sems[key], step)

            getattr(block, engmap[e])(body)


def build_program(n_layers=L, dbg=None, stop=None):
    nc = bass.Bass("TRN2", target_bir_lowering=False)
    P = Prog()
    st = ExitStack()

    def din(name, shape, dt=F32):
        return nc.dram_tensor(name, list(shape), dt, kind="ExternalInput").ap()

    x_d = din("x", [T, D])
    pos_d = din("pos", [1, T], I32)
    cst_d = din("cst", [128, NCST])
    gA_d = din("gA", [128, L * 8])
    gF_d = din("gF", [128, L * 8])
    gN_d = din("gN", [128, 8])
    lbl_d = din("lbl", [128, 32])
    dan_d = din("da_norm", [L, 512])
    hgn_d = din("hg_norm", [L, 512])
    lam_d = din("da_lambda", [1, L * 256])
    w_in_d = din("w_in", [L, D, INW])
    w_a_d = din("w_a", [L, 512, D])
    w_b_d = din("w_b", [L, 512, D])
    w_o_d = din("w_o", [L, D, D])
    w_g_d = din("w_gate", [L, D, FF])
    w_u_d = din("w_up", [L, D, FF])
    w_d_d = din("w_down", [L, FF, D])
    out_d = nc.dram_tensor("out", [T, D], F32, kind="ExternalOutput").ap()
    xd = nc.dram_tensor("xd", [T, D], F32, kind="Internal").ap()
    KVH = 2048 + 2080
    kv_srcs = [nc.dram_tensor("kv_src%d" % h, [128, KVH], BF16, kind="Internal").ap() for h in range(4)]
    kv_all2 = [[nc.dram_tensor("kv_all%d_%d" % (i, h), [256, KVH], BF16, kind="Internal").ap() for h in range(4)]
               for i in range(2)]
    st_src = nc.dram_tensor("st_src", [128, 512], F32, kind="Internal").ap()
    st_all2 = [nc.dram_tensor("st_all%d" % i, [256, 512], F32, kind="Internal").ap() for i in range(2)]
    dbg_out = {}
    if dbg:
        for name, shape in dbg.items():
            dbg_out[name] = nc.dram_tensor("dbg_" + name, list(shape), F32, kind="ExternalOutput").ap()
    groups = [[0, 1], [2, 3], [4, 5], [6, 7]]

    def sb(name, shape, dt):
        return st.enter_context(nc.sbuf_tensor(name, list(shape), dt))

    hT = sb("hT", [128, KC, T], BF16)
    ctab = sb("ctab", [128, T], BF16)
    stab = sb("stab", [128, T], BF16)
    cst = sb("cst_sb", [128, NCST], F32)
    identb = sb("identb", [128, 128], BF16)
    gA = sb("gA_sb", [128, L * 8], F32)
    gF = sb("gF_sb", [128, L * 8], F32)
    gN = sb("gN_sb", [128, 8], F32)
    lb = sb("lb_sb", [128, 32], F32)
    oml = sb("oml_sb", [128, 32], F32)
    nlam = sb("nlam_sb", [128, 8], F32)
    dan = sb("dan_sb", [128, 512], F32)
    hgn = sb("hgn_sb", [128, 512], F32)
    small = sb("small_sb", [128, 192], F32)
    posi_t = sb("posi_t", [128, 512], I32)
    wA = [sb("wA%d" % i, [128, 4096], BF16) for i in range(2)]
    wB = [sb("wB%d" % i, [128, 4096], BF16) for i in range(2)]
    ARENA = 124 * 1024
    arena = sb("arena", [128, ARENA // 2], BF16)
    ps_all = st.enter_context(nc.psum_tensor("ps_all", [128, 4096], F32))

    def av(off, shape, dt):
        n = int(np.prod(shape[1:]))
        if dt == F32:
            a = arena[:, off // 2: off // 2 + 2 * n].bitcast(F32)
        elif dt == I32:
            a = arena[:, off // 2: off // 2 + 2 * n].bitcast(I32)
        else:
            a = arena[:, off // 2: off // 2 + n]
        if len(shape) == 3:
            a = a.rearrange("p (a b) -> p a b", a=shape[1])
        elif len(shape) == 4:
            a = a.rearrange("p (a b c) -> p a b c", a=shape[1], b=shape[2])
        return a

    K = 1024
    xbuf = av(0, [128, NT, D], F32)

    def bank(b):
        return ps_all[:, b * 512:(b + 1) * 512]

    def RB(b):
        return P.R("bank", b, excl=True)

    psbf = ps_all[:, 7 * 512: 8 * 512].bitcast(BF16)

    def bcast_free(ap2, n_outer, n_inner):
        e = ap2.ap
        return bass.AP(ap2.tensor, ap2.offset, [list(e[0]), list(e[1]), [0, n_inner]])

    def bcast_mid(ap2, n_mid):
        e = ap2.ap
        return bass.AP(ap2.tensor, ap2.offset, [list(e[0]), [0, n_mid], list(e[1])])

    def mm(out, lhsT, rhs, start, stop, reads, writes, skip=False):
        P.emit("pe", lambda e: e.matmul(out, lhsT, rhs, start=start, stop=stop, skip_group_check=skip),
               reads, writes)

    def tr(out, in_, reads, writes):
        P.emit("pe", lambda e: e.transpose(out, in_, identb[:]), reads, writes)

    def act(out, in_, func, reads, writes, scale=None, bias=None, accum=None):
        kw = {}
        if scale is not None:
            kw["scale"] = scale
        if bias is not None:
            kw["bias"] = bias
        if accum is not None:
            kw["accum_out"] = accum
        P.emit("act", lambda e: e.activation(out, in_, func, **kw), reads, writes)

    def tt(eng, out, in0, in1, op, reads, writes):
        P.emit(eng, lambda e: e.tensor_tensor(out, in0, in1, op), reads, writes)

    def ts(eng, out, in0, s1, s2, op0, op1, reads, writes):
        if op1 is None:
            P.emit(eng, lambda e: e.tensor_scalar(out, in0, s1, None, op0), reads, writes)
        else:
            P.emit(eng, lambda e: e.tensor_scalar(out, in0, s1, s2, op0, op1), reads, writes)

    def stt(out, in0, scalar, in1, op0, op1, reads, writes):
        P.emit("dve", lambda e: e.scalar_tensor_tensor(out, in0, scalar, in1, op0, op1), reads, writes)

    def cp(eng, out, in_, reads, writes):
        if eng == "act":
            P.emit("act", lambda e: e.copy(out, in_), reads, writes)
        else:
            P.emit(eng, lambda e: e.tensor_copy(out, in_), reads, writes)

    def recip(out, in_, reads, writes):
        P.emit("dve", lambda e: e.reciprocal(out, in_), reads, writes)

    def memset(eng, ap, c, writes):
        P.emit(eng, lambda e: e.memset(ap, c), (), writes)

    def dma(q, out, in_, reads, writes, sem, nofence=False, **kw):
        P.emit(q, lambda e: e.dma_start(out=out, in_=in_, **kw), reads, writes, dma=sem, nofence=nofence)

    def dump(name, src_ap, reads):
        if name in dbg_out:
            dma("pool", dbg_out[name], src_ap, reads, [P.R("dbg", name)], "dbg_" + name)

    def wload(slot_t, ncols_total, dram_view, col_off, ncols, kch, res, sem):
        dst = slot_t[:, 0:kch * ncols_total].rearrange("p (k c) -> p k c", k=kch)[:, :, col_off:col_off + ncols]
        dma("pool", dst, dram_view.rearrange("(k p) c -> p k c", p=128), [], [res], sem,
            nofence=False)

    r_cst = P.R("cst")
    r_hT = [P.R("hT", i) for i in range(NT)]
    r_x = [P.R("x", i) for i in range(NT)]
    cc = lambda i: cst[:, i:i + 1]

    dma("sp", cst[:], cst_d, [], [r_cst], "ld_cst")
    r_par = P.R("params")
    for dst_t, src in ((gA, gA_d), (gF, gF_d), (gN, gN_d), (lb, lbl_d)):
        dma("sp", dst_t[:], src, [], [r_par], "ld_par")
    cp("dve", identb[:], cst[:, CI_ID:CI_ID + 128], [r_cst], [P.R("identb")])
    r_id = P.R("identb")
    for i in range(NT):
        dma("sp", xbuf[:, i, :], x_d[i * 128:(i + 1) * 128, :], [], [r_x[i]], "ld_x%d" % i)

    r_small = P.R("small")
    lbv = lb[:].rearrange("p (r l h) -> p r l h", r=2, l=4)
    act(lb[:], lb[:], AF.Exp, [r_par], [r_par])
    ssum = small[:, 0:8].rearrange("p (r h) -> p r h", r=2)
    tt("dve", ssum, lbv[:, :, 0, :], lbv[:, :, 1, :], ALU.add, [r_par], [r_small])
    tt("dve", ssum, ssum, lbv[:, :, 2, :], ALU.add, [r_par, r_small], [r_small])
    tt("dve", ssum, ssum, lbv[:, :, 3, :], ALU.add, [r_par, r_small], [r_small])
    recip(ssum, ssum, [r_small], [r_small])
    for l in range(4):
        tt("dve", lbv[:, :, l, :], lbv[:, :, l, :], ssum, ALU.mult, [r_par, r_small], [r_par])
    tt("dve", lbv[:, :, 2, :], lbv[:, :, 2, :], lbv[:, :, 1, :], ALU.add, [r_par], [r_par])
    tt("dve", lbv[:, :, 3, :], lbv[:, :, 3, :], lbv[:, :, 2, :], ALU.add, [r_par], [r_par])
    memset("dve", lbv[:, :, 0, :], 0.0, [r_par])
    r_oml = P.R("oml")
    ts("dve", oml[:], lb[:], -1.0, 1.0, ALU.mult, ALU.add, [r_par], [r_oml])

    lamt = av(64 * K, [128, L * 256], F32)
    r_lam = P.R("lamt")
    dma("sp", lamt, bass.AP(lam_d.tensor, 0, [[0, 128], [1, L * 256]]), [], [r_lam], "ld_lam")
    lam4 = lamt.rearrange("p (l f d) -> p l f d", l=L, f=4)
    lprod = av(72 * K, [128, L, 2, 64], F32)
    r_lp = P.R("lprod")
    for l in range(L):
        tt("dve", lprod[:, l, 0, :], lam4[:, l, 0, :], lam4[:, l, 1, :], ALU.mult, [r_lam], [r_lp])
        tt("dve", lprod[:, l, 1, :], lam4[:, l, 2, :], lam4[:, l, 3, :], ALU.mult, [r_lam], [r_lp])
    r_nlam = P.R("nlam")
    lsum = small[:, 8:16]
    for l in range(L):
        for j in range(2):
            act(lprod[:, l, j, :], lprod[:, l, j, :], AF.Identity, [r_lp], [r_lp, r_small],
                accum=small[:, 8 + l * 2 + j: 9 + l * 2 + j])
    act(lsum, lsum, AF.Exp, [r_small], [r_small])
    for l in range(L):
        lam_init = 0.8 - 0.6 * math.exp(-0.3 * l)
        stt(nlam[:, l:l + 1], small[:, 9 + 2 * l:10 + 2 * l], -lam_init, small[:, 8 + 2 * l:9 + 2 * l],
            ALU.add, ALU.subtract, [r_small], [r_nlam])

    P.fence()
    r_tab = P.R("tabs")
    CB = 512
    for cb in range(T // CB):
        cs_ = slice(cb * CB, (cb + 1) * CB)
        ang = av(72 * K, [128, CB], F32)
        t1 = av(80 * K, [128, CB], F32)
        t2 = av(88 * K, [128, CB], F32)
        t3 = av(96 * K, [128, CB], F32)
        r_pi, r_ang, r_t1, r_t2, r_t3 = P.R("posi"), P.R("ang"), P.R("t1"), P.R("t2"), P.R("t3")
        dma("sp", posi_t[:], bass.AP(pos_d.tensor, cb * CB, [[0, 128], [1, CB]]), [], [r_pi], "ld_pos")
        cp("dve", ang, posi_t[:], [r_pi], [r_ang])
        ts("dve", ang, ang, cc(CI_FREQ), None, ALU.mult, None, [r_ang, r_cst], [r_ang])
        ts("dve", t1, ang, 1.0 / (2 * math.pi), None, ALU.mult, None, [r_ang], [r_t1])
        cp("dve", posi_t[:], t1, [r_t1], [r_pi])
        cp("dve", t1, posi_t[:], [r_pi], [r_t1])
        stt(ang, t1, -2 * math.pi, ang, ALU.mult, ALU.add, [r_t1, r_ang], [r_ang])
        act(t1, ang, AF.Sin, [r_ang], [r_t1], scale=0.25)
        act(t2, ang, AF.Sin, [r_ang, r_cst], [r_t2], scale=0.25, bias=cc(CI_HPI))
        tt("dve", t3, t1, t2, ALU.mult, [r_t1, r_t2], [r_t3])
        ts("dve", t3, t3, 2.0, None, ALU.mult, None, [r_t3], [r_t3])
        tt("dve", t2, t1, t1, ALU.mult, [r_t1], [r_t2])
        ts("dve", t2, t2, -2.0, 1.0, ALU.mult, ALU.add, [r_t2], [r_t2])
        tt("dve", t1, t3, t2, ALU.mult, [r_t3, r_t2], [r_t1])
        ts("dve", stab[:, cs_], t1, 2.0, cc(CI_SGN), ALU.mult, ALU.mult, [r_t1, r_cst], [r_tab])
        tt("dve", t2, t3, t3, ALU.mult, [r_t3], [r_t2])
        ts("dve", ctab[:, cs_], t2, -2.0, 1.0, ALU.mult, ALU.add, [r_t2], [r_tab])
    dump("ctab", ctab[:], [r_tab])
    dump("stab", stab[:], [r_tab])
    P.fence()

    if stop == 'setup':
        n_layers = 0
    def rms_to_hT(gain_tile, l):
        xs = [av(64 * K + j * 2 * K, [128, D], BF16) for j in range(2)]
        r_xs = [P.R("xs", j) for j in range(2)]
        sqs = [av(68 * K + j * 4 * K, [128, D], F32) for j in range(2)]
        r_sqs = [P.R("sq", j) for j in range(2)]
        pb2 = [ps_all[:, (6 + j) * 512:(7 + j) * 512].bitcast(BF16) for j in range(2)]
        r_ssl = [P.R("ss", i) for i in range(NT)]
        r_rstd = P.R("rstd16")
        rstd16 = small[:, 64:80]
        for i in range(NT):
            act(sqs[i % 2], xbuf[:, i, :], AF.Square, [r_x[i]], [r_sqs[i % 2], r_ssl[i]], accum=small[:, 16 + i:17 + i])
        act(rstd16, small[:, 16:32], AF.Ln, r_ssl + [r_cst], [r_rstd], scale=1.0 / D, bias=cc(CI_EPS))
        act(rstd16, rstd16, AF.Exp, [r_rstd], [r_rstd], scale=-0.5)
        for i in range(NT):
            j = i % 2
            act(xs[j], xbuf[:, i, :], AF.Copy, [r_x[i], r_rstd], [r_xs[j]], scale=rstd16[:, i:i + 1])
            for k in range(KC):
                tr(pb2[j][:, k * 128:(k + 1) * 128], xs[j][:, k * 128:(k + 1) * 128], [r_xs[j], r_id], [RB(6 + j)])
            tt("dve", hT[:, :, i * 128:(i + 1) * 128], pb2[j].rearrange("p (k t) -> p k t", k=KC),
               bcast_free(gain_tile[:, l * 8:(l + 1) * 8], KC, 128), ALU.mult, [RB(6 + j), r_par], [r_hT[i]])

    def proj_fm(ps_out, w_slot_view, col0, tb, reads, bankres):
        for k in range(KC):
            mm(ps_out, w_slot_view[:, k, col0:col0 + 128], hT[:, k, tb * 512:(tb + 1) * 512],
               k == 0, k == KC - 1, reads + r_hT[tb * 4:(tb + 1) * 4], [bankres])

    for l in range(n_layers):
        kv_all, st_all = kv_all2[l % 2], st_all2[l % 2]
        r_xd = [P.R("xd", i) for i in range(NT)]
        for i in range(NT):
            dma("sp", xd[i * 128:(i + 1) * 128, :], xbuf[:, i, :], [r_x[i]], [r_xd[i]], "st_xd%d" % i)
        r_dan, r_hgn = P.R("dan"), P.R("hgn")
        dma("sp", dan[:], bass.AP(dan_d.tensor, l * 512, [[0, 128], [1, 512]]), [], [r_dan], "ld_dan")
        dma("sp", hgn[:], bass.AP(hgn_d.tensor, l * 512, [[0, 128], [1, 512]]), [], [r_hgn], "ld_hgn")
        lam_init = 0.8 - 0.6 * math.exp(-0.3 * l)
        ts("dve", dan[:], dan[:], 1.0 - lam_init, None, ALU.mult, None, [r_dan], [r_dan])
        rms_to_hT(gA, l)
        if l == 0:
            dump("hT", hT[:, 0, :], r_hT)
        P.fence()
        if stop == 'S0':
            P.fence()
            break

        kT_loc = av(0, [128, 4, T], BF16)
        V_loc = av(16 * K, [128, 4, NT, 130], BF16)
        Wp = av(34 * K, [128, KC, 512], BF16)
        vtok = av(108 * K, [128, NT, 512], BF16)
        rt1 = [av(42 * K + j * 2 * K, [128, 512], F32) for j in range(2)]
        rt2 = [av(46 * K + j * 2 * K, [128, 512], F32) for j in range(2)]
        r_kT, r_V, r_Wp, r_vtok = P.R("kT_loc"), P.R("V_loc"), P.R("Wp"), P.R("vtok")
        r_wA = [P.R("wA", j) for j in range(2)]
        r_wB = [P.R("wB", j) for j in range(2)]
        wAv = [wA[j][:].rearrange("p (k c) -> p k c", k=KC) for j in range(2)]
        wload(wA[0], 512, w_in_d[l, :, C_AK:C_AK + 512], 0, 512, KC, r_wA[0], "ld_wA0")
        wload(wA[1], 512, w_in_d[l, :, C_AV:C_AV + 512], 0, 512, KC, r_wA[1], "ld_wA1")
        memset("pool", V_loc[:, :, :, 128:129], 1.0, [r_V])
        memset("pool", V_loc[:, :, :, 129:130], 0.0, [r_V])
        memset("pool", Wp, 0.0, [r_Wp])

        def build_partner(wsrc_view, r_src):
            s4 = wsrc_view.rearrange("p k (g d) -> p k g d", d=64)
            d4 = Wp.rearrange("p k (g d) -> p k g d", d=64)
            for k in range(KC):
                cp("pool", d4[:, k, :, 0:8], s4[:, k, :, 8:16], [r_src], [r_Wp])
                cp("pool", d4[:, k, :, 8:16], s4[:, k, :, 0:8], [r_src], [r_Wp])

        def rope_proj(wv, r_w, dstT, r_dst, j0):
            n = j0
            for h in range(4):
                for tb in range(NB):
                    b0, b1 = (n % 2) * 2, (n % 2) * 2 + 1
                    proj_fm(bank(b0), wv, h * 128, tb, [r_w], RB(b0))
                    proj_fm(bank(b1), Wp, h * 128, tb, [r_Wp], RB(b1))
                    j = n % 2
                    r1, r2 = P.R("rt1", j), P.R("rt2", j)
                    tt("dve", rt1[j], bank(b0), ctab[:, tb * 512:(tb + 1) * 512], ALU.mult, [RB(b0), r_tab], [r1])
                    tt("dve", rt2[j], bank(b1), stab[:, tb * 512:(tb + 1) * 512], ALU.mult, [RB(b1), r_tab], [r2])
                    tt("pool", dstT[:, h, tb * 512:(tb + 1) * 512], rt1[j], rt2[j], ALU.add, [r1, r2], [r_dst])
                    n += 1

        build_partner(wAv[0], r_wA[0])
        rope_proj(wAv[0], r_wA[0], kT_loc, r_kT, 0)
        for i in range(NT):
            b = 4 + (i % 2)
            for k in range(KC):
                mm(bank(b), hT[:, k, i * 128:(i + 1) * 128], wAv[1][:, k, :], k == 0, k == KC - 1,
                   [r_hT[i], r_wA[1]], [RB(b)])
            cp("act", V_loc[:, :, i, 0:128], bank(b).rearrange("p (h d) -> p h d", h=4), [RB(b)], [r_V])
        r_kvs = [P.R("kv_src", h) for h in range(4)]
        r_kva = [P.R("kv_all", l % 2, h) for h in range(4)]
        for h in range(4):
            dma("sp", kv_srcs[h][:, 0:2048], kT_loc[:, h, :], [r_kT], [r_kvs[h]], "st_kv%d" % h)
            dma("sp", kv_srcs[h][:, 2048:KVH], V_loc[:, h, :, :].rearrange("p i d -> p (i d)"), [r_V], [r_kvs[h]],
                "st_kv%d" % h)

        def kv_exchange():
            for h in range(4):
                P.emit("pool", lambda e, src=kv_srcs[h], dst=kv_all[h]: e.collective_compute(
                    "AllGather", ALU.bypass, replica_groups=groups, ins=[src], outs=[dst]),
                    [r_kvs[h]], [r_kva[h]], dma="cc_kv%d" % h, inc=1)
        wload(wA[1], 512, w_in_d[l, :, C_BI:C_BI + 512], 0, 512, KC, r_wA[1], "ld_wA1")
        for i in range(NT):
            b = 4 + (i % 2)
            for k in range(KC):
                mm(bank(b), hT[:, k, i * 128:(i + 1) * 128], wAv[1][:, k, :], k == 0, k == KC - 1,
                   [r_hT[i], r_wA[1]], [RB(b)])
            cp("act", vtok[:, i, :], bank(b), [RB(b)], [r_vtok])
        if l == 0:
            dump("kT", kT_loc[:, 0, :], [r_kT])
            dump("vtok", vtok[:, 0, :], [r_vtok])
        P.fence()
        if stop == 'S1':
            P.fence()
            break

        o1 = av(92 * K, [128, NT, 512], BF16)
        o_bT = av(76 * K, [128, 4, T], BF16)
        r_o1 = [P.R("o1", i) for i in range(NT)]
        r_obT = P.R("o_bT")
        S32 = av(41 * K, [128, 4, 128], F32)
        Sb = av(43 * K, [128, 4, 128], BF16)
        r_S32 = [P.R("S32", h) for h in range(4)]
        r_Sb = [P.R("Sb", h) for h in range(4)]
        hg_scb = [av(h * 10496, [128, NT, 64], BF16) for h in range(4)]
        hg_ktok = [av(h * 10496 + 2048, [128, NT, 128], BF16) for h in range(4)]
        hg_qh = [av(h * 10496 + 6144, [128, T], BF16) for h in range(4)]
        hg_dec = [av(h * 10496 + 10240, [128, 32], F32) for h in range(4)]
        g_t = av(44 * K, [128, T], F32)
        a_t = av(52 * K, [128, T], F32)
        kk_t = av(60 * K, [128, T], BF16)
        T0 = 64 * K
        tmpA = wB[1][:, 0:1024].bitcast(F32)
        tmpEq = wB[1][:, 1024:2048].bitcast(F32)
        tmpEk = wB[1][:, 2048:3072].bitcast(F32)
        qtb = wB[1][:, 3072:3584]
        ktb = wB[1][:, 3584:4096]
        kcb = av(T0, [128, 512], BF16)
        chs = av(T0 + 1 * K, [128, 8, 32], F32)
        osum = av(T0 + 2 * K, [128, 4, 128], F32)
        ybf = av(T0 + 4 * K, [128, 512], BF16)
        sgt = av(T0 + 5 * K, [128, 512], F32)
        sqj = av(T0 + 7 * K, [128, 128], F32)
        stin = av(T0 + 8 * K, [128, 2, 512], F32)
        r_g, r_a, r_kk, r_tA, r_Eq, r_Ek = P.R("g_t"), P.R("a_t"), P.R("kk_t"), P.R("tmpA"), P.R("tmpEq"), P.R("tmpEk")
        r_qtb, r_ktb, r_kcb, r_chs = P.R("qtb"), P.R("ktb"), P.R("kcb"), P.R("chs")
        r_osum, r_ybf, r_sgt, r_sqj, r_stin = P.R("osum"), P.R("ybf"), P.R("sgt"), P.R("sqj"), P.R("stin")
        r_hgp = [P.R("hgp", h) for h in range(4)]
        EkF = wB[1][:, 0:4096].bitcast(F32)
        kcf = av(T0 + 8 * K, [128, T], BF16)
        r_EkF, r_kcf = P.R("EkF"), P.R("kcf")
        wBg = wB[0][:].rearrange("p (k c) -> p k c", k=KC)

        for ph in (1, 2):
            gcol = C_G1 if ph == 1 else C_G2
            ldir = ph - 1
            if ph == 1:
                for h in range(4):
                    memset("pool", S32[:, h, :], 0.0, [r_S32[h]])
                    memset("pool", Sb[:, h, :], 0.0, [r_Sb[h]])
            else:
                pass

            def state_exchange():
                r_sts, r_sta = P.R("st_src"), P.R("st_all", l % 2)
                dma("sp", st_src, S32.rearrange("p h d -> p (h d)"), r_S32, [r_sts], "st_st")
                P.emit("pool", lambda e, st_all=st_all: e.collective_compute("AllGather", ALU.bypass, replica_groups=groups,
                                                                             ins=[st_src], outs=[st_all]),
                       [r_sts], [r_sta], dma="cc_st", inc=1)
                dma("sp", stin, st_all.rearrange("(r p) c -> p r c", p=128), [r_sta], [r_stin], "ld_st")
            def hg_load(h):
                j = h % 2
                wload(wA[j], 256, w_in_d[l, :, C_BQ + h * 128:C_BQ + (h + 1) * 128], 0, 128, KC, r_wA[j], "ld_wA%d" % j)
                wload(wA[j], 256, w_in_d[l, :, gcol + h * 128:gcol + (h + 1) * 128], 128, 128, KC, r_wA[j], "ld_wA%d" % j)

            g2 = [g_t, av(76 * K, [128, T], F32)]
            a2 = [a_t, av(84 * K, [128, T], F32)]
            kk2 = [kk_t, av(T0 + 2 * K, [128, T], BF16)]
            chs2 = [chs, av(T0 + 6 * K, [128, 8, 32], F32)]
            Ek2 = [wB[1][:, 0:4096].bitcast(F32), wB[0][:, 0:4096].bitcast(F32)]
            goff = [44 * K, 76 * K]
            aoff = [52 * K, 84 * K]
            r_g2 = [r_g, P.R("g_B")]
            r_a2 = [r_a, P.R("a_B")]
            r_kk2 = [r_kk, P.R("kk_B")]
            r_chs2 = [r_chs, P.R("chs_B")]
            r_Ek2 = [r_EkF, P.R("EkF_B")]
            r_qtf2 = [P.R("qtf", q) for q in range(2)]
            r_ktf2 = [P.R("ktf", q) for q in range(2)]
            r_kcf2 = [P.R("kcf", q) for q in range(2)]
            P.alias(r_Ek2[1], r_wB[0])
            v3 = lambda ap: ap.rearrange("p (c t) -> p c t", t=64)
            mcol = CI_MF if ph == 1 else CI_MB

            def stage1(h):
                j = h % 2
                wv = wA[j][:, 0:KC * 256].rearrange("p (k c) -> p k c", k=KC)
                g_, a_, kk_, chs_ = g2[j], a2[j], kk2[j], chs2[j]
                rg, ra, rkk, rch = r_g2[j], r_a2[j], r_kk2[j], r_chs2[j]
                lbc = lb[:, ldir * 16 + l * 4 + h: ldir * 16 + l * 4 + h + 1]
                omc = oml[:, ldir * 16 + l * 4 + h: ldir * 16 + l * 4 + h + 1]
                rb03 = [RB(0), RB(1), RB(2), RB(3)]
                for tb in range(NB):
                    proj_fm(bank(tb), wv, 128, tb, [r_wA[j]], RB(tb))
                    yield
                act(g_, ps_all[:, 0:2048], AF.Exp, rb03, [rg], scale=-1.0)
                yield
                ts("dve", g_, g_, 1.0, None, ALU.add, None, [rg], [rg])
                yield
                recip(g_, g_, [rg], [rg])
                yield
                ts("dve", g_, g_, omc, lbc, ALU.mult, ALU.add, [rg, r_par, r_oml], [rg])
                yield
                ts("dve", kk_, g_, -1.0, 1.0, ALU.mult, ALU.add, [rg], [rkk])
                yield
                act(g_, g_, AF.Ln, [rg], [rg])
                yield
                P.emit("dve", lambda e, a_=a_, g_=g_: e.tensor_tensor_scan(
                    a_, cst[:, CI_ONE:CI_ONE + 1].broadcast_to([128, T]), g_, 0.0, ALU.mult, ALU.add),
                       [rg, r_cst], [ra])
                yield
                a3, g3 = v3(a_), v3(g_)
                am, u0, p1, D1, D2, tq = (chs_[:, n, :] for n in range(6))
                if ph == 1:
                    cp("dve", am, a3[:, :, 32], [ra], [rch]); yield
                    tt("dve", u0, a3[:, :, 0], g3[:, :, 0], ALU.subtract, [ra, rg], [rch]); yield
                    cp("dve", p1, a3[:, :, 63], [ra], [rch]); yield
                    tt("dve", D1, am, u0, ALU.subtract, [rch], [rch]); yield
                    tt("dve", D2, p1, am, ALU.subtract, [rch], [rch]); yield
                    tt("dve", tq, p1, u0, ALU.subtract, [rch], [rch]); yield
                else:
                    cp("dve", p1, a3[:, :, 63], [ra], [rch]); yield
                    tt("dve", g_, g_, a_, ALU.subtract, [rg, ra], [rg]); yield
                    cp("dve", am, g3[:, :, 32], [rg], [rch]); yield
                    cp("dve", u0, g3[:, :, 0], [rg], [rch]); yield
                    tt("dve", D1, p1, am, ALU.add, [rch], [rch]); yield
                    tt("dve", D2, u0, am, ALU.subtract, [rch], [rch]); yield
                    tt("dve", tq, p1, u0, ALU.add, [rch], [rch]); yield
                act(D1, D1, AF.Exp, [rch], [rch]); yield
                act(D2, D2, AF.Exp, [rch], [rch]); yield
                act(hg_dec[h], tq, AF.Exp, [rch], [r_hgp[h]]); yield

            def stage2(h):
                j = h % 2
                wv = wA[j][:, 0:KC * 256].rearrange("p (k c) -> p k c", k=KC)
                chs_ = chs2[j]
                rkk, rch, rEk = r_kk2[j], r_chs2[j], r_Ek2[j]
                kk_, EkF_ = kk2[j], Ek2[j]
                am, u0, p1, D1, D2, tq = (chs_[:, n, :] for n in range(6))
                if ph == 1:
                    asrc, r_asrc, EqB, r_EqB, off_src, off_eq = a2[j], r_a2[j], g2[j], r_g2[j], aoff[j], goff[j]
                else:
                    asrc, r_asrc, EqB, r_EqB, off_src, off_eq = g2[j], r_g2[j], a2[j], r_a2[j], goff[j], aoff[j]
                rb47 = [RB(4), RB(5), RB(6), RB(7)]
                tt("dve", v3(asrc), v3(asrc), bcast_free(am, 32, 64), ALU.subtract, [r_asrc, rch], [r_asrc]); yield
                act(EqB, asrc, AF.Exp, [r_asrc], [r_EqB]); yield
                act(EkF_, asrc, AF.Exp, [r_asrc], [rEk], scale=-1.0); yield
                for tb in range(NB):
                    proj_fm(bank(4 + tb), wv, 0, tb, [r_wA[j]], RB(4 + tb))
                    yield
                qtf = av(off_src, [128, T], BF16)
                ktf = av(off_src + 4 * K, [128, T], BF16)
                kcf_ = av(off_eq, [128, T], BF16)
                r_qtf, r_ktf, r_kcf_ = r_qtf2[j], r_ktf2[j], r_kcf2[j]
                P.alias(r_qtf, r_asrc)
                P.alias(r_ktf, r_asrc)
                tt("dve", qtf, ps_all[:, 2048:4096], EqB, ALU.mult, rb47 + [r_EqB], [r_qtf]); yield
                tt("pool", ktf, kk_, EkF_, ALU.mult, [rkk, rEk], [r_ktf]); yield
                tt("dve", v3(EqB), v3(EqB), bcast_free(D1, 32, 64), ALU.mult, [r_EqB, rch], [r_EqB]); yield
                tt("pool", v3(EkF_), v3(EkF_), bcast_free(D2, 32, 64), ALU.mult, [rEk, rch], [rEk]); yield
                tt("dve", hg_qh[h], ps_all[:, 2048:4096], EqB, ALU.mult, rb47 + [r_EqB], [r_hgp[h]]); yield
                P.alias(r_kcf_, r_EqB)
                tt("pool", kcf_, kk_, EkF_, ALU.mult, [rkk, rEk], [r_kcf_]); yield
                for c in range(32):
                    i, pb = c // 2, (c % 2) * 64
                    mm(ps_all[pb:pb + 64, 2048 + i * 64: 2048 + (i + 1) * 64], ktf[:, c * 64:(c + 1) * 64],
                       qtf[:, c * 64:(c + 1) * 64], True, True, [r_ktf, r_qtf], [RB(4 + i // 8)])
                    if c % 8 == 7:
                        yield
                tt("dve", hg_scb[h], ps_all[:, 2048:3072].rearrange("p (i t) -> p i t", t=64),
                   bcast_mid(cst[:, mcol:mcol + 64], 16), ALU.mult, [RB(4), RB(5), r_cst], [r_hgp[h]]); yield
                pbT = ps_all[:, 3072:4096].bitcast(BF16)
                for c in range(32):
                    i, pb = c // 2, (c % 2) * 64
                    tr(pbT[pb:pb + 64, i * 128:(i + 1) * 128], kcf_[:, c * 64:(c + 1) * 64], [r_kcf_, r_id], [RB(6 + i // 8)])
                    if c % 8 == 7:
                        yield
                cp("act", hg_ktok[h], pbT.rearrange("p (i d) -> p i d", d=128), [RB(6), RB(7)], [r_hgp[h]]); yield
                P.alias(r_asrc, r_qtf)
                P.alias(r_asrc, r_ktf)
                P.alias(r_EqB, r_kcf_)

            def run_gens(gens):
                gens = list(gens)
                while gens:
                    for g in list(gens):
                        try:
                            next(g)
                        except StopIteration:
                            gens.remove(g)

            hg_load(0)
            hg_load(1)
            if ph == 1:
                kv_exchange()
            if ph == 2:
                state_exchange()
            run_gens([stage1(0)])
            for h in range(4):
                run_gens([stage2(h)] + ([stage1(h + 1)] if h + 1 < 4 else []))
                if h + 2 < 4:
                    hg_load(h + 2)
            P.alias(r_wB[0], r_Ek2[1])
            if ph == 2:
                S32f = S32.rearrange("p h d -> p (h d)")
                ts("dve", S32f, stin[:, 0, :], cc(CI_SEL), None, ALU.mult, None, [r_stin, r_cst], r_S32)
                stt(S32f, stin[:, 1, :], cc(CI_SEL + 1), S32f, ALU.mult, ALU.add, [r_stin, r_cst] + r_S32, r_S32)
                for h in range(4):
                    cp("act", Sb[:, h, :], S32[:, h, :], [r_S32[h]], [r_Sb[h]])
            corder = range(32) if ph == 1 else range(31, -1, -1)
            for c in corder:
                i, pb = c // 2, (c % 2) * 64
                for h in range(4):
                    hs = slice(h * 128, (h + 1) * 128)
                    bo, bs = bank(h), bank(4 + h)
                    mm(bo[pb:pb + 64, 0:128], hg_scb[h][pb:pb + 64, i, :], vtok[pb:pb + 64, i, hs], True, False,
                       [r_hgp[h], r_vtok], [RB(h)])
                    mm(bo[pb:pb + 64, 0:128], hg_qh[h][:, c * 64:(c + 1) * 64], Sb[:, h, :], False, True,
                       [r_hgp[h], r_Sb[h]], [RB(h)])
                    mm(bs[:, 0:128], hg_ktok[h][pb:pb + 64, i, :], vtok[pb:pb + 64, i, hs], True, True,
                       [r_hgp[h], r_vtok], [RB(4 + h)])
                    if ph == 1:
                        cp("act", o1[pb:pb + 64, i, hs], bo[pb:pb + 64, 0:128], [RB(h)], [r_o1[i]])
                    else:
                        tt("dve", o1[pb:pb + 64, i, hs], bo[pb:pb + 64, 0:128], o1[pb:pb + 64, i, hs], ALU.add,
                           [RB(h), r_o1[i]], [r_o1[i]])
                    stt(S32[:, h, :], S32[:, h, :], hg_dec[h][:, c:c + 1], bs[:, 0:128], ALU.mult, ALU.add,
                        [r_S32[h], r_hgp[h], RB(4 + h)], [r_S32[h]])
                    cp("act", Sb[:, h, :], S32[:, h, :], [r_S32[h]], [r_Sb[h]])
            if ph == 2:
                wload(wB[0], 512, w_in_d[l, :, C_BG:C_BG + 512], 0, 512, KC, r_wB[0], "ld_wB0")
                P.fence()
                ss64 = small[:, 96:160]
                r_ss64 = P.R("ss64")
                sqj2 = av(T0, [128, 128], F32)
                r_sqj2 = P.R("sqj2")
                ybf4 = av(T0 + 2 * K, [128, 4, 512], BF16)
                tmpf = av(T0 + 6 * K, [128, 512], F32)
                sg2 = [av(T0 + 8 * K + q * 2 * K, [128, 512], F32) for q in range(2)]
                r_ybf4, r_tmpf = P.R("ybf4"), P.R("tmpf")
                r_sg2 = [P.R("sg2", q) for q in range(2)]
                pb2 = [ps_all[:, (6 + q) * 512:(7 + q) * 512].bitcast(BF16) for q in range(2)]
                for i in range(NT):
                    for h in range(4):
                        act(sqj2, o1[:, i, h * 128:(h + 1) * 128], AF.Square, [r_o1[i]], [P.R("ss64e", i * 4 + h)],
                            accum=small[:, 96 + i * 4 + h: 97 + i * 4 + h])
                act(ss64, ss64, AF.Ln, [P.R("ss64e", q) for q in range(64)] + [r_cst], [r_ss64], scale=1.0 / 128,
                    bias=cc(CI_EPS))
                act(ss64, ss64, AF.Exp, [r_ss64], [r_ss64], scale=-0.5)
                nq = 0
                for tb in range(NB):
                    for t4 in range(4):
                        i = tb * 4 + t4
                        tt("dve", tmpf.rearrange("p (h d) -> p h d", h=4), o1[:, i, :].rearrange("p (h d) -> p h d", h=4),
                           bcast_free(ss64[:, i * 4:(i + 1) * 4], 4, 128), ALU.mult, [r_o1[i], r_ss64], [r_tmpf])
                        tt("dve", ybf4[:, t4, :], tmpf, hgn[:], ALU.mult, [r_tmpf, r_hgn], [r_ybf4])
                    for h in range(4):
                        q = nq % 2
                        nq += 1
                        proj_fm(bank(q), wBg, h * 128, tb, [r_wB[0]], RB(q))
                        act(sg2[q], bank(q), AF.Sigmoid, [RB(q)], [r_sg2[q]])
                        for t4 in range(4):
                            tr(pb2[q][:, t4 * 128:(t4 + 1) * 128], ybf4[:, t4, h * 128:(h + 1) * 128], [r_ybf4, r_id], [RB(6 + q)])
                        tt("dve", o_bT[:, h, tb * 512:(tb + 1) * 512], pb2[q][:, 0:512], sg2[q], ALU.mult,
                           [RB(6 + q), r_sg2[q]], [r_obT])
            P.fence()
        if l == 0:
            dump("obT", o_bT[:, 0, :], [r_obT])
        if stop == 'S4':
            P.fence()
            break

        o_aT = av(108 * K, [128, 4, T], BF16)
        r_oaT = P.R("o_aT")
        qT = [av(j * 4 * K, [128, T], BF16) for j in range(4)]
        Kf2 = [av(16 * K + j * 8 * K, [128, 2, T], BF16) for j in range(2)]
        Vf2 = [av(32 * K + j * 9 * K, [128, 2, NT * 130], BF16) for j in range(2)]
        Et = [av(50 * K + j * 2 * K, [128, 1024], BF16) for j in range(3)]
        ep_t0 = av(56 * K, [128, 4, 128], F32)
        ep_o = av(58 * K, [128, 4, 128], F32)
        ep_y = av(60 * K, [128, 4, 128], BF16)
        ep_sq = av(61 * K, [128, 128], F32)
        ep_acc = av(62 * K, [128, 1040], F32)
        Wp = av(50 * K, [128, KC, 512], BF16)
        rt1 = [av(58 * K + j * 2 * K, [128, 512], F32) for j in range(2)]
        rt2 = [av(62 * K + j * 2 * K, [128, 512], F32) for j in range(2)]
        r_epa, r_epo, r_epy = P.R("ep_acc"), P.R("ep_o"), P.R("ep_y")
        r_qT = [P.R("qT", j) for j in range(4)]
        r_Kf2 = [P.R("Kf", j) for j in range(2)]
        r_Vf2 = [P.R("Vf", j) for j in range(2)]
        r_Et = [P.R("Et", j) for j in range(3)]
        r_ep = P.R("ep")
        r_Wp = P.R("Wp")
        wload(wA[0], 512, w_in_d[l, :, C_AQ:C_AQ + 512], 0, 512, KC, r_wA[0], "ld_wA0")
        memset("pool", Wp, 0.0, [r_Wp])
        build_partner(wAv[0], r_wA[0])

        def kv_load(h):
            kv3 = kv_all[h].rearrange("(r p) c -> p r c", p=128)
            dma("sp", Kf2[h % 2], kv3[:, :, 0:2048], [r_kva[h]], [r_Kf2[h % 2]], "ld_Kf%d" % (h % 2))
            dma("sp", Vf2[h % 2], kv3[:, :, 2048:KVH], [r_kva[h]], [r_Vf2[h % 2]], "ld_Vf%d" % (h % 2))

        kv_load(0)
        kv_load(1)
        n_e = 0
        for h in range(4):
            for tb in range(NB):
                b0, b1 = 4 + (tb % 2) * 2, 5 + (tb % 2) * 2
                proj_fm(bank(b0), wAv[0], h * 128, tb, [r_wA[0]], RB(b0))
                proj_fm(bank(b1), Wp, h * 128, tb, [r_Wp], RB(b1))
                j = tb % 2
                r1, r2 = P.R("rt1", j), P.R("rt2", j)
                tt("dve", rt1[j], bank(b0), ctab[:, tb * 512:(tb + 1) * 512], ALU.mult, [RB(b0), r_tab], [r1])
                tt("dve", rt2[j], bank(b1), stab[:, tb * 512:(tb + 1) * 512], ALU.mult, [RB(b1), r_tab], [r2])
                tt("pool", qT[h][:, tb * 512:(tb + 1) * 512], rt1[j], rt2[j], ALU.add, [r1, r2], [r_qT[h]])
            if l == 0 and h == 0:
                dump("qT", qT[0], [r_qT[0]])
        P.fence()
        for h in range(4):
            jq = h
            Kf, Vf, r_Kf, r_Vf = Kf2[h % 2], Vf2[h % 2], r_Kf2[h % 2], r_Vf2[h % 2]
            steps = [(qb, kt) for qb in range(NB) for kt in range(32)]

            def emit_qk(s):
                qb, kt = steps[s]
                sbi = s % 2
                r_, i_ = kt // 16, kt % 16
                for c in range(2):
                    ps_ = slice(c * 64, (c + 1) * 64)
                    bnk = 2 * sbi + c
                    mm(bank(bnk), Kf[ps_, r_, i_ * 128:(i_ + 1) * 128], qT[jq][ps_, qb * 512:(qb + 1) * 512], True, True,
                       [r_Kf, r_qT[jq]], [RB(bnk)])

            def emit_exp_pv(s):
                nonlocal n_e
                qb, kt = steps[s]
                sbi = s % 2
                r_, i_ = kt // 16, kt % 16
                eb = n_e % 3
                n_e += 1
                act(Et[eb], ps_all[:, sbi * 1024:(sbi + 1) * 1024], AF.Exp, [RB(2 * sbi), RB(2 * sbi + 1)],
                    [r_Et[eb]], scale=0.125)
                for c in range(2):
                    for jj in range(4):
                        a_ = jj * 2 + c
                        bnk, col = 4 + a_ // 3, (a_ % 3) * 130
                        first = (kt == 0 and c == 0 and jj in (0, 2, 3))
                        mm(bank(bnk)[:, col:col + 130], Et[eb][:, c * 512 + jj * 128: c * 512 + (jj + 1) * 128],
                           Vf[:, r_, i_ * 130:(i_ + 1) * 130], first, kt == 31,
                           [r_Et[eb], r_Vf], [RB(bnk)], skip=True)

            def emit_evac(qb):
                cp("dve", ep_acc[:, 0:390], bank(4)[:, 0:390], [RB(4)], [r_epa])
                cp("dve", ep_acc[:, 390:780], bank(5)[:, 0:390], [RB(5)], [r_epa])
                cp("dve", ep_acc[:, 780:1040], bank(6)[:, 0:260], [RB(6)], [r_epa])

            def emit_epilogue(qb):
                acc3 = ep_acc.rearrange("p (a d) -> p a d", d=130)
                acc4 = ep_acc.rearrange("p (j c d) -> p j c d", c=2, d=130)
                rr = small[:, 48:56]
                rr2 = rr.rearrange("p (j c) -> p j c", c=2)
                r_rr = P.R("rr")
                recip(rr, acc3[:, :, 128], [r_epa], [r_rr])
                tt("dve", rr2[:, :, 1], rr2[:, :, 1], nlam[:, l:l + 1].broadcast_to([128, 4]), ALU.mult,
                   [r_rr, r_nlam], [r_rr])
                tt("dve", ep_t0, acc4[:, :, 0, 0:128], bcast_free(rr2[:, :, 0], 4, 128), ALU.mult, [r_epa, r_rr], [r_ep])
                tt("dve", ep_o, acc4[:, :, 1, 0:128], bcast_free(rr2[:, :, 1], 4, 128), ALU.mult, [r_epa, r_rr], [r_epo])
                tt("dve", ep_o, ep_o, ep_t0, ALU.add, [r_epo, r_ep], [r_epo])
                s2 = small[:, 56:60]
                r_s2 = P.R("ss2")
                for jj in range(4):
                    act(ep_sq, ep_o[:, jj, :], AF.Square, [r_epo], [P.R("ep_sq"), r_s2], accum=small[:, 56 + jj:57 + jj])
                act(s2, s2, AF.Ln, [r_s2, r_cst], [r_s2], scale=1.0 / 128, bias=cc(CI_EPS))
                act(s2, s2, AF.Exp, [r_s2], [r_s2], scale=-0.5)
                tt("dve", ep_o, ep_o, bcast_free(s2, 4, 128), ALU.mult, [r_epo, r_s2], [r_epo])
                tt("dve", ep_y, ep_o, bcast_mid(dan[:, h * 128:(h + 1) * 128], 4), ALU.mult, [r_epo, r_dan], [r_epy])
                for jj in range(4):
                    tr(psbf[:, jj * 128:(jj + 1) * 128], ep_y[:, jj, :], [r_epy, r_id], [RB(7)])
                cp("dve", o_aT[:, h, qb * 512:(qb + 1) * 512], psbf[:, 0:512], [RB(7)], [r_oaT])

            emit_qk(0)
            pending = None
            for s in range(len(steps)):
                if s + 1 < len(steps):
                    emit_qk(s + 1)
                emit_exp_pv(s)
                qb, kt = steps[s]
                if pending is not None and s == pending[1]:
                    emit_epilogue(pending[0])
                    pending = None
                if kt == 31:
                    emit_evac(qb)
                    if qb == NB - 1:
                        emit_epilogue(qb)
                    else:
                        pending = (qb, s + 6)
            if h + 2 < 4:
                kv_load(h + 2)
        if l == 0:
            dump("oaT", o_aT[:, 0, :], [r_oaT])
        P.fence()
        if stop == 'S3':
            P.fence()
            break

        mg = [av(92 * K + j * 4 * K, [128, T], BF16) for j in range(4)]
        sg5 = [av(64 * K + j * 2 * K, [128, 512], F32) for j in range(4)]
        r_mg = [P.R("mg", j) for j in range(4)]
        r_sg5 = [P.R("sg5", j) for j in range(4)]
        for i in range(NT):
            dma("sp", xbuf[:, i, :], xd[i * 128:(i + 1) * 128, :], [r_xd[i]], [r_x[i]], "ld_x%d" % i)
        n5 = 0
        r_wo = [P.R("wo5", q) for q in range(4)]

        def wo_view(ch):
            return wB[ch % 2][:, 1024 + ((ch // 2) % 2) * 1024: 2048 + ((ch // 2) % 2) * 1024]

        def s5_load(ch):
            j = ch % 2
            wload(wA[j], 256, w_in_d[l, :, C_GA + ch * 128:C_GA + (ch + 1) * 128], 0, 128, KC, r_wA[j], "ld_wA%d" % j)
            wload(wA[j], 256, w_in_d[l, :, C_GB + ch * 128:C_GB + (ch + 1) * 128], 128, 128, KC, r_wA[j], "ld_wA%d" % j)
            wload(wB[j], 256, w_a_d[l, :, ch * 128:(ch + 1) * 128], 0, 128, 4, r_wB[j], "ld_wB%d" % j)
            wload(wB[j], 256, w_b_d[l, :, ch * 128:(ch + 1) * 128], 128, 128, 4, r_wB[j], "ld_wB%d" % j)
            dma("pool", wo_view(ch), w_o_d[l, ch * 128:(ch + 1) * 128, :], [], [r_wo[ch % 4]], "ld_wo%d" % (ch % 4))

        s5_load(0)
        for ch in range(KC):
            j = ch % 2
            jm = ch % 4
            wv = wA[j][:, 0:KC * 256].rearrange("p (k c) -> p k c", k=KC)
            wab = wB[j][:, 0:4 * 256].rearrange("p (k c) -> p k c", k=4)
            if ch + 1 < KC:
                s5_load(ch + 1)
            for tb in range(NB):
                sl = slice(tb * 512, (tb + 1) * 512)
                b4 = (n5 % 2) * 4
                sa, sb_ = (n5 % 2) * 2, (n5 % 2) * 2 + 1
                n5 += 1
                proj_fm(bank(b4), wv, 0, tb, [r_wA[j]], RB(b4))
                proj_fm(bank(b4 + 1), wv, 128, tb, [r_wA[j]], RB(b4 + 1))
                for k in range(4):
                    mm(bank(b4 + 2), wab[:, k, 0:128], o_aT[:, k, sl], k == 0, k == 3, [r_wB[j], r_oaT], [RB(b4 + 2)])
                for k in range(4):
                    mm(bank(b4 + 3), wab[:, k, 128:256], o_bT[:, k, sl], k == 0, k == 3, [r_wB[j], r_obT], [RB(b4 + 3)])
                act(sg5[sa], bank(b4), AF.Sigmoid, [RB(b4)], [r_sg5[sa]])
                act(sg5[sb_], bank(b4 + 1), AF.Sigmoid, [RB(b4 + 1)], [r_sg5[sb_]])
                tt("dve", sg5[sa], bank(b4 + 2), sg5[sa], ALU.mult, [RB(b4 + 2), r_sg5[sa]], [r_sg5[sa]])
                tt("dve", sg5[sb_], bank(b4 + 3), sg5[sb_], ALU.mult, [RB(b4 + 3), r_sg5[sb_]], [r_sg5[sb_]])
                tt("pool", mg[jm][:, sl], sg5[sa], sg5[sb_], ALU.add, [r_sg5[sa], r_sg5[sb_]], [r_mg[jm]])
            if ch % 2 == 1:
                for i in range(NT):
                    for hf in range(2):
                        b = (i * 2 + hf) % 8
                        for q_, (cj, wj) in enumerate((((ch - 1) % 4, (ch - 1) % 2), (ch % 4, ch % 2))):
                            mm(bank(b), mg[cj][:, i * 128:(i + 1) * 128], wo_view(ch - 1 + q_)[:, hf * 512:(hf + 1) * 512],
                               q_ == 0, q_ == 1, [r_mg[cj], r_wo[(ch - 1 + q_) % 4]], [RB(b)])
                        tt("dve", xbuf[:, i, hf * 512:(hf + 1) * 512], bank(b), xbuf[:, i, hf * 512:(hf + 1) * 512], ALU.add,
                           [RB(b), r_x[i]], [r_x[i]])
        if l == 0:
            dump("x1", xbuf[:, 0, :], r_x)
        P.fence()
        if stop == 'S5':
            P.fence()
            break

        rms_to_hT(gF, l)
        P.fence()
        actb = [av(64 * K + j * 16 * K, [128, 4, T], BF16) for j in range(2)]
        r_actb = [P.R("actb", j) for j in range(2)]
        su = [av(96 * K + j * 2 * K, [128, 512], F32) for j in range(2)]
        r_su = [P.R("su", j) for j in range(2)]
        hgroups = [(0, 4), (4, 4), (8, 4), (12, 4), (16, 3), (19, 3)]
        for gi, (c0, ncg) in enumerate(hgroups):
            j = gi % 2
            wgu = wA[j][:].rearrange("p (k c) -> p k c", k=KC)
            ncol = ncg * 128
            wload(wA[j], 512, w_g_d[l, :, c0 * 128:c0 * 128 + ncol], 0, ncol, KC, r_wA[j], "ld_wA%d" % j)
            wgu2 = wB[j][:].rearrange("p (k c) -> p k c", k=KC)
            wload(wB[j], 512, w_u_d[l, :, c0 * 128:c0 * 128 + ncol], 0, ncol, KC, r_wB[j], "ld_wB%d" % j)
            for cg in range(ncg):
                for tb in range(NB):
                    sl = slice(tb * 512, (tb + 1) * 512)
                    n = (cg * NB + tb) % 2
                    proj_fm(bank(2 * n), wgu, cg * 128, tb, [r_wA[j]], RB(2 * n))
                    proj_fm(bank(2 * n + 1), wgu2, cg * 128, tb, [r_wB[j]], RB(2 * n + 1))
                    act(su[n], bank(2 * n), AF.Silu, [RB(2 * n)], [r_su[n]])
                    tt("dve", actb[j][:, cg, sl], bank(2 * n + 1), su[n], ALU.mult, [RB(2 * n + 1), r_su[n]], [r_actb[j]])
            wdn = av(100 * K, [128, 4, D], BF16)
            r_wdn = P.R("wdn")
            dma("pool", wdn[:, 0:ncg, :], w_d_d[l, c0 * 128:(c0 + ncg) * 128, :].rearrange("(k p) c -> p k c", p=128),
                [], [r_wdn], "ld_wdn")
            for i in range(NT):
                for hf in range(2):
                    b = 4 + (i * 2 + hf) % 4
                    for cg in range(ncg):
                        mm(bank(b), actb[j][:, cg, i * 128:(i + 1) * 128], wdn[:, cg, hf * 512:(hf + 1) * 512],
                           cg == 0, cg == ncg - 1, [r_actb[j], r_wdn], [RB(b)])
                    tt("dve", xbuf[:, i, hf * 512:(hf + 1) * 512], bank(b), xbuf[:, i, hf * 512:(hf + 1) * 512], ALU.add,
                       [RB(b), r_x[i]], [r_x[i]])
        if l == 0:
            dump("x2", xbuf[:, 0, :], r_x)
        P.fence()

    gNb = av(64 * K, [128, D], F32)
    r_gNb = P.R("gNb")
    fin = [av(68 * K + j * 4 * K, [128, D], F32) for j in range(2)]
    r_fin = [P.R("fin", j) for j in range(2)]
    sqf = av(76 * K, [128, D], F32)
    gNfull_d = din("gNfull", [1, D])
    dma("sp", gNb, bass.AP(gNfull_d.tensor, 0, [[0, 128], [1, D]]), [], [r_gNb], "ld_gNb")
    r_out = P.R("out")
    for i in range(NT):
        j = i % 2
        ssc = small[:, 16 + i:17 + i]
        r_ss = P.R("ss", i)
        act(sqf, xbuf[:, i, :], AF.Square, [r_x[i]], [P.R("sqf"), r_ss], accum=ssc)
        act(ssc, ssc, AF.Ln, [r_ss, r_cst], [r_ss], scale=1.0 / D, bias=cc(CI_EPS))
        act(ssc, ssc, AF.Exp, [r_ss], [r_ss], scale=-0.5)
        stt(fin[j], xbuf[:, i, :], ssc, gNb, ALU.mult, ALU.mult, [r_x[i], r_ss, r_gNb], [r_fin[j]])
        dma("sp", out_d[i * 128:(i + 1) * 128, :], fin[j], [r_fin[j]], [r_out], "st_out%d" % j)
    P.wait_all("sp")
    P.build(nc, st)
    st.close()
    return nc


_PROG_CACHE = {}


def _consts():
    c = np.zeros((128, NCST), np.float32)
    c[:, CI_ID:CI_ID + 128] = np.eye(128, dtype=np.float32)
    s = np.arange(128) % 64
    t = np.arange(64)
    c[:, CI_MF:CI_MF + 64] = (s[:, None] <= t[None, :]).astype(np.float32)
    c[:, CI_MB:CI_MB + 64] = (s[:, None] >= t[None, :]).astype(np.float32)
    d = np.arange(128) % 64
    inv = 500000.0 ** (-(np.arange(0, 16, 2, dtype=np.float32) / 16.0))
    f = np.zeros(128, np.float32)
    f[d < 16] = inv[d[d < 16] % 8]
    c[:, CI_FREQ] = f
    c[:, CI_SGN] = np.where(d < 8, -1.0, 1.0)
    c[:, CI_EPS] = EPS
    c[:, CI_HPI] = math.pi / 2
    c[:, CI_ONE] = 1.0
    return c


def _prep_inputs(inputs):
    x = np.asarray(inputs["x"], np.float32)
    pos = np.asarray(inputs["positions"], np.int32)
    w_in = np.asarray(inputs["w_in"], np.float32)
    w_in_odd = np.concatenate([w_in[:, :, :C_G1], w_in[:, :, C_G2:C_BI], w_in[:, :, C_G1:C_G2], w_in[:, :, C_BI:]], axis=2)
    w_in_odd = np.ascontiguousarray(w_in_odd)
    lbl = np.asarray(inputs["hg_lb_logits"], np.float32)
    fm = lambda a: np.ascontiguousarray(a.reshape(-1, 128).T)
    lbl_even = fm(lbl.reshape(2, L, 4, 128).reshape(32, 128))
    lbl_odd = fm(lbl[::-1].reshape(32, 128))
    cbase = _consts()
    shared = {
        "gA": fm(np.asarray(inputs["attn_norm"], np.float32).reshape(L * 8, 128)),
        "gF": fm(np.asarray(inputs["ffn_norm"], np.float32).reshape(L * 8, 128)),
        "gN": fm(np.asarray(inputs["final_norm"], np.float32).reshape(8, 128)),
        "gNfull": np.asarray(inputs["final_norm"], np.float32).reshape(1, D),
        "da_norm": np.asarray(inputs["da_norm"], np.float32),
        "hg_norm": np.asarray(inputs["hg_norm"], np.float32),
        "da_lambda": np.asarray(inputs["da_lambda"], np.float32).reshape(1, L * 256),
        "w_a": np.asarray(inputs["w_a"], np.float32),
        "w_b": np.asarray(inputs["w_b"], np.float32),
        "w_o": np.asarray(inputs["w_o"], np.float32),
        "w_gate": np.asarray(inputs["w_gate"], np.float32),
        "w_up": np.asarray(inputs["w_up"], np.float32),
        "w_down": np.asarray(inputs["w_down"], np.float32),
    }
    maps = []
    for c in range(NCORES):
        b, half = c // 2, c % 2
        if half == 0:
            xs = x[b, :T]
            ps = pos[b, :T]
        else:
            xs = x[b][::-1][:T]
            ps = pos[b][::-1][:T]
        cs = cbase.copy()
        cs[:, CI_SEL + (1 - half)] = 1.0
        m = dict(shared)
        m.update({"x": np.ascontiguousarray(xs), "pos": np.ascontiguousarray(ps).reshape(1, T),
                  "cst": cs, "w_in": w_in if half == 0 else w_in_odd,
                  "lbl": lbl_even if half == 0 else lbl_odd})
        maps.append(m)
    return maps


def _assemble(results, key="out"):
    out = np.empty((4, 4096, D), np.float32)
    for c in range(NCORES):
        b, half = c // 2, c % 2
        y = np.asarray(results[c][key], np.float32)
        if half == 0:
            out[b, :T] = y
        else:
            out[b, T:] = y[::-1]
    return out


def kernel(**inputs):
    if "nc" not in _PROG_CACHE:
        _PROG_CACHE["nc"] = build_program(L)
    nc = _PROG_CACHE["nc"]
    maps = _prep_inputs(inputs)
    res = run_bass_kernel_spmd(nc, maps, core_ids=list(range(NCORES)))
    return _assemble(res.results)
```

```python
import math
from contextlib import ExitStack
import numpy as np
import concourse.bass as bass
import concourse.mybir as mybir
from concourse.bass_utils import run_bass_kernel_spmd

F32 = mybir.dt.float32
BF16 = mybir.dt.bfloat16
I32 = mybir.dt.int32
AF = mybir.ActivationFunctionType
ALU = mybir.AluOpType

NCORES = 8
T = 2048
NT = T // 128
NB = T // 512
D = 1024
KC = 8
L = 4
FF = 2816
INW = 6144
EPS = 1e-6
C_AQ, C_AK, C_AV, C_BQ, C_G1, C_G2, C_BI, C_BG, C_GA, C_GB = 0, 512, 1024, 1536, 2048, 2560, 3072, 3584, 4096, 5120
KVW = 4 * 2048 + 4 * 16 * 130

CI_ID = 0
CI_MF = 128
CI_MB = 192
CI_FREQ = 256
CI_SGN = 257
CI_EPS = 258
CI_HPI = 259
CI_ONE = 260
CI_SEL = 261
NCST = 264


class Res:
    __slots__ = ("name", "w", "r", "excl")

    def __init__(self, name, excl=False):
        self.name, self.w, self.r, self.excl = name, None, {}, excl


class Prog:
    ENGS = ("pe", "act", "dve", "pool", "sp")

    def __init__(self):
        self.ops = {e: [] for e in self.ENGS}
        self.cnt = {e: 0 for e in self.ENGS}
        self.seen = {e: {} for e in self.ENGS}
        self.fence_deps = {}
        self.res = {}
        self.epoch = {e: 0 for e in self.ENGS}
        self.ekey = {e: e for e in self.ENGS}
    EPOCH_MAX = 12000
    STRICT_SAME_ENGINE = True

    def _own(self, k, eng):
        return k == eng or (isinstance(k, str) and k.startswith(eng + "@"))

    def R(self, *key, excl=False):
        r = self.res.get(key)
        if r is None:
            r = self.res[key] = Res(key, excl)
        return r

    def alias(self, dst, src):
        for k, v in src.r.items():
            if dst.r.get(k, 0) < v:
                dst.r[k] = v
        if src.w is not None and dst.r.get(src.w[0], 0) < src.w[1]:
            dst.r[src.w[0]] = src.w[1]

    def fence(self):
        self.fence_deps = {k: v for k, v in self.cnt.items() if not str(k).startswith("cc_kv")}

    def emit(self, eng, fn, reads=(), writes=(), dma=None, inc=None, nofence=False):
        deps = {} if nofence else dict(self.fence_deps)
        raw_own = {}

        def add(d):
            if d is not None and deps.get(d[0], 0) < d[1]:
                deps[d[0]] = d[1]

        writes = list(writes) + [r for r in reads if r.excl]
        for r in reads:
            add(r.w)
            if r.w is not None and self._own(r.w[0], eng):
                raw_own[r.w[0]] = max(raw_own.get(r.w[0], 0), r.w[1])
        for w in writes:
            if not (dma is not None and w.w is not None and w.w[0] == dma):
                add(w.w)
            for k, v in w.r.items():
                add((k, v))
        if dma is None and (eng == "pe" or not self.STRICT_SAME_ENGINE):
            for k in [k for k in deps if self._own(k, eng)]:
                deps.pop(k)
            if eng != "pe":
                deps.update(raw_own)
        waits = []
        seen = self.seen[eng]
        for k, v in deps.items():
            if seen.get(k, 0) < v:
                seen[k] = v
                waits.append((k, v))
        if dma is None:
            if self.cnt[self.ekey[eng]] >= self.EPOCH_MAX:
                self.epoch[eng] += 1
                self.ekey[eng] = "%s@%d" % (eng, self.epoch[eng])
                self.cnt[self.ekey[eng]] = 0
            key, step = self.ekey[eng], 1
        else:
            key, step = dma, (16 if inc is None else inc)
            if key not in self.cnt:
                self.cnt[key] = 0
        self.cnt[key] += step
        val = self.cnt[key]
        self.ops[eng].append((waits, fn, key, step))
        for r in reads:
            if r.r.get(key, 0) < val:
                r.r[key] = val
        for w in writes:
            w.w = (key, val)
            w.r = {}
        return (key, val)

    def wait_all(self, eng):
        waits = []
        for k, v in self.cnt.items():
            if v > 0 and self.seen[eng].get(k, 0) < v and not self._own(k, eng):
                self.seen[eng][k] = v
                waits.append((k, v))
        self.ops[eng].append((waits, None, None, 0))

    def build(self, nc, st):
        sems = {k: st.enter_context(nc.semaphore("s_" + str(k))) for k in self.cnt}
        engmap = {"pe": "tensor", "act": "scalar", "dve": "vector", "pool": "gpsimd", "sp": "sync"}
        block = st.enter_context(nc.Block())
        for e in self.ENGS:
            ops = self.ops[e]

            def body(eng, ops=ops):
                for waits, fn, key, step in ops:
                    for k, v in waits:
                        eng.wait_ge(sems[k], v)
                    if fn is not None:
                        fn(eng).then_inc(sems[key], step)

            getattr(block, engmap[e])(body)


def build_program(n_layers=L, dbg=None, stop=None):
    nc = bass.Bass("TRN2", target_bir_lowering=False)
    P = Prog()
    st = ExitStack()

    def din(name, shape, dt=F32):
        return nc.dram_tensor(name, list(shape), dt, kind="ExternalInput").ap()

    x_d = din("x", [T, D])
    pos_d = din("pos", [1, T], I32)
    cst_d = din("cst", [128, NCST])
    gA_d = din("gA", [128, L * 8])
    gF_d = din("gF", [128, L * 8])
    gN_d = din("gN", [128, 8])
    lbl_d = din("lbl", [128, 32])
    dan_d = din("da_norm", [L, 512])
    hgn_d = din("hg_norm", [L, 512])
    lam_d = din("da_lambda", [1, L * 256])
    w_in_d = din("w_in", [L, D, INW])
    w_a_d = din("w_a", [L, 512, D])
    w_b_d = din("w_b", [L, 512, D])
    w_o_d = din("w_o", [L, D, D])
    w_g_d = din("w_gate", [L, D, FF])
    w_u_d = din("w_up", [L, D, FF])
    w_d_d = din("w_down", [L, FF, D])
    out_d = nc.dram_tensor("out", [T, D], F32, kind="ExternalOutput").ap()
    xd = nc.dram_tensor("xd", [T, D], F32, kind="Internal").ap()
    KVH = 2048 + 2080
    kv_srcs = [nc.dram_tensor("kv_src%d" % h, [128, KVH], BF16, kind="Internal").ap() for h in range(4)]
    kv_all2 = [[nc.dram_tensor("kv_all%d_%d" % (i, h), [256, KVH], BF16, kind="Internal").ap() for h in range(4)]
               for i in range(2)]
    st_src = nc.dram_tensor("st_src", [128, 512], F32, kind="Internal").ap()
    st_all2 = [nc.dram_tensor("st_all%d" % i, [256, 512], F32, kind="Internal").ap() for i in range(2)]
    dbg_out = {}
    if dbg:
        for name, shape in dbg.items():
            dbg_out[name] = nc.dram_tensor("dbg_" + name, list(shape), F32, kind="ExternalOutput").ap()
    groups = [[0, 1], [2, 3], [4, 5], [6, 7]]

    def sb(name, shape, dt):
        return st.enter_context(nc.sbuf_tensor(name, list(shape), dt))

    hT = sb("hT", [128, KC, T], BF16)
    ctab = sb("ctab", [128, T], BF16)
    stab = sb("stab", [128, T], BF16)
    cst = sb("cst_sb", [128, NCST], F32)
    identb = sb("identb", [128, 128], BF16)
    gA = sb("gA_sb", [128, L * 8], F32)
    gF = sb("gF_sb", [128, L * 8], F32)
    gN = sb("gN_sb", [128, 8], F32)
    lb = sb("lb_sb", [128, 32], F32)
    oml = sb("oml_sb", [128, 32], F32)
    nlam = sb("nlam_sb", [128, 8], F32)
    dan = sb("dan_sb", [128, 512], F32)
    hgn = sb("hgn_sb", [128, 512], F32)
    small = sb("small_sb", [128, 192], F32)
    posi_t = sb("posi_t", [128, 512], I32)
    wA = [sb("wA%d" % i, [128, 4096], BF16) for i in range(2)]
    wB = [sb("wB%d" % i, [128, 4096], BF16) for i in range(2)]
    ARENA = 124 * 1024
    arena = sb("arena", [128, ARENA // 2], BF16)
    ps_all = st.enter_context(nc.psum_tensor("ps_all", [128, 4096], F32))

    def av(off, shape, dt):
        n = int(np.prod(shape[1:]))
        if dt == F32:
            a = arena[:, off // 2: off // 2 + 2 * n].bitcast(F32)
        elif dt == I32:
            a = arena[:, off // 2: off // 2 + 2 * n].bitcast(I32)
        else:
            a = arena[:, off // 2: off // 2 + n]
        if len(shape) == 3:
            a = a.rearrange("p (a b) -> p a b", a=shape[1])
        elif len(shape) == 4:
            a = a.rearrange("p (a b c) -> p a b c", a=shape[1], b=shape[2])
        return a

    K = 1024
    xbuf = av(0, [128, NT, D], F32)

    def bank(b):
        return ps_all[:, b * 512:(b + 1) * 512]

    def RB(b):
        return P.R("bank", b, excl=True)

    psbf = ps_all[:, 7 * 512: 8 * 512].bitcast(BF16)

    def bcast_free(ap2, n_outer, n_inner):
        e = ap2.ap
        return bass.AP(ap2.tensor, ap2.offset, [list(e[0]), list(e[1]), [0, n_inner]])

    def bcast_mid(ap2, n_mid):
        e = ap2.ap
        return bass.AP(ap2.tensor, ap2.offset, [list(e[0]), [0, n_mid], list(e[1])])

    def mm(out, lhsT, rhs, start, stop, reads, writes, skip=False):
        P.emit("pe", lambda e: e.matmul(out, lhsT, rhs, start=start, stop=stop, skip_group_check=skip),
               reads, writes)

    def tr(out, in_, reads, writes):
        P.emit("pe", lambda e: e.transpose(out, in_, identb[:]), reads, writes)

    def act(out, in_, func, reads, writes, scale=None, bias=None, accum=None):
        kw = {}
        if scale is not None:
            kw["scale"] = scale
        if bias is not None:
            kw["bias"] = bias
        if accum is not None:
            kw["accum_out"] = accum
        P.emit("act", lambda e: e.activation(out, in_, func, **kw), reads, writes)

    def tt(eng, out, in0, in1, op, reads, writes):
        P.emit(eng, lambda e: e.tensor_tensor(out, in0, in1, op), reads, writes)

    def ts(eng, out, in0, s1, s2, op0, op1, reads, writes):
        if op1 is None:
            P.emit(eng, lambda e: e.tensor_scalar(out, in0, s1, None, op0), reads, writes)
        else:
            P.emit(eng, lambda e: e.tensor_scalar(out, in0, s1, s2, op0, op1), reads, writes)

    def stt(out, in0, scalar, in1, op0, op1, reads, writes):
        P.emit("dve", lambda e: e.scalar_tensor_tensor(out, in0, scalar, in1, op0, op1), reads, writes)

    def cp(eng, out, in_, reads, writes):
        if eng == "act":
            P.emit("act", lambda e: e.copy(out, in_), reads, writes)
        else:
            P.emit(eng, lambda e: e.tensor_copy(out, in_), reads, writes)

    def recip(out, in_, reads, writes):
        P.emit("dve", lambda e: e.reciprocal(out, in_), reads, writes)

    def memset(eng, ap, c, writes):
        P.emit(eng, lambda e: e.memset(ap, c), (), writes)

    def dma(q, out, in_, reads, writes, sem, nofence=False, **kw):
        P.emit(q, lambda e: e.dma_start(out=out, in_=in_, **kw), reads, writes, dma=sem, nofence=nofence)

    def dump(name, src_ap, reads):
        if name in dbg_out:
            dma("pool", dbg_out[name], src_ap, reads, [P.R("dbg", name)], "dbg_" + name)

    def wload(slot_t, ncols_total, dram_view, col_off, ncols, kch, res, sem):
        dst = slot_t[:, 0:kch * ncols_total].rearrange("p (k c) -> p k c", k=kch)[:, :, col_off:col_off + ncols]
        dma("pool", dst, dram_view.rearrange("(k p) c -> p k c", p=128), [], [res], sem,
            nofence=False)

    r_cst = P.R("cst")
    r_hT = [P.R("hT", i) for i in range(NT)]
    r_x = [P.R("x", i) for i in range(NT)]
    cc = lambda i: cst[:, i:i + 1]

    dma("sp", cst[:], cst_d, [], [r_cst], "ld_cst")
    r_par = P.R("params")
    for dst_t, src in ((gA, gA_d), (gF, gF_d), (gN, gN_d), (lb, lbl_d)):
        dma("sp", dst_t[:], src, [], [r_par], "ld_par")
    cp("dve", identb[:], cst[:, CI_ID:CI_ID + 128], [r_cst], [P.R("identb")])
    r_id = P.R("identb")
    for i in range(NT):
        dma("sp", xbuf[:, i, :], x_d[i * 128:(i + 1) * 128, :], [], [r_x[i]], "ld_x%d" % i)

    r_small = P.R("small")
    lbv = lb[:].rearrange("p (r l h) -> p r l h", r=2, l=4)
    act(lb[:], lb[:], AF.Exp, [r_par], [r_par])
    ssum = small[:, 0:8].rearrange("p (r h) -> p r h", r=2)
    tt("dve", ssum, lbv[:, :, 0, :], lbv[:, :, 1, :], ALU.add, [r_par], [r_small])
    tt("dve", ssum, ssum, lbv[:, :, 2, :], ALU.add, [r_par, r_small], [r_small])
    tt("dve", ssum, ssum, lbv[:, :, 3, :], ALU.add, [r_par, r_small], [r_small])
    recip(ssum, ssum, [r_small], [r_small])
    for l in range(4):
        tt("dve", lbv[:, :, l, :], lbv[:, :, l, :], ssum, ALU.mult, [r_par, r_small], [r_par])
    tt("dve", lbv[:, :, 2, :], lbv[:, :, 2, :], lbv[:, :, 1, :], ALU.add, [r_par], [r_par])
    tt("dve", lbv[:, :, 3, :], lbv[:, :, 3, :], lbv[:, :, 2, :], ALU.add, [r_par], [r_par])
    memset("dve", lbv[:, :, 0, :], 0.0, [r_par])
    r_oml = P.R("oml")
    ts("dve", oml[:], lb[:], -1.0, 1.0, ALU.mult, ALU.add, [r_par], [r_oml])

    lamt = av(64 * K, [128, L * 256], F32)
    r_lam = P.R("lamt")
    dma("sp", lamt, bass.AP(lam_d.tensor, 0, [[0, 128], [1, L * 256]]), [], [r_lam], "ld_lam")
    lam4 = lamt.rearrange("p (l f d) -> p l f d", l=L, f=4)
    lprod = av(72 * K, [128, L, 2, 64], F32)
    r_lp = P.R("lprod")
    for l in range(L):
        tt("dve", lprod[:, l, 0, :], lam4[:, l, 0, :], lam4[:, l, 1, :], ALU.mult, [r_lam], [r_lp])
        tt("dve", lprod[:, l, 1, :], lam4[:, l, 2, :], lam4[:, l, 3, :], ALU.mult, [r_lam], [r_lp])
    r_nlam = P.R("nlam")
    lsum = small[:, 8:16]
    for l in range(L):
        for j in range(2):
            act(lprod[:, l, j, :], lprod[:, l, j, :], AF.Identity, [r_lp], [r_lp, r_small],
                accum=small[:, 8 + l * 2 + j: 9 + l * 2 + j])
    act(lsum, lsum, AF.Exp, [r_small], [r_small])
    for l in range(L):
        lam_init = 0.8 - 0.6 * math.exp(-0.3 * l)
        stt(nlam[:, l:l + 1], small[:, 9 + 2 * l:10 + 2 * l], -lam_init, small[:, 8 + 2 * l:9 + 2 * l],
            ALU.add, ALU.subtract, [r_small], [r_nlam])

    P.fence()
    r_tab = P.R("tabs")
    CB = 512
    for cb in range(T // CB):
        cs_ = slice(cb * CB, (cb + 1) * CB)
        ang = av(72 * K, [128, CB], F32)
        t1 = av(80 * K, [128, CB], F32)
        t2 = av(88 * K, [128, CB], F32)
        t3 = av(96 * K, [128, CB], F32)
        r_pi, r_ang, r_t1, r_t2, r_t3 = P.R("posi"), P.R("ang"), P.R("t1"), P.R("t2"), P.R("t3")
        dma("sp", posi_t[:], bass.AP(pos_d.tensor, cb * CB, [[0, 128], [1, CB]]), [], [r_pi], "ld_pos")
        cp("dve", ang, posi_t[:], [r_pi], [r_ang])
        ts("dve", ang, ang, cc(CI_FREQ), None, ALU.mult, None, [r_ang, r_cst], [r_ang])
        ts("dve", t1, ang, 1.0 / (2 * math.pi), None, ALU.mult, None, [r_ang], [r_t1])
        cp("dve", posi_t[:], t1, [r_t1], [r_pi])
        cp("dve", t1, posi_t[:], [r_pi], [r_t1])
        stt(ang, t1, -2 * math.pi, ang, ALU.mult, ALU.add, [r_t1, r_ang], [r_ang])
        act(t1, ang, AF.Sin, [r_ang], [r_t1], scale=0.25)
        act(t2, ang, AF.Sin, [r_ang, r_cst], [r_t2], scale=0.25, bias=cc(CI_HPI))
        tt("dve", t3, t1, t2, ALU.mult, [r_t1, r_t2], [r_t3])
        ts("dve", t3, t3, 2.0, None, ALU.mult, None, [r_t3], [r_t3])
        tt("dve", t2, t1, t1, ALU.mult, [r_t1], [r_t2])
        ts("dve", t2, t2, -2.0, 1.0, ALU.mult, ALU.add, [r_t2], [r_t2])
        tt("dve", t1, t3, t2, ALU.mult, [r_t3, r_t2], [r_t1])
        ts("dve", stab[:, cs_], t1, 2.0, cc(CI_SGN), ALU.mult, ALU.mult, [r_t1, r_cst], [r_tab])
        tt("dve", t2, t3, t3, ALU.mult, [r_t3], [r_t2])
        ts("dve", ctab[:, cs_], t2, -2.0, 1.0, ALU.mult, ALU.add, [r_t2], [r_tab])
    dump("ctab", ctab[:], [r_tab])
    dump("stab", stab[:], [r_tab])
    P.fence()

    if stop == 'setup':
        n_layers = 0
    def rms_to_hT(gain_tile, l):
        xs = [av(64 * K + j * 2 * K, [128, D], BF16) for j in range(2)]
        r_xs = [P.R("xs", j) for j in range(2)]
        sqs = [av(68 * K + j * 4 * K, [128, D], F32) for j in range(2)]
        r_sqs = [P.R("sq", j) for j in range(2)]
        pb2 = [ps_all[:, (6 + j) * 512:(7 + j) * 512].bitcast(BF16) for j in range(2)]
        r_ssl = [P.R("ss", i) for i in range(NT)]
        r_rstd = P.R("rstd16")
        rstd16 = small[:, 64:80]
        for i in range(NT):
            act(sqs[i % 2], xbuf[:, i, :], AF.Square, [r_x[i]], [r_sqs[i % 2], r_ssl[i]], accum=small[:, 16 + i:17 + i])
        act(rstd16, small[:, 16:32], AF.Ln, r_ssl + [r_cst], [r_rstd], scale=1.0 / D, bias=cc(CI_EPS))
        act(rstd16, rstd16, AF.Exp, [r_rstd], [r_rstd], scale=-0.5)
        for i in range(NT):
            j = i % 2
            act(xs[j], xbuf[:, i, :], AF.Copy, [r_x[i], r_rstd], [r_xs[j]], scale=rstd16[:, i:i + 1])
            for k in range(KC):
                tr(pb2[j][:, k * 128:(k + 1) * 128], xs[j][:, k * 128:(k + 1) * 128], [r_xs[j], r_id], [RB(6 + j)])
            tt("dve", hT[:, :, i * 128:(i + 1) * 128], pb2[j].rearrange("p (k t) -> p k t", k=KC),
               bcast_free(gain_tile[:, l * 8:(l + 1) * 8], KC, 128), ALU.mult, [RB(6 + j), r_par], [r_hT[i]])

    def proj_fm(ps_out, w_slot_view, col0, tb, reads, bankres):
        for k in range(KC):
            mm(ps_out, w_slot_view[:, k, col0:col0 + 128], hT[:, k, tb * 512:(tb + 1) * 512],
               k == 0, k == KC - 1, reads + r_hT[tb * 4:(tb + 1) * 4], [bankres])

    for l in range(n_layers):
        kv_all, st_all = kv_all2[l % 2], st_all2[l % 2]
        r_xd = [P.R("xd", i) for i in range(NT)]
        for i in range(NT):
            dma("sp", xd[i * 128:(i + 1) * 128, :], xbuf[:, i, :], [r_x[i]], [r_xd[i]], "st_xd%d" % i)
        r_dan, r_hgn = P.R("dan"), P.R("hgn")
        dma("sp", dan[:], bass.AP(dan_d.tensor, l * 512, [[0, 128], [1, 512]]), [], [r_dan], "ld_dan")
        dma("sp", hgn[:], bass.AP(hgn_d.tensor, l * 512, [[0, 128], [1, 512]]), [], [r_hgn], "ld_hgn")
        lam_init = 0.8 - 0.6 * math.exp(-0.3 * l)
        ts("dve", dan[:], dan[:], 1.0 - lam_init, None, ALU.mult, None, [r_dan], [r_dan])
        rms_to_hT(gA, l)
        if l == 0:
            dump("hT", hT[:, 0, :], r_hT)
        P.fence()
        if stop == 'S0':
            P.fence()
            break

        kT_loc = av(0, [128, 4, T], BF16)
        V_loc = av(16 * K, [128, 4, NT, 130], BF16)
        Wp = av(34 * K, [128, KC, 512], BF16)
        vtok = av(108 * K, [128, NT, 512], BF16)
        rt1 = [av(42 * K + j * 2 * K, [128, 512], F32) for j in range(2)]
        rt2 = [av(46 * K + j * 2 * K, [128, 512], F32) for j in range(2)]
        r_kT, r_V, r_Wp, r_vtok = P.R("kT_loc"), P.R("V_loc"), P.R("Wp"), P.R("vtok")
        r_wA = [P.R("wA", j) for j in range(2)]
        r_wB = [P.R("wB", j) for j in range(2)]
        wAv = [wA[j][:].rearrange("p (k c) -> p k c", k=KC) for j in range(2)]
        wload(wA[0], 512, w_in_d[l, :, C_AK:C_AK + 512], 0, 512, KC, r_wA[0], "ld_wA0")
        wload(wA[1], 512, w_in_d[l, :, C_AV:C_AV + 512], 0, 512, KC, r_wA[1], "ld_wA1")
        memset("pool", V_loc[:, :, :, 128:129], 1.0, [r_V])
        memset("pool", V_loc[:, :, :, 129:130], 0.0, [r_V])
        memset("pool", Wp, 0.0, [r_Wp])

        def build_partner(wsrc_view, r_src):
            s4 = wsrc_view.rearrange("p k (g d) -> p k g d", d=64)
            d4 = Wp.rearrange("p k (g d) -> p k g d", d=64)
            for k in range(KC):
                cp("pool", d4[:, k, :, 0:8], s4[:, k, :, 8:16], [r_src], [r_Wp])
                cp("pool", d4[:, k, :, 8:16], s4[:, k, :, 0:8], [r_src], [r_Wp])

        def rope_proj(wv, r_w, dstT, r_dst, j0):
            n = j0
            for h in range(4):
                for tb in range(NB):
                    b0, b1 = (n % 2) * 2, (n % 2) * 2 + 1
                    proj_fm(bank(b0), wv, h * 128, tb, [r_w], RB(b0))
                    proj_fm(bank(b1), Wp, h * 128, tb, [r_Wp], RB(b1))
                    j = n % 2
                    r1, r2 = P.R("rt1", j), P.R("rt2", j)
                    tt("dve", rt1[j], bank(b0), ctab[:, tb * 512:(tb + 1) * 512], ALU.mult, [RB(b0), r_tab], [r1])
                    tt("dve", rt2[j], bank(b1), stab[:, tb * 512:(tb + 1) * 512], ALU.mult, [RB(b1), r_tab], [r2])
                    tt("pool", dstT[:, h, tb * 512:(tb + 1) * 512], rt1[j], rt2[j], ALU.add, [r1, r2], [r_dst])
                    n += 1

        build_partner(wAv[0], r_wA[0])
        rope_proj(wAv[0], r_wA[0], kT_loc, r_kT, 0)
        for i in range(NT):
            b = 4 + (i % 2)
            for k in range(KC):
                mm(bank(b), hT[:, k, i * 128:(i + 1) * 128], wAv[1][:, k, :], k == 0, k == KC - 1,
                   [r_hT[i], r_wA[1]], [RB(b)])
            cp("act", V_loc[:, :, i, 0:128], bank(b).rearrange("p (h d) -> p h d", h=4), [RB(b)], [r_V])
        r_kvs = [P.R("kv_src", h) for h in range(4)]
        r_kva = [P.R("kv_all", l % 2, h) for h in range(4)]
        for h in range(4):
            dma("sp", kv_srcs[h][:, 0:2048], kT_loc[:, h, :], [r_kT], [r_kvs[h]], "st_kv%d" % h)
            dma("sp", kv_srcs[h][:, 2048:KVH], V_loc[:, h, :, :].rearrange("p i d -> p (i d)"), [r_V], [r_kvs[h]],
                "st_kv%d" % h)

        def kv_exchange():
            for h in range(4):
                P.emit("pool", lambda e, src=kv_srcs[h], dst=kv_all[h]: e.collective_compute(
                    "AllGather", ALU.bypass, replica_groups=groups, ins=[src], outs=[dst]),
                    [r_kvs[h]], [r_kva[h]], dma="cc_kv%d" % h, inc=1)
        wload(wA[1], 512, w_in_d[l, :, C_BI:C_BI + 512], 0, 512, KC, r_wA[1], "ld_wA1")
        for i in range(NT):
            b = 4 + (i % 2)
            for k in range(KC):
                mm(bank(b), hT[:, k, i * 128:(i + 1) * 128], wAv[1][:, k, :], k == 0, k == KC - 1,
                   [r_hT[i], r_wA[1]], [RB(b)])
            cp("act", vtok[:, i, :], bank(b), [RB(b)], [r_vtok])
        if l == 0:
            dump("kT", kT_loc[:, 0, :], [r_kT])
            dump("vtok", vtok[:, 0, :], [r_vtok])
        P.fence()
        if stop == 'S1':
            P.fence()
            break

        o1 = av(92 * K, [128, NT, 512], BF16)
        o_bT = av(76 * K, [128, 4, T], BF16)
        r_o1 = [P.R("o1", i) for i in range(NT)]
        r_obT = P.R("o_bT")
        S32 = av(41 * K, [128, 4, 128], F32)
        Sb = av(43 * K, [128, 4, 128], BF16)
        r_S32 = [P.R("S32", h) for h in range(4)]
        r_Sb = [P.R("Sb", h) for h in range(4)]
        hg_scb = [av(h * 10496, [128, NT, 64], BF16) for h in range(4)]
        hg_ktok = [av(h * 10496 + 2048, [128, NT, 128], BF16) for h in range(4)]
        hg_qh = [av(h * 10496 + 6144, [128, T], BF16) for h in range(4)]
        hg_dec = [av(h * 10496 + 10240, [128, 32], F32) for h in range(4)]
        g_t = av(44 * K, [128, T], F32)
        a_t = av(52 * K, [128, T], F32)
        kk_t = av(60 * K, [128, T], BF16)
        T0 = 64 * K
        tmpA = wB[1][:, 0:1024].bitcast(F32)
        tmpEq = wB[1][:, 1024:2048].bitcast(F32)
        tmpEk = wB[1][:, 2048:3072].bitcast(F32)
        qtb = wB[1][:, 3072:3584]
        ktb = wB[1][:, 3584:4096]
        kcb = av(T0, [128, 512], BF16)
        chs = av(T0 + 1 * K, [128, 8, 32], F32)
        osum = av(T0 + 2 * K, [128, 4, 128], F32)
        ybf = av(T0 + 4 * K, [128, 512], BF16)
        sgt = av(T0 + 5 * K, [128, 512], F32)
        sqj = av(T0 + 7 * K, [128, 128], F32)
        stin = av(T0 + 8 * K, [128, 2, 512], F32)
        r_g, r_a, r_kk, r_tA, r_Eq, r_Ek = P.R("g_t"), P.R("a_t"), P.R("kk_t"), P.R("tmpA"), P.R("tmpEq"), P.R("tmpEk")
        r_qtb, r_ktb, r_kcb, r_chs = P.R("qtb"), P.R("ktb"), P.R("kcb"), P.R("chs")
        r_osum, r_ybf, r_sgt, r_sqj, r_stin = P.R("osum"), P.R("ybf"), P.R("sgt"), P.R("sqj"), P.R("stin")
        r_hgp = [P.R("hgp", h) for h in range(4)]
        EkF = wB[1][:, 0:4096].bitcast(F32)
        kcf = av(T0 + 8 * K, [128, T], BF16)
        r_EkF, r_kcf = P.R("EkF"), P.R("kcf")
        wBg = wB[0][:].rearrange("p (k c) -> p k c", k=KC)

        for ph in (1, 2):
            gcol = C_G1 if ph == 1 else C_G2
            ldir = ph - 1
            if ph == 1:
                for h in range(4):
                    memset("pool", S32[:, h, :], 0.0, [r_S32[h]])
                    memset("pool", Sb[:, h, :], 0.0, [r_Sb[h]])
            else:
                pass

            def state_exchange():
                r_sts, r_sta = P.R("st_src"), P.R("st_all", l % 2)
                dma("sp", st_src, S32.rearrange("p h d -> p (h d)"), r_S32, [r_sts], "st_st")
                P.emit("pool", lambda e, st_all=st_all: e.collective_compute("AllGather", ALU.bypass, replica_groups=groups,
                                                                             ins=[st_src], outs=[st_all]),
                       [r_sts], [r_sta], dma="cc_st", inc=1)
                dma("sp", stin, st_all.rearrange("(r p) c -> p r c", p=128), [r_sta], [r_stin], "ld_st")
            def hg_load(h):
                j = h % 2
                wload(wA[j], 256, w_in_d[l, :, C_BQ + h * 128:C_BQ + (h + 1) * 128], 0, 128, KC, r_wA[j], "ld_wA%d" % j)
                wload(wA[j], 256, w_in_d[l, :, gcol + h * 128:gcol + (h + 1) * 128], 128, 128, KC, r_wA[j], "ld_wA%d" % j)

            g2 = [g_t, av(76 * K, [128, T], F32)]
            a2 = [a_t, av(84 * K, [128, T], F32)]
            kk2 = [kk_t, av(T0 + 2 * K, [128, T], BF16)]
            chs2 = [chs, av(T0 + 6 * K, [128, 8, 32], F32)]
            Ek2 = [wB[1][:, 0:4096].bitcast(F32), wB[0][:, 0:4096].bitcast(F32)]
            goff = [44 * K, 76 * K]
            aoff = [52 * K, 84 * K]
            r_g2 = [r_g, P.R("g_B")]
            r_a2 = [r_a, P.R("a_B")]
            r_kk2 = [r_kk, P.R("kk_B")]
            r_chs2 = [r_chs, P.R("chs_B")]
            r_Ek2 = [r_EkF, P.R("EkF_B")]
            r_qtf2 = [P.R("qtf", q) for q in range(2)]
            r_ktf2 = [P.R("ktf", q) for q in range(2)]
            r_kcf2 = [P.R("kcf", q) for q in range(2)]
            P.alias(r_Ek2[1], r_wB[0])
            v3 = lambda ap: ap.rearrange("p (c t) -> p c t", t=64)
            mcol = CI_MF if ph == 1 else CI_MB

            def stage1(h):
                j = h % 2
                wv = wA[j][:, 0:KC * 256].rearrange("p (k c) -> p k c", k=KC)
                g_, a_, kk_, chs_ = g2[j], a2[j], kk2[j], chs2[j]
                rg, ra, rkk, rch = r_g2[j], r_a2[j], r_kk2[j], r_chs2[j]
                lbc = lb[:, ldir * 16 + l * 4 + h: ldir * 16 + l * 4 + h + 1]
                omc = oml[:, ldir * 16 + l * 4 + h: ldir * 16 + l * 4 + h + 1]
                rb03 = [RB(0), RB(1), RB(2), RB(3)]
                for tb in range(NB):
                    proj_fm(bank(tb), wv, 128, tb, [r_wA[j]], RB(tb))
                    yield
                act(g_, ps_all[:, 0:2048], AF.Exp, rb03, [rg], scale=-1.0)
                yield
                ts("dve", g_, g_, 1.0, None, ALU.add, None, [rg], [rg])
                yield
                recip(g_, g_, [rg], [rg])
                yield
                ts("dve", g_, g_, omc, lbc, ALU.mult, ALU.add, [rg, r_par, r_oml], [rg])
                yield
                ts("dve", kk_, g_, -1.0, 1.0, ALU.mult, ALU.add, [rg], [rkk])
                yield
                act(g_, g_, AF.Ln, [rg], [rg])
                yield
                P.emit("dve", lambda e, a_=a_, g_=g_: e.tensor_tensor_scan(
                    a_, cst[:, CI_ONE:CI_ONE + 1].broadcast_to([128, T]), g_, 0.0, ALU.mult, ALU.add),
                       [rg, r_cst], [ra])
                yield
                a3, g3 = v3(a_), v3(g_)
                am, u0, p1, D1, D2, tq = (chs_[:, n, :] for n in range(6))
                if ph == 1:
                    cp("dve", am, a3[:, :, 32], [ra], [rch]); yield
                    tt("dve", u0, a3[:, :, 0], g3[:, :, 0], ALU.subtract, [ra, rg], [rch]); yield
                    cp("dve", p1, a3[:, :, 63], [ra], [rch]); yield
                    tt("dve", D1, am, u0, ALU.subtract, [rch], [rch]); yield
                    tt("dve", D2, p1, am, ALU.subtract, [rch], [rch]); yield
                    tt("dve", tq, p1, u0, ALU.subtract, [rch], [rch]); yield
                else:
                    cp("dve", p1, a3[:, :, 63], [ra], [rch]); yield
                    tt("dve", g_, g_, a_, ALU.subtract, [rg, ra], [rg]); yield
                    cp("dve", am, g3[:, :, 32], [rg], [rch]); yield
                    cp("dve", u0, g3[:, :, 0], [rg], [rch]); yield
                    tt("dve", D1, p1, am, ALU.add, [rch], [rch]); yield
                    tt("dve", D2, u0, am, ALU.subtract, [rch], [rch]); yield
                    tt("dve", tq, p1, u0, ALU.add, [rch], [rch]); yield
                act(D1, D1, AF.Exp, [rch], [rch]); yield
                act(D2, D2, AF.Exp, [rch], [rch]); yield
                act(hg_dec[h], tq, AF.Exp, [rch], [r_hgp[h]]); yield

            def stage2(h):
                j = h % 2
                wv = wA[j][:, 0:KC * 256].rearrange("p (k c) -> p k c", k=KC)
                chs_ = chs2[j]
                rkk, rch, rEk = r_kk2[j], r_chs2[j], r_Ek2[j]
                kk_, EkF_ = kk2[j], Ek2[j]
                am, u0, p1, D1, D2, tq = (chs_[:, n, :] for n in range(6))
                if ph == 1:
                    asrc, r_asrc, EqB, r_EqB, off_src, off_eq = a2[j], r_a2[j], g2[j], r_g2[j], aoff[j], goff[j]
                else:
                    asrc, r_asrc, EqB, r_EqB, off_src, off_eq = g2[j], r_g2[j], a2[j], r_a2[j], goff[j], aoff[j]
                rb47 = [RB(4), RB(5), RB(6), RB(7)]
                tt("dve", v3(asrc), v3(asrc), bcast_free(am, 32, 64), ALU.subtract, [r_asrc, rch], [r_asrc]); yield
                act(EqB, asrc, AF.Exp, [r_asrc], [r_EqB]); yield
                act(EkF_, asrc, AF.Exp, [r_asrc], [rEk], scale=-1.0); yield
                for tb in range(NB):
                    proj_fm(bank(4 + tb), wv, 0, tb, [r_wA[j]], RB(4 + tb))
                    yield
                qtf = av(off_src, [128, T], BF16)
                ktf = av(off_src + 4 * K, [128, T], BF16)
                kcf_ = av(off_eq, [128, T], BF16)
                r_qtf, r_ktf, r_kcf_ = r_qtf2[j], r_ktf2[j], r_kcf2[j]
                P.alias(r_qtf, r_asrc)
                P.alias(r_ktf, r_asrc)
                tt("dve", qtf, ps_all[:, 2048:4096], EqB, ALU.mult, rb47 + [r_EqB], [r_qtf]); yield
                tt("pool", ktf, kk_, EkF_, ALU.mult, [rkk, rEk], [r_ktf]); yield
                tt("dve", v3(EqB), v3(EqB), bcast_free(D1, 32, 64), ALU.mult, [r_EqB, rch], [r_EqB]); yield
                tt("pool", v3(EkF_), v3(EkF_), bcast_free(D2, 32, 64), ALU.mult, [rEk, rch], [rEk]); yield
                tt("dve", hg_qh[h], ps_all[:, 2048:4096], EqB, ALU.mult, rb47 + [r_EqB], [r_hgp[h]]); yield
                P.alias(r_kcf_, r_EqB)
                tt("pool", kcf_, kk_, EkF_, ALU.mult, [rkk, rEk], [r_kcf_]); yield
                for c in range(32):
                    i, pb = c // 2, (c % 2) * 64
                    mm(ps_all[pb:pb + 64, 2048 + i * 64: 2048 + (i + 1) * 64], ktf[:, c * 64:(c + 1) * 64],
                       qtf[:, c * 64:(c + 1) * 64], True, True, [r_ktf, r_qtf], [RB(4 + i // 8)])
                    if c % 8 == 7:
                        yield
                tt("dve", hg_scb[h], ps_all[:, 2048:3072].rearrange("p (i t) -> p i t", t=64),
                   bcast_mid(cst[:, mcol:mcol + 64], 16), ALU.mult, [RB(4), RB(5), r_cst], [r_hgp[h]]); yield
                pbT = ps_all[:, 3072:4096].bitcast(BF16)
                for c in range(32):
                    i, pb = c // 2, (c % 2) * 64
                    tr(pbT[pb:pb + 64, i * 128:(i + 1) * 128], kcf_[:, c * 64:(c + 1) * 64], [r_kcf_, r_id], [RB(6 + i // 8)])
                    if c % 8 == 7:
                        yield
                cp("act", hg_ktok[h], pbT.rearrange("p (i d) -> p i d", d=128), [RB(6), RB(7)], [r_hgp[h]]); yield
                P.alias(r_asrc, r_qtf)
                P.alias(r_asrc, r_ktf)
                P.alias(r_EqB, r_kcf_)

            def run_gens(gens):
                gens = list(gens)
                while gens:
                    for g in list(gens):
                        try:
                            next(g)
                        except StopIteration:
                            gens.remove(g)

            hg_load(0)
            hg_load(1)
            if ph == 1:
                kv_exchange()
            if ph == 2:
                state_exchange()
            run_gens([stage1(0)])
            for h in range(4):
                run_gens([stage2(h)] + ([stage1(h + 1)] if h + 1 < 4 else []))
                if h + 2 < 4:
                    hg_load(h + 2)
            P.alias(r_wB[0], r_Ek2[1])
            if ph == 2:
                S32f = S32.rearrange("p h d -> p (h d)")
                ts("dve", S32f, stin[:, 0, :], cc(CI_SEL), None, ALU.mult, None, [r_stin, r_cst], r_S32)
                stt(S32f, stin[:, 1, :], cc(CI_SEL + 1), S32f, ALU.mult, ALU.add, [r_stin, r_cst] + r_S32, r_S32)
                for h in range(4):
                    cp("act", Sb[:, h, :], S32[:, h, :], [r_S32[h]], [r_Sb[h]])
            corder = range(32) if ph == 1 else range(31, -1, -1)
            for c in corder:
                i, pb = c // 2, (c % 2) * 64
                for h in range(4):
                    hs = slice(h * 128, (h + 1) * 128)
                    bo, bs = bank(h), bank(4 + h)
                    mm(bo[pb:pb + 64, 0:128], hg_scb[h][pb:pb + 64, i, :], vtok[pb:pb + 64, i, hs], True, False,
                       [r_hgp[h], r_vtok], [RB(h)])
                    mm(bo[pb:pb + 64, 0:128], hg_qh[h][:, c * 64:(c + 1) * 64], Sb[:, h, :], False, True,
                       [r_hgp[h], r_Sb[h]], [RB(h)])
                    mm(bs[:, 0:128], hg_ktok[h][pb:pb + 64, i, :], vtok[pb:pb + 64, i, hs], True, True,
                       [r_hgp[h], r_vtok], [RB(4 + h)])
                    if ph == 1:
                        cp("act", o1[pb:pb + 64, i, hs], bo[pb:pb + 64, 0:128], [RB(h)], [r_o1[i]])
                    else:
                        tt("dve", o1[pb:pb + 64, i, hs], bo[pb:pb + 64, 0:128], o1[pb:pb + 64, i, hs], ALU.add,
                           [RB(h), r_o1[i]], [r_o1[i]])
                    stt(S32[:, h, :], S32[:, h, :], hg_dec[h][:, c:c + 1], bs[:, 0:128], ALU.mult, ALU.add,
                        [r_S32[h], r_hgp[h], RB(4 + h)], [r_S32[h]])
                    cp("act", Sb[:, h, :], S32[:, h, :], [r_S32[h]], [r_Sb[h]])
                if ph == 2 and c % 2 == 0:
                    for h in range(4):
                        act(av(T0, [128, 128], F32), o1[:, i, h * 128:(h + 1) * 128], AF.Square, [r_o1[i]],
                            [P.R("ss64e", i * 4 + h)], accum=small[:, 96 + i * 4 + h: 97 + i * 4 + h])
            if ph == 2:
                wload(wB[0], 512, w_in_d[l, :, C_BG:C_BG + 512], 0, 512, KC, r_wB[0], "ld_wB0")
                P.fence()
                ss64 = small[:, 96:160]
                r_ss64 = P.R("ss64")
                sqj2 = av(T0, [128, 128], F32)
                r_sqj2 = P.R("sqj2")
                ybf4 = av(T0 + 2 * K, [128, 4, 512], BF16)
                tmpf = av(T0 + 6 * K, [128, 512], F32)
                sg2 = [av(T0 + 8 * K + q * 2 * K, [128, 512], F32) for q in range(2)]
                r_ybf4, r_tmpf = P.R("ybf4"), P.R("tmpf")
                r_sg2 = [P.R("sg2", q) for q in range(2)]
                pb2 = [ps_all[:, (6 + q) * 512:(7 + q) * 512].bitcast(BF16) for q in range(2)]
                act(ss64, ss64, AF.Ln, [P.R("ss64e", q) for q in range(64)] + [r_cst], [r_ss64], scale=1.0 / 128,
                    bias=cc(CI_EPS))
                act(ss64, ss64, AF.Exp, [r_ss64], [r_ss64], scale=-0.5)
                nq = 0
                for tb in range(NB):
                    for t4 in range(4):
                        i = tb * 4 + t4
                        tt("dve", tmpf.rearrange("p (h d) -> p h d", h=4), o1[:, i, :].rearrange("p (h d) -> p h d", h=4),
                           bcast_free(ss64[:, i * 4:(i + 1) * 4], 4, 128), ALU.mult, [r_o1[i], r_ss64], [r_tmpf])
                        tt("dve", ybf4[:, t4, :], tmpf, hgn[:], ALU.mult, [r_tmpf, r_hgn], [r_ybf4])
                    for h in range(4):
                        q = nq % 2
                        nq += 1
                        proj_fm(bank(q), wBg, h * 128, tb, [r_wB[0]], RB(q))
                        act(sg2[q], bank(q), AF.Sigmoid, [RB(q)], [r_sg2[q]])
                        for t4 in range(4):
                            tr(pb2[q][:, t4 * 128:(t4 + 1) * 128], ybf4[:, t4, h * 128:(h + 1) * 128], [r_ybf4, r_id], [RB(6 + q)])
                        tt("dve", o_bT[:, h, tb * 512:(tb + 1) * 512], pb2[q][:, 0:512], sg2[q], ALU.mult,
                           [RB(6 + q), r_sg2[q]], [r_obT])
            P.fence()
        if l == 0:
            dump("obT", o_bT[:, 0, :], [r_obT])
        if stop == 'S4':
            P.fence()
            break

        o_aT = av(108 * K, [128, 4, T], BF16)
        r_oaT = P.R("o_aT")
        qT = [av(j * 4 * K, [128, T], BF16) for j in range(4)]
        Kf2 = [av(16 * K + j * 8 * K, [128, 2, T], BF16) for j in range(2)]
        Vf2 = [av(32 * K + j * 9 * K, [128, 2, NT * 130], BF16) for j in range(2)]
        Et = [av(50 * K + j * 2 * K, [128, 1024], BF16) for j in range(3)]
        ep_t0 = av(56 * K, [128, 4, 128], F32)
        ep_o = av(58 * K, [128, 4, 128], F32)
        ep_y = av(60 * K, [128, 4, 128], BF16)
        ep_sq = av(61 * K, [128, 128], F32)
        ep_acc = av(62 * K, [128, 1040], F32)
        Wp = av(50 * K, [128, KC, 512], BF16)
        rt1 = [av(58 * K + j * 2 * K, [128, 512], F32) for j in range(2)]
        rt2 = [av(62 * K + j * 2 * K, [128, 512], F32) for j in range(2)]
        r_epa, r_epo, r_epy = P.R("ep_acc"), P.R("ep_o"), P.R("ep_y")
        r_qT = [P.R("qT", j) for j in range(4)]
        r_Kf2 = [P.R("Kf", j) for j in range(2)]
        r_Vf2 = [P.R("Vf", j) for j in range(2)]
        r_Et = [P.R("Et", j) for j in range(3)]
        r_ep = P.R("ep")
        r_Wp = P.R("Wp")
        wload(wA[0], 512, w_in_d[l, :, C_AQ:C_AQ + 512], 0, 512, KC, r_wA[0], "ld_wA0")
        memset("pool", Wp, 0.0, [r_Wp])
        build_partner(wAv[0], r_wA[0])

        def kv_load(h):
            kv3 = kv_all[h].rearrange("(r p) c -> p r c", p=128)
            dma("sp", Kf2[h % 2], kv3[:, :, 0:2048], [r_kva[h]], [r_Kf2[h % 2]], "ld_Kf%d" % (h % 2))
            dma("sp", Vf2[h % 2], kv3[:, :, 2048:KVH], [r_kva[h]], [r_Vf2[h % 2]], "ld_Vf%d" % (h % 2))

        kv_load(0)
        kv_load(1)
        n_e = 0
        for h in range(4):
            for tb in range(NB):
                b0, b1 = 4 + (tb % 2) * 2, 5 + (tb % 2) * 2
                proj_fm(bank(b0), wAv[0], h * 128, tb, [r_wA[0]], RB(b0))
                proj_fm(bank(b1), Wp, h * 128, tb, [r_Wp], RB(b1))
                j = tb % 2
                r1, r2 = P.R("rt1", j), P.R("rt2", j)
                tt("dve", rt1[j], bank(b0), ctab[:, tb * 512:(tb + 1) * 512], ALU.mult, [RB(b0), r_tab], [r1])
                tt("dve", rt2[j], bank(b1), stab[:, tb * 512:(tb + 1) * 512], ALU.mult, [RB(b1), r_tab], [r2])
                tt("pool", qT[h][:, tb * 512:(tb + 1) * 512], rt1[j], rt2[j], ALU.add, [r1, r2], [r_qT[h]])
            if l == 0 and h == 0:
                dump("qT", qT[0], [r_qT[0]])
        P.fence()
        for h in range(4):
            jq = h
            Kf, Vf, r_Kf, r_Vf = Kf2[h % 2], Vf2[h % 2], r_Kf2[h % 2], r_Vf2[h % 2]
            steps = [(qb, kt) for qb in range(NB) for kt in range(32)]

            def emit_qk(s):
                qb, kt = steps[s]
                sbi = s % 2
                r_, i_ = kt // 16, kt % 16
                for c in range(2):
                    ps_ = slice(c * 64, (c + 1) * 64)
                    bnk = 2 * sbi + c
                    mm(bank(bnk), Kf[ps_, r_, i_ * 128:(i_ + 1) * 128], qT[jq][ps_, qb * 512:(qb + 1) * 512], True, True,
                       [r_Kf, r_qT[jq]], [RB(bnk)])

            def emit_exp_pv(s):
                nonlocal n_e
                qb, kt = steps[s]
                sbi = s % 2
                r_, i_ = kt // 16, kt % 16
                eb = n_e % 3
                n_e += 1
                act(Et[eb], ps_all[:, sbi * 1024:(sbi + 1) * 1024], AF.Exp, [RB(2 * sbi), RB(2 * sbi + 1)],
                    [r_Et[eb]], scale=0.125)
                for c in range(2):
                    for jj in range(4):
                        a_ = jj * 2 + c
                        bnk, col = 4 + a_ // 3, (a_ % 3) * 130
                        first = (kt == 0 and c == 0 and jj in (0, 2, 3))
                        mm(bank(bnk)[:, col:col + 130], Et[eb][:, c * 512 + jj * 128: c * 512 + (jj + 1) * 128],
                           Vf[:, r_, i_ * 130:(i_ + 1) * 130], first, kt == 31,
                           [r_Et[eb], r_Vf], [RB(bnk)], skip=True)

            def emit_evac(qb):
                cp("dve", ep_acc[:, 0:390], bank(4)[:, 0:390], [RB(4)], [r_epa])
                cp("dve", ep_acc[:, 390:780], bank(5)[:, 0:390], [RB(5)], [r_epa])
                cp("dve", ep_acc[:, 780:1040], bank(6)[:, 0:260], [RB(6)], [r_epa])

            def emit_epilogue(qb):
                acc3 = ep_acc.rearrange("p (a d) -> p a d", d=130)
                acc4 = ep_acc.rearrange("p (j c d) -> p j c d", c=2, d=130)
                rr = small[:, 48:56]
                rr2 = rr.rearrange("p (j c) -> p j c", c=2)
                r_rr = P.R("rr")
                recip(rr, acc3[:, :, 128], [r_epa], [r_rr])
                tt("dve", rr2[:, :, 1], rr2[:, :, 1], nlam[:, l:l + 1].broadcast_to([128, 4]), ALU.mult,
                   [r_rr, r_nlam], [r_rr])
                tt("dve", ep_t0, acc4[:, :, 0, 0:128], bcast_free(rr2[:, :, 0], 4, 128), ALU.mult, [r_epa, r_rr], [r_ep])
                tt("dve", ep_o, acc4[:, :, 1, 0:128], bcast_free(rr2[:, :, 1], 4, 128), ALU.mult, [r_epa, r_rr], [r_epo])
                tt("dve", ep_o, ep_o, ep_t0, ALU.add, [r_epo, r_ep], [r_epo])
                s2 = small[:, 56:60]
                r_s2 = P.R("ss2")
                for jj in range(4):
                    act(ep_sq, ep_o[:, jj, :], AF.Square, [r_epo], [P.R("ep_sq"), r_s2], accum=small[:, 56 + jj:57 + jj])
                act(s2, s2, AF.Ln, [r_s2, r_cst], [r_s2], scale=1.0 / 128, bias=cc(CI_EPS))
                act(s2, s2, AF.Exp, [r_s2], [r_s2], scale=-0.5)
                tt("dve", ep_o, ep_o, bcast_free(s2, 4, 128), ALU.mult, [r_epo, r_s2], [r_epo])
                tt("dve", ep_y, ep_o, bcast_mid(dan[:, h * 128:(h + 1) * 128], 4), ALU.mult, [r_epo, r_dan], [r_epy])
                for jj in range(4):
                    tr(psbf[:, jj * 128:(jj + 1) * 128], ep_y[:, jj, :], [r_epy, r_id], [RB(7)])
                cp("dve", o_aT[:, h, qb * 512:(qb + 1) * 512], psbf[:, 0:512], [RB(7)], [r_oaT])

            emit_qk(0)
            pending = None
            for s in range(len(steps)):
                if s + 1 < len(steps):
                    emit_qk(s + 1)
                emit_exp_pv(s)
                qb, kt = steps[s]
                if pending is not None and s == pending[1]:
                    emit_epilogue(pending[0])
                    pending = None
                if kt == 31:
                    emit_evac(qb)
                    if qb == NB - 1:
                        emit_epilogue(qb)
                    else:
                        pending = (qb, s + 6)
            if h + 2 < 4:
                kv_load(h + 2)
        if l == 0:
            dump("oaT", o_aT[:, 0, :], [r_oaT])
        P.fence()
        if stop == 'S3':
            P.fence()
            break

        mg = [av(92 * K + j * 4 * K, [128, T], BF16) for j in range(4)]
        sg5 = [av(64 * K + j * 2 * K, [128, 512], F32) for j in range(4)]
        r_mg = [P.R("mg", j) for j in range(4)]
        r_sg5 = [P.R("sg5", j) for j in range(4)]
        for i in range(NT):
            dma("sp", xbuf[:, i, :], xd[i * 128:(i + 1) * 128, :], [r_xd[i]], [r_x[i]], "ld_x%d" % i)
        n5 = 0
        r_wo = [P.R("wo5", q) for q in range(4)]

        def wo_view(ch):
            return wB[ch % 2][:, 1024 + ((ch // 2) % 2) * 1024: 2048 + ((ch // 2) % 2) * 1024]

        def s5_load(ch):
            j = ch % 2
            wload(wA[j], 256, w_in_d[l, :, C_GA + ch * 128:C_GA + (ch + 1) * 128], 0, 128, KC, r_wA[j], "ld_wA%d" % j)
            wload(wA[j], 256, w_in_d[l, :, C_GB + ch * 128:C_GB + (ch + 1) * 128], 128, 128, KC, r_wA[j], "ld_wA%d" % j)
            wload(wB[j], 256, w_a_d[l, :, ch * 128:(ch + 1) * 128], 0, 128, 4, r_wB[j], "ld_wB%d" % j)
            wload(wB[j], 256, w_b_d[l, :, ch * 128:(ch + 1) * 128], 128, 128, 4, r_wB[j], "ld_wB%d" % j)
            dma("pool", wo_view(ch), w_o_d[l, ch * 128:(ch + 1) * 128, :], [], [r_wo[ch % 4]], "ld_wo%d" % (ch % 4))

        s5_load(0)
        for ch in range(KC):
            j = ch % 2
            jm = ch % 4
            wv = wA[j][:, 0:KC * 256].rearrange("p (k c) -> p k c", k=KC)
            wab = wB[j][:, 0:4 * 256].rearrange("p (k c) -> p k c", k=4)
            if ch + 1 < KC:
                s5_load(ch + 1)
            for tb in range(NB):
                sl = slice(tb * 512, (tb + 1) * 512)
                b4 = (n5 % 2) * 4
                sa, sb_ = (n5 % 2) * 2, (n5 % 2) * 2 + 1
                n5 += 1
                proj_fm(bank(b4), wv, 0, tb, [r_wA[j]], RB(b4))
                proj_fm(bank(b4 + 1), wv, 128, tb, [r_wA[j]], RB(b4 + 1))
                for k in range(4):
                    mm(bank(b4 + 2), wab[:, k, 0:128], o_aT[:, k, sl], k == 0, k == 3, [r_wB[j], r_oaT], [RB(b4 + 2)])
                for k in range(4):
                    mm(bank(b4 + 3), wab[:, k, 128:256], o_bT[:, k, sl], k == 0, k == 3, [r_wB[j], r_obT], [RB(b4 + 3)])
                act(sg5[sa], bank(b4), AF.Sigmoid, [RB(b4)], [r_sg5[sa]])
                act(sg5[sb_], bank(b4 + 1), AF.Sigmoid, [RB(b4 + 1)], [r_sg5[sb_]])
                tt("dve", sg5[sa], bank(b4 + 2), sg5[sa], ALU.mult, [RB(b4 + 2), r_sg5[sa]], [r_sg5[sa]])
                tt("dve", sg5[sb_], bank(b4 + 3), sg5[sb_], ALU.mult, [RB(b4 + 3), r_sg5[sb_]], [r_sg5[sb_]])
                tt("pool", mg[jm][:, sl], sg5[sa], sg5[sb_], ALU.add, [r_sg5[sa], r_sg5[sb_]], [r_mg[jm]])
            if ch % 2 == 1:
                for i in range(NT):
                    for hf in range(2):
                        b = (i * 2 + hf) % 8
                        for q_, (cj, wj) in enumerate((((ch - 1) % 4, (ch - 1) % 2), (ch % 4, ch % 2))):
                            mm(bank(b), mg[cj][:, i * 128:(i + 1) * 128], wo_view(ch - 1 + q_)[:, hf * 512:(hf + 1) * 512],
                               q_ == 0, q_ == 1, [r_mg[cj], r_wo[(ch - 1 + q_) % 4]], [RB(b)])
                        tt("dve", xbuf[:, i, hf * 512:(hf + 1) * 512], bank(b), xbuf[:, i, hf * 512:(hf + 1) * 512], ALU.add,
                           [RB(b), r_x[i]], [r_x[i]])
        if l == 0:
            dump("x1", xbuf[:, 0, :], r_x)
        P.fence()
        if stop == 'S5':
            P.fence()
            break

        rms_to_hT(gF, l)
        P.fence()
        actb = [av(64 * K + j * 16 * K, [128, 4, T], BF16) for j in range(2)]
        r_actb = [P.R("actb", j) for j in range(2)]
        su = [av(96 * K + j * 2 * K, [128, 512], F32) for j in range(2)]
        r_su = [P.R("su", j) for j in range(2)]
        hgroups = [(0, 4), (4, 4), (8, 4), (12, 4), (16, 3), (19, 3)]
        for gi, (c0, ncg) in enumerate(hgroups):
            j = gi % 2
            wgu = wA[j][:].rearrange("p (k c) -> p k c", k=KC)
            ncol = ncg * 128
            wload(wA[j], 512, w_g_d[l, :, c0 * 128:c0 * 128 + ncol], 0, ncol, KC, r_wA[j], "ld_wA%d" % j)
            wgu2 = wB[j][:].rearrange("p (k c) -> p k c", k=KC)
            wload(wB[j], 512, w_u_d[l, :, c0 * 128:c0 * 128 + ncol], 0, ncol, KC, r_wB[j], "ld_wB%d" % j)
            for cg in range(ncg):
                for tb in range(NB):
                    sl = slice(tb * 512, (tb + 1) * 512)
                    n = (cg * NB + tb) % 2
                    proj_fm(bank(2 * n), wgu, cg * 128, tb, [r_wA[j]], RB(2 * n))
                    proj_fm(bank(2 * n + 1), wgu2, cg * 128, tb, [r_wB[j]], RB(2 * n + 1))
                    act(su[n], bank(2 * n), AF.Silu, [RB(2 * n)], [r_su[n]])
                    tt("dve", actb[j][:, cg, sl], bank(2 * n + 1), su[n], ALU.mult, [RB(2 * n + 1), r_su[n]], [r_actb[j]])
            wdn = av(100 * K, [128, 4, D], BF16)
            r_wdn = P.R("wdn")
            dma("pool", wdn[:, 0:ncg, :], w_d_d[l, c0 * 128:(c0 + ncg) * 128, :].rearrange("(k p) c -> p k c", p=128),
                [], [r_wdn], "ld_wdn")
            for i in range(NT):
                for hf in range(2):
                    b = 4 + (i * 2 + hf) % 4
                    for cg in range(ncg):
                        mm(bank(b), actb[j][:, cg, i * 128:(i + 1) * 128], wdn[:, cg, hf * 512:(hf + 1) * 512],
                           cg == 0, cg == ncg - 1, [r_actb[j], r_wdn], [RB(b)])
                    tt("dve", xbuf[:, i, hf * 512:(hf + 1) * 512], bank(b), xbuf[:, i, hf * 512:(hf + 1) * 512], ALU.add,
                       [RB(b), r_x[i]], [r_x[i]])
        if l == 0:
            dump("x2", xbuf[:, 0, :], r_x)
        P.fence()

    gNb = av(64 * K, [128, D], F32)
    r_gNb = P.R("gNb")
    fin = [av(68 * K + j * 4 * K, [128, D], F32) for j in range(2)]
    r_fin = [P.R("fin", j) for j in range(2)]
    sqf = av(76 * K, [128, D], F32)
    gNfull_d = din("gNfull", [1, D])
    dma("sp", gNb, bass.AP(gNfull_d.tensor, 0, [[0, 128], [1, D]]), [], [r_gNb], "ld_gNb")
    r_out = P.R("out")
    for i in range(NT):
        j = i % 2
        ssc = small[:, 16 + i:17 + i]
        r_ss = P.R("ss", i)
        act(sqf, xbuf[:, i, :], AF.Square, [r_x[i]], [P.R("sqf"), r_ss], accum=ssc)
        act(ssc, ssc, AF.Ln, [r_ss, r_cst], [r_ss], scale=1.0 / D, bias=cc(CI_EPS))
        act(ssc, ssc, AF.Exp, [r_ss], [r_ss], scale=-0.5)
        stt(fin[j], xbuf[:, i, :], ssc, gNb, ALU.mult, ALU.mult, [r_x[i], r_ss, r_gNb], [r_fin[j]])
        dma("sp", out_d[i * 128:(i + 1) * 128, :], fin[j], [r_fin[j]], [r_out], "st_out%d" % j)
    P.wait_all("sp")
    P.build(nc, st)
    st.close()
    return nc


_PROG_CACHE = {}


def _consts():
    c = np.zeros((128, NCST), np.float32)
    c[:, CI_ID:CI_ID + 128] = np.eye(128, dtype=np.float32)
    s = np.arange(128) % 64
    t = np.arange(64)
    c[:, CI_MF:CI_MF + 64] = (s[:, None] <= t[None, :]).astype(np.float32)
    c[:, CI_MB:CI_MB + 64] = (s[:, None] >= t[None, :]).astype(np.float32)
    d = np.arange(128) % 64
    inv = 500000.0 ** (-(np.arange(0, 16, 2, dtype=np.float32) / 16.0))
    f = np.zeros(128, np.float32)
    f[d < 16] = inv[d[d < 16] % 8]
    c[:, CI_FREQ] = f
    c[:, CI_SGN] = np.where(d < 8, -1.0, 1.0)
    c[:, CI_EPS] = EPS
    c[:, CI_HPI] = math.pi / 2
    c[:, CI_ONE] = 1.0
    return c


def _prep_inputs(inputs):
    x = np.asarray(inputs["x"], np.float32)
    pos = np.asarray(inputs["positions"], np.int32)
    w_in = np.asarray(inputs["w_in"], np.float32)
    w_in_odd = np.concatenate([w_in[:, :, :C_G1], w_in[:, :, C_G2:C_BI], w_in[:, :, C_G1:C_G2], w_in[:, :, C_BI:]], axis=2)
    w_in_odd = np.ascontiguousarray(w_in_odd)
    lbl = np.asarray(inputs["hg_lb_logits"], np.float32)
    fm = lambda a: np.ascontiguousarray(a.reshape(-1, 128).T)
    lbl_even = fm(lbl.reshape(2, L, 4, 128).reshape(32, 128))
    lbl_odd = fm(lbl[::-1].reshape(32, 128))
    cbase = _consts()
    shared = {
        "gA": fm(np.asarray(inputs["attn_norm"], np.float32).reshape(L * 8, 128)),
        "gF": fm(np.asarray(inputs["ffn_norm"], np.float32).reshape(L * 8, 128)),
        "gN": fm(np.asarray(inputs["final_norm"], np.float32).reshape(8, 128)),
        "gNfull": np.asarray(inputs["final_norm"], np.float32).reshape(1, D),
        "da_norm": np.asarray(inputs["da_norm"], np.float32),
        "hg_norm": np.asarray(inputs["hg_norm"], np.float32),
        "da_lambda": np.asarray(inputs["da_lambda"], np.float32).reshape(1, L * 256),
        "w_a": np.asarray(inputs["w_a"], np.float32),
        "w_b": np.asarray(inputs["w_b"], np.float32),
        "w_o": np.asarray(inputs["w_o"], np.float32),
        "w_gate": np.asarray(inputs["w_gate"], np.float32),
        "w_up": np.asarray(inputs["w_up"], np.float32),
        "w_down": np.asarray(inputs["w_down"], np.float32),
    }
    maps = []
    for c in range(NCORES):
        b, half = c // 2, c % 2
        if half == 0:
            xs = x[b, :T]
            ps = pos[b, :T]
        else:
            xs = x[b][::-1][:T]
            ps = pos[b][::-1][:T]
        cs = cbase.copy()
        cs[:, CI_SEL + (1 - half)] = 1.0
        m = dict(shared)
        m.update({"x": np.ascontiguousarray(xs), "pos": np.ascontiguousarray(ps).reshape(1, T),
                  "cst": cs, "w_in": w_in if half == 0 else w_in_odd,
                  "lbl": lbl_even if half == 0 else lbl_odd})
        maps.append(m)
    return maps


def _assemble(results, key="out"):
    out = np.empty((4, 4096, D), np.float32)
    for c in range(NCORES):
        b, half = c // 2, c % 2
        y = np.asarray(results[c][key], np.float32)
        if half == 0:
            out[b, :T] = y
        else:
            out[b, T:] = y[::-1]
    return out


def kernel(**inputs):
    if "nc" not in _PROG_CACHE:
        _PROG_CACHE["nc"] = build_program(L)
    nc = _PROG_CACHE["nc"]
    maps = _prep_inputs(inputs)
    res = run_bass_kernel_spmd(nc, maps, core_ids=list(range(NCORES)))
    return _assemble(res.results)
```
